# Optimizing a Trainium2 kernel written in Bass

```python
import math
import jax, jax.numpy as jnp
from jax import lax
import numpy as np

D_MODEL = 1024
BATCH = 4
SEQ = 8192
DEPTH = 2
DEC_BATCH = 16
DEC_SEQ = 16
PAST_LEN = 1024

CHUNK = 64
N_HEADS = 16
HEAD_DIM = D_MODEL // N_HEADS
D_FF = 4 * D_MODEL
BAND_CHUNKS = 8
WINDOW = BAND_CHUNKS * CHUNK
REL_CLIP = 256
N_REL = 2 * REL_CLIP + 1
Q_BLOCK = 128
N_BAND_LAYERS = (DEPTH + 1) // 2
N_FOX_LAYERS = DEPTH // 2
RMS_EPS = 1e-6
NEG_INF = -1e30

kernel_name = "streaming_band_fox_macaron_encoder"


def rmsnorm(x, g):
    xf = x.astype(jnp.float32)
    y = xf * lax.rsqrt(jnp.mean(xf * xf, axis=-1, keepdims=True) + RMS_EPS)
    return (y * g.astype(jnp.float32)).astype(x.dtype)


def swiglu(h, w_gate, w_up, w_down):
    return (jax.nn.silu(h @ w_gate) * (h @ w_up)) @ w_down


def half_ffn(x, g, w_gate, w_up, w_down):
    return x + 0.5 * swiglu(rmsnorm(x, g), w_gate, w_up, w_down)


def qkv_proj(h, w_qkv):
    q, k, v = jnp.split(h @ w_qkv, 3, axis=-1)
    shp = h.shape[:-1] + (N_HEADS, HEAD_DIM)
    return q.reshape(shp), k.reshape(shp), v.reshape(shp)


def attend(q, k, v, bias, valid):
    s = jnp.einsum("bqhd,bkhd->bhqk", q, k, preferred_element_type=jnp.float32) * (HEAD_DIM ** -0.5) + bias
    p = jax.nn.softmax(jnp.where(valid, s, NEG_INF), axis=-1)
    return jnp.einsum("bhqk,bkhd->bqhd", p.astype(v.dtype), v)


def rel_position_bias(dist, table):
    idx = jnp.clip(dist, -REL_CLIP, REL_CLIP) + REL_CLIP
    return jnp.moveaxis(table[idx], -1, 0)[None].astype(jnp.float32)


def band_attention_prompt(q, k, v, table):
    B, T = q.shape[:2]
    n_chunks = T // CHUNK
    band = WINDOW + CHUNK
    pad = ((0, 0), (WINDOW, 0), (0, 0), (0, 0))
    kp, vp = jnp.pad(k, pad), jnp.pad(v, pad)
    qc = q.reshape(B, n_chunks, CHUNK, N_HEADS, HEAD_DIM).swapaxes(0, 1)
    q_rel = jnp.arange(CHUNK)
    k_rel = jnp.arange(band) - WINDOW
    bias = rel_position_bias(q_rel[:, None] - k_rel[None, :], table)

    def one_chunk(args):
        c, qb = args
        start = c * CHUNK
        kb = lax.dynamic_slice_in_dim(kp, start, band, axis=1)
        vb = lax.dynamic_slice_in_dim(vp, start, band, axis=1)
        valid = (start + k_rel >= 0)[None, None, None, :]
        return attend(qb, kb, vb, bias, valid)

    out = lax.map(one_chunk, (jnp.arange(n_chunks), qc))
    return out.swapaxes(0, 1).reshape(B, T, N_HEADS, HEAD_DIM)


def band_attention_sample(q, k_new, v_new, k_cache, v_cache, table):
    T = q.shape[1]
    wc = k_cache.shape[1]
    k = jnp.concatenate([k_cache.astype(k_new.dtype), k_new], axis=1)
    v = jnp.concatenate([v_cache.astype(v_new.dtype), v_new], axis=1)
    q_pos = PAST_LEN + jnp.arange(T)
    k_pos = jnp.concatenate([PAST_LEN - wc + jnp.arange(wc), q_pos])
    bias = rel_position_bias(q_pos[:, None] - k_pos[None, :], table)
    q_chunk, k_chunk = q_pos[:, None] // CHUNK, k_pos[None, :] // CHUNK
    valid = ((k_chunk <= q_chunk) & (k_chunk >= q_chunk - BAND_CHUNKS))[None, None]
    return attend(q, k, v, bias, valid)


def log_forget(h, w_f, b_f):
    return jax.nn.log_sigmoid(jnp.einsum("btd,dh->bth", h, w_f, preferred_element_type=jnp.float32)
                              + b_f.astype(jnp.float32))


def fox_prompt(q, k, v, logf):
    B, T = q.shape[:2]
    n_blocks = T // Q_BLOCK
    cum = lax.cumsum(logf, axis=1)
    cum_k = cum.swapaxes(1, 2)
    qb = q.reshape(B, n_blocks, Q_BLOCK, N_HEADS, HEAD_DIM).swapaxes(0, 1)
    cb = cum.reshape(B, n_blocks, Q_BLOCK, N_HEADS).swapaxes(0, 1)
    k_pos = jnp.arange(T)

    def one_block(args):
        i, qi, ci = args
        q_pos = i * Q_BLOCK + jnp.arange(Q_BLOCK)
        bias = ci.swapaxes(1, 2)[..., None] - cum_k[:, :, None, :]
        valid = (k_pos[None, :] <= q_pos[:, None])[None, None]
        return attend(qi, k, v, bias, valid)

    out = lax.map(one_block, (jnp.arange(n_blocks), qb, cb))
    return out.swapaxes(0, 1).reshape(B, T, N_HEADS, HEAD_DIM)


def fox_sample(q, k_new, v_new, logf_new, k_cache, v_cache, logf_cache):
    T = q.shape[1]
    P = k_cache.shape[1]
    k = jnp.concatenate([k_cache.astype(k_new.dtype), k_new], axis=1)
    v = jnp.concatenate([v_cache.astype(v_new.dtype), v_new], axis=1)
    cum = lax.cumsum(jnp.concatenate([logf_cache.astype(jnp.float32), logf_new], axis=1), axis=1)
    bias = cum[:, P:].swapaxes(1, 2)[..., None] - cum.swapaxes(1, 2)[:, :, None, :]
    valid = (jnp.arange(P + T)[None, :] <= (P + jnp.arange(T))[:, None])[None, None]
    return attend(q, k, v, bias, valid)


def setup_inputs(seed: int = 0) -> dict:
    key = jax.random.key(seed)
    ks = jax.random.split(key, 20)
    win_cache = min(WINDOW, PAST_LEN)

    def nrm(k, shape, scale):
        return jax.random.normal(k, shape, jnp.float32) * scale

    return {
        "x_prompt": nrm(ks[0], (BATCH, SEQ, D_MODEL), 1.0),
        "x_sample": nrm(ks[1], (DEC_BATCH, DEC_SEQ, D_MODEL), 1.0),
        "cache_band_k": nrm(ks[2], (N_BAND_LAYERS, DEC_BATCH, win_cache, N_HEADS, HEAD_DIM), 1.0),
        "cache_band_v": nrm(ks[3], (N_BAND_LAYERS, DEC_BATCH, win_cache, N_HEADS, HEAD_DIM), 1.0),
        "cache_fox_k": nrm(ks[4], (N_FOX_LAYERS, DEC_BATCH, PAST_LEN, N_HEADS, HEAD_DIM), 1.0),
        "cache_fox_v": nrm(ks[5], (N_FOX_LAYERS, DEC_BATCH, PAST_LEN, N_HEADS, HEAD_DIM), 1.0),
        "cache_fox_logf": jax.nn.log_sigmoid(3.0 + nrm(ks[6], (N_FOX_LAYERS, DEC_BATCH, PAST_LEN, N_HEADS), 1.0)),
        "norm_g": 1.0 + nrm(ks[7], (DEPTH, 3, D_MODEL), 0.05),
        "w_qkv": nrm(ks[8], (DEPTH, D_MODEL, 3 * D_MODEL), D_MODEL ** -0.5),
        "w_o": nrm(ks[9], (DEPTH, D_MODEL, D_MODEL), D_MODEL ** -0.5),
        "w_ffn_gate": nrm(ks[10], (DEPTH, 2, D_MODEL, D_FF), D_MODEL ** -0.5),
        "w_ffn_up": nrm(ks[11], (DEPTH, 2, D_MODEL, D_FF), D_MODEL ** -0.5),
        "w_ffn_down": nrm(ks[12], (DEPTH, 2, D_FF, D_MODEL), D_FF ** -0.5),
        "rel_bias": nrm(ks[13], (N_BAND_LAYERS, N_REL, N_HEADS), 0.5),
        "w_forget": nrm(ks[14], (N_FOX_LAYERS, D_MODEL, N_HEADS), D_MODEL ** -0.5),
        "b_forget": jnp.linspace(1.0, 5.0, N_HEADS, dtype=jnp.float32)[None, :]
                     + nrm(ks[15], (N_FOX_LAYERS, N_HEADS), 0.1),
        "final_norm_g": 1.0 + nrm(ks[16], (D_MODEL,), 0.05),
    }


def reference(x_prompt, x_sample, cache_band_k, cache_band_v, cache_fox_k, cache_fox_v, cache_fox_logf,
              norm_g, w_qkv, w_o, w_ffn_gate, w_ffn_up, w_ffn_down, rel_bias, w_forget, b_forget,
              final_norm_g):
    xp, xs = x_prompt, x_sample
    band_kp, band_vp, band_ks, band_vs = [], [], [], []
    fox_kp, fox_vp, fox_lp, fox_ks, fox_vs, fox_ls = [], [], [], [], [], []
    for layer in range(DEPTH):
        ffn_a = (norm_g[layer, 0], w_ffn_gate[layer, 0], w_ffn_up[layer, 0], w_ffn_down[layer, 0])
        xp = half_ffn(xp, *ffn_a)
        xs = half_ffn(xs, *ffn_a)
        hp = rmsnorm(xp, norm_g[layer, 1])
        hs = rmsnorm(xs, norm_g[layer, 1])
        qp, kp, vp = qkv_proj(hp, w_qkv[layer])
        qs, ks_, vs_ = qkv_proj(hs, w_qkv[layer])
        if layer % 2 == 0:
            a = layer // 2
            op = band_attention_prompt(qp, kp, vp, rel_bias[a])
            os_ = band_attention_sample(qs, ks_, vs_, cache_band_k[a], cache_band_v[a], rel_bias[a])
            keep = max(kp.shape[1] - WINDOW, 0)
            band_kp.append(kp[:, keep:])
            band_vp.append(vp[:, keep:])
            band_ks.append(ks_)
            band_vs.append(vs_)
        else:
            b = layer // 2
            lfp = log_forget(hp, w_forget[b], b_forget[b])
            lfs = log_forget(hs, w_forget[b], b_forget[b])
            op = fox_prompt(qp, kp, vp, lfp)
            os_ = fox_sample(qs, ks_, vs_, lfs, cache_fox_k[b], cache_fox_v[b], cache_fox_logf[b])
            fox_kp.append(kp)
            fox_vp.append(vp)
            fox_lp.append(lfp)
            fox_ks.append(ks_)
            fox_vs.append(vs_)
            fox_ls.append(lfs)
        xp = xp + op.reshape(xp.shape) @ w_o[layer]
        xs = xs + os_.reshape(xs.shape) @ w_o[layer]
        ffn_b = (norm_g[layer, 2], w_ffn_gate[layer, 1], w_ffn_up[layer, 1], w_ffn_down[layer, 1])
        xp = half_ffn(xp, *ffn_b)
        xs = half_ffn(xs, *ffn_b)
    y_prompt = rmsnorm(xp, final_norm_g)
    y_sample = rmsnorm(xs, final_norm_g)
    return (y_prompt, y_sample,
            jnp.stack(band_kp), jnp.stack(band_vp), jnp.stack(band_ks), jnp.stack(band_vs),
            jnp.stack(fox_kp), jnp.stack(fox_vp), jnp.stack(fox_lp),
            jnp.stack(fox_ks), jnp.stack(fox_vs), jnp.stack(fox_ls))
```

```python
import numpy as np
import ml_dtypes
from contextlib import ExitStack
import concourse.bass as bass
import concourse.mybir as mybir
from concourse.bass_utils import run_bass_kernel_spmd

F32 = mybir.dt.float32
BF16 = mybir.dt.bfloat16
AF = mybir.ActivationFunctionType
ALU = mybir.AluOpType

D = 1024
DC = 8
FF = 4096
FC = 32
H = 16
HD = 64
NEG = -30000.0
VW = 66
SAME_ENG_SYNC = True
LOOKAHEAD = 4


class Res:
    __slots__ = ("name", "w", "r")

    def __init__(self, name):
        self.name = name
        self.w = None
        self.r = {}


class Op:
    __slots__ = ("eng", "fn", "waits", "signaled", "sigval", "dma", "coll")

    def __init__(self, eng, fn):
        self.eng = eng
        self.fn = fn
        self.waits = []
        self.signaled = False
        self.sigval = None
        self.dma = None
        self.coll = False


class Rec:
    def __init__(self):
        self.call = None

    def __getattr__(self, name):
        def f(*a, **k):
            self.call = (name, a, k)
            return self
        return f


def _rec(fn):
    r = Rec()
    fn(r)
    assert r.call is not None
    return r.call


class Chan:
    def __init__(self, name):
        self.name = name
        self.count = 0
        self.sem = None


class Sched:
    ENG = ("pe", "act", "dve", "pool", "sp")

    def __init__(self):
        self.ops = {k: [] for k in self.ENG}
        self.chans = []

    def chan(self, name):
        c = Chan(name)
        self.chans.append(c)
        return c

    def _dep(self, op, ev):
        if ev is None:
            return
        if ev[0] == "op":
            src = ev[1]
            if src.eng == op.eng and (op.eng == "pe" or not SAME_ENG_SYNC):
                return
            if src is op:
                return
            src.signaled = True
        op.waits.append(ev)

    def issue(self, eng, fn, reads=(), writes=()):
        op = Op(eng, _rec(fn) if fn is not None else None)
        ev = ("op", op)
        for r in reads:
            self._dep(op, r.w)
        for w in writes:
            self._dep(op, w.w)
            for e in w.r.values():
                self._dep(op, e)
        for r in reads:
            r.r[eng] = ev
        for w in writes:
            w.w = ev
            w.r = {}
        self.ops[eng].append(op)
        return op

    def dma(self, queue, chan, fn, reads=(), writes=(), coll=False, wars=()):
        op = Op(queue, _rec(fn))
        op.coll = coll
        chan.count += 1 if coll else 16
        op.dma = (chan, chan.count)
        ev = ("dma", chan, chan.count)
        for w in wars:
            self._dep(op, w.w)
            for e in w.r.values():
                self._dep(op, e)
        for r in reads:
            self._dep(op, r.w)
        for w in writes:
            self._dep(op, w.w)
            for e in w.r.values():
                self._dep(op, e)
        for r in reads:
            r.r[("dma", chan.name)] = ev
        for w in writes:
            w.w = ev
            w.r = {}
        self.ops[queue].append(op)
        return op

    def emit(self, nc, es):
        sems = {k: es.enter_context(nc.semaphore("sem_" + k)) for k in self.ENG}
        for c in self.chans:
            c.sem = es.enter_context(nc.semaphore("ch_" + c.name))
        for k in self.ENG:
            n = 0
            for op in self.ops[k]:
                if op.signaled:
                    n += 1
                    op.sigval = n
        block = es.enter_context(nc.Block())

        def run(k, eng):
            waited = {}
            for op in self.ops[k]:
                for ev in op.waits:
                    if ev[0] == "op":
                        sem, val, key = sems[ev[1].eng], ev[1].sigval, ev[1].eng
                    else:
                        sem, val, key = ev[1].sem, ev[2], ev[1].name
                    if waited.get(key, 0) >= val:
                        continue
                    waited[key] = val
                    eng.wait_ge(sem, val)
                if op.fn is None:
                    continue
                name, a, kw = op.fn
                ins = getattr(eng, name)(*a, **kw)
                if op.dma is not None and op.coll:
                    ins.then_inc(op.dma[0].sem)
                elif op.dma is not None:
                    ins.then_inc(op.dma[0].sem, 16)
                elif op.signaled:
                    ins.then_inc(sems[k], 1)

        @block.tensor
        def _(e):
            run("pe", e)

        @block.scalar
        def _(e):
            run("act", e)

        @block.vector
        def _(e):
            run("dve", e)

        @block.gpsimd
        def _(e):
            run("pool", e)

        @block.sync
        def _(e):
            run("sp", e)
            for c in self.chans:
                if c.count > 0:
                    e.wait_ge(c.sem, c.count)


class Rot:
    def __init__(self, items):
        self.items = items
        self.i = 0

    def next(self):
        it = self.items[self.i % len(self.items)]
        self.i += 1
        return it


def build(T, NS, PAIR=False, L1=True):
    nc = bass.Bass("TRN2", target_bir_lowering=False)
    S = Sched()
    es = ExitStack()
    NT = T // 512
    NBLK = T // 128

    def din(name, shape, dt=F32):
        return nc.dram_tensor(name, list(shape), dt, kind="ExternalInput").ap()

    def dout(name, shape, dt=F32):
        return nc.dram_tensor(name, list(shape), dt, kind="ExternalOutput").ap()

    def dscr(name, shape, dt):
        return nc.dram_tensor(name, list(shape), dt, kind="Internal").ap()

    x_d = din("x", [T, D])
    xs_d = din("xs", [max(NS, 1), 16, D])
    cbk_d = din("cbk", [max(NS, 1), 512, D])
    cbv_d = din("cbv", [max(NS, 1), 512, D])
    cfk_d = din("cfk", [max(NS, 1), 1024, D])
    cfv_d = din("cfv", [max(NS, 1), 1024, D])
    cfl_d = din("cfl", [max(NS, 1), 1024, H])
    gT_d = din("gT", [128, 7 * DC])
    wqkv_d = din("w_qkv", [2, D, 3 * D])
    wo_d = din("w_o", [2, D, D])
    wg_d = din("w_g", [4, D, FF])
    wu_d = din("w_u", [4, D, FF])
    wd_d = din("w_d", [4, FF, D])
    rsk_d = din("rsk", [H, 128, 640])
    wf_d = din("wf", [128, DC * H])
    bf_d = din("bfb", [128, H])
    cst_d = din("cst", [128, 4 * 128])
    msk_d = din("msk", [128, 8 * 512], BF16)

    if PAIR:
        xh_d = din("xh", [512, D])
        hm_d = din("hmask", [128, 1])
    y_d = dout("y", [T, D])
    ys_d = dout("ys", [max(NS, 1), 16, D])
    bkp_d = dout("bkp", [512, D])
    bvp_d = dout("bvp", [512, D])
    bks_d = dout("bks", [max(NS, 1), 16, D])
    bvs_d = dout("bvs", [max(NS, 1), 16, D])
    fkp_d = dout("fkp", [T, D])
    fvp_d = dout("fvp", [T, D])
    flp_d = dout("flp", [T, H])
    fks_d = dout("fks", [max(NS, 1), 16, D])
    fvs_d = dout("fvs", [max(NS, 1), 16, D])
    fls_d = dout("fls", [max(NS, 1), 16, H])

    import os
    dbg_d = dout("dbg", [128, 12, 512], BF16) if (os.environ.get("KDBG", "0") == "1") else None
    wqkv_b = dscr("wqkv_b", [2, D, 3 * D], BF16)
    wo_b = dscr("wo_b", [2, D, D], BF16)
    wg_b = dscr("wg_b", [4, D, FF], BF16)
    wu_b = dscr("wu_b", [4, D, FF], BF16)
    wd_b = dscr("wd_b", [4, FF, D], BF16)
    def dcol(name, shape, dt):
        return nc.dram_tensor(name, list(shape), dt).ap()

    HB = 4 if PAIR else 0
    ksp1_2d = dcol("ksp1", [DC * 128, T], BF16)
    vsp1_2d = dcol("vsp1", [DC * 128, NBLK * 2 * VW], BF16)
    ksp = [dscr("ksp0", [DC, 128, T + 128 * HB], BF16), ksp1_2d.rearrange("(c p) t -> c p t", p=128)]
    vsp = [dscr("vsp0", [DC, 128, NBLK + HB, 2 * VW], BF16),
           vsp1_2d.rearrange("(c p) (b e) -> c p b e", p=128, e=2 * VW)]
    if PAIR:
        ksg_2d = dcol("ksg", [2 * DC * 128, T], BF16)
        vsg_2d = dcol("vsg", [2 * DC * 128, NBLK * 2 * VW], BF16)
        ksg = ksg_2d.rearrange("(c r p) t -> r c p t", r=2, p=128)
        vsg = vsg_2d.rearrange("(c r p) (b e) -> r c p b e", r=2, p=128, e=2 * VW)
        ncd = dcol("ncd", [128, (NBLK + 1) * H], F32)
        ncg = dcol("ncg", [2 * 128, (NBLK + 1) * H], F32)
        xsp = dscr("xsp", [NT, 128, DC * 512], F32)
        qsp = dscr("qsp", [NT, 2, 128, DC * 512], BF16)
        R_xsp, R_qsp, R_ksg, R_vsg, R_ncd, R_ncg = (Res(n) for n in ("xsp", "qsp", "ksg", "vsg", "ncd", "ncg"))
    kss = [dscr(f"kss{l}", [max(NS, 1), DC, 128, 1024], BF16) for l in range(2)]
    vss = [dscr(f"vss{l}", [max(NS, 1), DC, 128, 8, 2 * VW], BF16) for l in range(2)]
    R_ksp = [Res(f"ksp{l}") for l in range(2)]
    R_vsp = [Res(f"vsp{l}") for l in range(2)]
    R_kss = [Res(f"kss{l}") for l in range(2)]
    R_vss = [Res(f"vss{l}") for l in range(2)]
    R_w = {}

    def sb(name, shape, dt):
        return es.enter_context(nc.sbuf_tensor("s_" + name, list(shape), dt))

    xT = sb("xT", [128, DC, 512], F32); R_xT = Res("xT")
    xn = sb("xn", [128, DC, 512], BF16); R_xn = Res("xn")
    hb = sb("hb", [128, FC * 512], BF16); R_hb = Res("hb")
    h_v = hb[:, :].rearrange("p (f t) -> p f t", f=FC)
    stgA = hb[:, 0:8192].bitcast(F32).rearrange("p (j d) -> p j d", j=4)
    stgB = hb[:, 8192:16384].bitcast(F32).rearrange("p (j d) -> p j d", j=4)
    attnT2 = hb[:, 0:4096].rearrange("p (c t) -> p c t", c=DC)
    R_at = Res("attnT2")
    junk2 = sb("junk2", [128, 1], F32)
    oddst = [sb(f"oddst{i}", [64, 512], BF16) for i in range(2)]
    R_odd = [Res(f"oddst{i}") for i in range(2)]
    C_odd = [S.chan(f"odd{i}") for i in range(2)]
    oddrot = Rot([0, 1])
    qT2 = [sb(f"qT{g}", [128, DC, 512], BF16) for g in range(2)]; R_qT = Res("qT")
    kT = sb("kT", [128, DC, 512], BF16); R_kT = Res("kT")
    vA = sb("vA", [128, 4, H, VW], BF16); R_vA = Res("vA")
    NRING = 4
    ring = [sb(f"ring{i}", [128, 8, 512], BF16) for i in range(NRING)]
    R_ring = [Res(f"ring{i}") for i in range(NRING)]
    C_ring = [S.chan(f"ring{i}") for i in range(NRING)]
    ringrot = Rot(list(range(NRING)))
    hk = [sb(f"hk{i}", [128, 1024], BF16) for i in range(2)]
    hv = [sb(f"hv{i}", [128, 8, 2 * VW], BF16) for i in range(2)]
    R_hs = [Res(f"hs{i}") for i in range(2)]
    C_hk = [S.chan(f"hk{i}") for i in range(2)]
    C_hv = [S.chan(f"hv{i}") for i in range(2)]
    hsrot = Rot([0, 1])
    rsk = [sb(f"rsk{i}", [128, 640], F32) for i in range(2)]
    R_rsk = [Res(f"rsk{i}") for i in range(2)]
    C_rsk = [S.chan(f"rsk{i}") for i in range(2)]
    rskrot = Rot([0, 1])
    cst = sb("cst", [128, 512], F32); R_cst = Res("cst")
    ident = cst[:, 0:128]
    ones = cst[:, 128:256]
    tri = cst[:, 256:384]
    trineg = cst[:, 384:512]
    msk = sb("msk", [128, 8, 512], BF16)
    gT = sb("gT", [128, 7 * DC], F32)
    wf = sb("wf", [128, DC, H], BF16)
    wf32 = sb("wf32", [128, DC * H], F32)
    bfb = sb("bfb", [128, H], F32)
    ncum_p = sb("ncum_p", [128, NBLK, H], F32); R_ncp = Res("ncum_p")
    ncum_s = sb("ncum_s", [128, max(NS, 1), 8, H], F32); R_ncs = Res("ncum_s")
    ncum_c = sb("ncum_c", [128, H], F32); R_ncc = Res("ncum_c")
    carry_p = sb("carry_p", [128, H], F32); R_cp = Res("carry_p")
    carry_s = sb("carry_s", [128, max(NS, 1), H], F32); R_cs = Res("carry_s")
    lft = sb("lft", [128, 4, H], F32); R_lft = Res("lft")
    zt = sb("zt", [128, 4, H], F32); R_zt = Res("zt")
    f32t = [sb(f"f32t{i}", [128, 512], F32) for i in range(4)]
    R_f32t = [Res(f"f32t{i}") for i in range(4)]
    f32rot = Rot(list(range(4)))
    pt = [sb(f"pt{i}", [128, 512], BF16) for i in range(6)]
    R_pt = [Res(f"pt{i}") for i in range(6)]
    ptrot = Rot([0, 1, 2, 3, 4, 5])
    cq = [sb(f"cq{i}", [128, 512], F32) for i in range(2)]
    R_cq = [Res(f"cq{i}") for i in range(2)]
    cqrot = Rot([0, 1])
    dgt = [sb(f"dgt{i}", [128, 128], F32) for i in range(2)]
    R_dgt = [Res(f"dgt{i}") for i in range(2)]
    dgrot = Rot([0, 1])
    oT = sb("oT", [HD + 1, 512], F32); R_oT = Res("oT")
    rd = sb("rd", [128, 512], F32); R_rd = Res("rd")
    C_ld = S.chan("ld")
    C_ld2 = S.chan("ld2")
    C_stA = S.chan("stA")
    C_stB = S.chan("stB")
    C_stL = S.chan("stL")
    C_spk = S.chan("spk")
    C_spv = S.chan("spv")
    R_const = Res("const")

    ps = [es.enter_context(nc.psum_tensor(f"ps{i}", [128, 512], F32)) for i in range(8)]
    R_ps = [Res(f"ps{i}") for i in range(8)]
    rotA = Rot([0, 1, 2, 3])
    rotB = Rot([4, 5, 6, 7])
    rotS = Rot([0, 1, 2, 3, 4])
    rotO = Rot([5, 6, 7])

    stg32 = [hb[:, 0:8192].bitcast(F32), hb[:, 8192:16384].bitcast(F32)]
    R_stg = [Res("stg0"), Res("stg1")]
    C_stg = [S.chan("stg0"), S.chan("stg1")]
    C_wst = [S.chan(f"wst{i}") for i in range(NRING)]
    ncast = [0]

    def cast_slab(src3, dst3, rw, d0, d1):
        k = ncast[0] % 2
        i = ringrot.next()
        eng = "dve"
        ncast[0] += 1
        sv = stg32[k].rearrange("p (a b) -> p a b", a=d0)
        rv = ring[i][:, :, :].rearrange("p a b -> p (a b)").rearrange("p (a b) -> p a b", a=d0)
        S.dma("sp", C_stg[k], lambda e: e.dma_start(out=sv, in_=src3), writes=[R_stg[k]])
        if eng == "act":
            S.issue("act", lambda e: e.activation(out=rv, in_=sv, func=AF.Copy), reads=[R_stg[k]], writes=[R_ring[i]])
        else:
            S.issue(eng, lambda e: e.tensor_copy(out=rv, in_=sv), reads=[R_stg[k]], writes=[R_ring[i]])
        S.dma("pool", C_wst[i], lambda e: e.dma_start(out=dst3, in_=rv), reads=[R_ring[i]], writes=[rw])

    def cast_w(nm, dst, src, i, kind):
        rw = R_w[(nm, i)] = Res(f"w_{nm}{i}")
        if kind == "kf":
            sv = src[i].rearrange("(kc p) f -> p kc f", p=128)
            dv = dst[i].rearrange("(kc p) f -> p kc f", p=128)
            for fb in range(sv.shape[2] // 512):
                cast_slab(sv[:, :, 512 * fb:512 * fb + 512], dv[:, :, 512 * fb:512 * fb + 512], rw, 8, 512)
        elif kind == "fd":
            sv = src[i].rearrange("(fc p) d -> p fc d", p=128)
            dv = dst[i].rearrange("(fc p) d -> p fc d", p=128)
            for fb in range(8):
                cast_slab(sv[:, 4 * fb:4 * fb + 4, :], dv[:, 4 * fb:4 * fb + 4, :], rw, 4, 1024)
        else:
            sv = src[i].rearrange("(kc p) d -> p kc d", p=128)
            dv = dst[i].rearrange("(kc p) d -> p kc d", p=128)
            for fb in range(2):
                cast_slab(sv[:, 4 * fb:4 * fb + 4, :], dv[:, 4 * fb:4 * fb + 4, :], rw, 4, 1024)

    for l in range(2):
        cast_w("g", wg_b, wg_d, 2 * l, "kf")
        cast_w("u", wu_b, wu_d, 2 * l, "kf")
        cast_w("d", wd_b, wd_d, 2 * l, "fd")
        cast_w("qkv", wqkv_b, wqkv_d, l, "kf")
        cast_w("o", wo_b, wo_d, l, "o")
        cast_w("g", wg_b, wg_d, 2 * l + 1, "kf")
        cast_w("u", wu_b, wu_d, 2 * l + 1, "kf")
        cast_w("d", wd_b, wd_d, 2 * l + 1, "fd")

    junk = sb("junk", [128, 1], F32)
    S.issue("dve", lambda e: e.memset(junk[:], 0.0), reads=[R_stg[0], R_stg[1]], writes=[R_hb])
    for dst, src in ((cst, cst_d), (gT, gT_d), (wf32, wf_d), (bfb, bf_d)):
        S.dma("sp", C_ld, lambda e, d=dst, s=src: e.dma_start(out=d[:], in_=s), writes=[R_const])
    S.dma("sp", C_ld, lambda e: e.dma_start(out=msk[:], in_=msk_d.rearrange("p (b t) -> p b t", b=8)),
          writes=[R_const])
    S.issue("dve", lambda e: e.tensor_copy(out=wf[:].rearrange("p c h -> p (c h)"), in_=wf32[:]),
            reads=[R_const], writes=[R_const])
    S.issue("dve", lambda e: e.memset(carry_p[:], 0.0), writes=[R_cp])

    def ring_load(src_ap, rw):
        i = ringrot.next()
        S.dma("sp", C_ring[i], lambda e: e.dma_start(out=ring[i][:], in_=src_ap),
              reads=[rw], writes=[R_ring[i]])
        return i

    def evac_copy(eng, out, in_, reads, writes):
        if eng == "act":
            S.issue("act", lambda e: e.activation(out=out, in_=in_, func=AF.Copy), reads=reads, writes=writes)
        else:
            S.issue(eng, lambda e: e.tensor_copy(out=out, in_=in_), reads=reads, writes=writes)

    def rmsnorm(N, gidx, out_bf=True, out_ap=None):
        b = rotA.next()
        for c in range(DC):
            t = f32rot.next()
            S.issue("act", lambda e, c=c, t=t: e.activation(out=f32t[t][:, :N], in_=xT[:, c, :N], func=AF.Square),
                    reads=[R_xT], writes=[R_f32t[t]])
            S.issue("pe", lambda e, c=c, t=t: e.matmul(ps[b][:, :N], lhsT=ones, rhs=f32t[t][:, :N],
                                                      start=(c == 0), stop=(c == DC - 1)),
                    reads=[R_f32t[t], R_const], writes=[R_ps[b]])
        S.issue("act", lambda e: e.activation(out=rstd[:, :N], in_=ps[b][:, :N], func=AF.Sqrt,
                                              scale=1.0 / D, bias=epsb[:, 0:1]),
                reads=[R_ps[b], R_const], writes=[R_rstd])
        S.issue("dve", lambda e: e.reciprocal(out=rstd[:, :N], in_=rstd[:, :N]),
                reads=[R_rstd], writes=[R_rstd])
        for c in range(DC):
            if out_ap is None:
                S.issue("dve", lambda e, c=c: e.scalar_tensor_tensor(
                    out=xn[:, c, :N], in0=xT[:, c, :N], scalar=gT[:, gidx * DC + c: gidx * DC + c + 1],
                    in1=rstd[:, :N], op0=ALU.mult, op1=ALU.mult),
                    reads=[R_xT, R_rstd, R_const], writes=[R_xn])
            else:
                out_ap(c)

    epsb = sb("epsb", [128, 1], F32)
    rstd = sb("rstd", [128, 512], F32); R_rstd = Res("rstd")
    S.issue("dve", lambda e: e.memset(epsb[:], 1e-6), writes=[R_const])

    def ffn(N, l, a):
        wi = 2 * l + a
        wgv = wg_b[wi].rearrange("(kc p) f -> p kc f", p=128)
        wuv = wu_b[wi].rearrange("(kc p) f -> p kc f", p=128)
        wdv = wd_b[wi].rearrange("(fc p) d -> p fc d", p=128)
        rmsnorm(N, 3 * l + (0 if a == 0 else 2))
        for fb in range(8):
            sg = ring_load(wgv[:, :, 512 * fb: 512 * fb + 512], R_w[("g", wi)])
            su = ring_load(wuv[:, :, 512 * fb: 512 * fb + 512], R_w[("u", wi)])
            for fcl in range(4):
                fc = 4 * fb + fcl
                bg = rotA.next()
                bu = rotA.next()
                for kc in range(DC):
                    S.issue("pe", lambda e, kc=kc, fcl=fcl, bg=bg, sg=sg: e.matmul(
                        ps[bg][:, :N], lhsT=ring[sg][:, kc, 128 * fcl: 128 * fcl + 128], rhs=xn[:, kc, :N],
                        start=(kc == 0), stop=(kc == DC - 1)),
                        reads=[R_ring[sg], R_xn], writes=[R_ps[bg]])
                for kc in range(DC):
                    S.issue("pe", lambda e, kc=kc, fcl=fcl, bu=bu, su=su: e.matmul(
                        ps[bu][:, :N], lhsT=ring[su][:, kc, 128 * fcl: 128 * fcl + 128], rhs=xn[:, kc, :N],
                        start=(kc == 0), stop=(kc == DC - 1)),
                        reads=[R_ring[su], R_xn], writes=[R_ps[bu]])
                t = f32rot.next()
                S.issue("act", lambda e, bg=bg, t=t: e.activation(out=f32t[t][:, :N], in_=ps[bg][:, :N], func=AF.Silu),
                        reads=[R_ps[bg]], writes=[R_f32t[t]])
                S.issue("dve", lambda e, bu=bu, t=t, fc=fc: e.tensor_tensor(
                    out=h_v[:, fc, :N], in0=f32t[t][:, :N], in1=ps[bu][:, :N], op=ALU.mult),
                    reads=[R_f32t[t], R_ps[bu]], writes=[R_hb])
        for dp in range(2):
            banks = [4, 5, 6, 7]
            for s in range(4):
                sd = ring_load(wdv[:, 8 * s: 8 * s + 8, 512 * dp: 512 * dp + 512], R_w[("d", wi)])
                for m in range(4):
                    for fcl in range(8):
                        S.issue("pe", lambda e, m=m, fcl=fcl, s=s, sd=sd: e.matmul(
                            ps[banks[m]][:, :N], lhsT=ring[sd][:, fcl, 128 * m: 128 * m + 128],
                            rhs=h_v[:, 8 * s + fcl, :N], start=(s == 0 and fcl == 0), stop=(s == 3 and fcl == 7)),
                            reads=[R_ring[sd], R_hb], writes=[R_ps[banks[m]]])
            for m in range(4):
                c = 4 * dp + m
                S.issue("dve", lambda e, m=m, c=c: e.scalar_tensor_tensor(
                    out=xT[:, c, :N], in0=ps[banks[m]][:, :N], scalar=0.5, in1=xT[:, c, :N],
                    op0=ALU.mult, op1=ALU.add),
                    reads=[R_ps[banks[m]], R_xT], writes=[R_xT])

    def load_x(src_rows, rows, nsub):
        N = rows * nsub
        S.dma("sp", C_ld, lambda e: e.dma_start(out=stgA[0:rows, 0:nsub, :],
                                                in_=src_rows.rearrange("(j p) d -> p j d", p=rows)),
              writes=[R_hb])
        for c in range(DC):
            b = rotA.next()
            for j in range(nsub):
                S.issue("pe", lambda e, c=c, j=j, b=b: e.transpose(
                    out=ps[b][:, j * rows:(j + 1) * rows], in_=stgA[0:rows, j, 128 * c:128 * c + 128],
                    identity=ident[0:rows, 0:rows]),
                    reads=[R_hb, R_const], writes=[R_ps[b]])
            evac_copy("act" if c % 2 == 0 else "dve", xT[:, c, :N], ps[b][:, :N], [R_ps[b]], [R_xT])

    def qkv(N, rows, nsub, l, k_out, v_out, lf_out):
        wv = wqkv_b[l].rearrange("(kc p) f -> p kc f", p=128)
        rmsnorm(N, 3 * l + 1)
        KQ = int(os.environ.get("KQ", "9"))
        if KQ <= 2:
            k_out = None
        if KQ <= 4:
            v_out = None
        for s in range(6):
            if (KQ <= 1 and s >= 2) or (KQ <= 3 and s >= 4):
                break
            sl = ring_load(wv[:, :, 512 * s:512 * s + 512], R_w[("qkv", l)])
            for ml in range(4):
                m = 4 * s + ml
                b = rotB.next()
                for kc in range(DC):
                    S.issue("pe", lambda e, kc=kc, ml=ml, b=b, sl=sl: e.matmul(
                        ps[b][:, :N], lhsT=ring[sl][:, kc, 128 * ml:128 * ml + 128], rhs=xn[:, kc, :N],
                        start=(kc == 0), stop=(kc == DC - 1)),
                        reads=[R_ring[sl], R_xn], writes=[R_ps[b]])
                c = m % DC
                if m < 8:
                    for g in range(2):
                        S.issue("dve", lambda e, b=b, c=c, g=g: e.tensor_scalar(
                            out=qT2[g][64 * g:64 * g + 64, c, :N], in0=ps[b][64 * g:64 * g + 64, :N], scalar1=0.125,
                            scalar2=None, op0=ALU.mult), reads=[R_ps[b]], writes=[R_qT])
                elif m < 16:
                    if k_out is None:
                        S.issue("dve", lambda e, b=b, c=c: e.tensor_copy(out=kT[:, c, :N], in_=ps[b][:, :N]),
                                reads=[R_ps[b]], writes=[R_kT])
                    else:
                        t = f32rot.next()
                        evac_copy("act", f32t[t][:, :N], ps[b][:, :N], [R_ps[b]], [R_f32t[t]])
                        S.issue("dve", lambda e, t=t, c=c: e.tensor_copy(out=kT[:, c, :N], in_=f32t[t][:, :N]),
                                reads=[R_f32t[t]], writes=[R_kT])
                        b2 = rotA.next()
                        for j in range(nsub):
                            S.issue("pe", lambda e, j=j, b2=b2, t=t: e.transpose(
                                out=ps[b2][0:rows, 128 * j:128 * j + 128], in_=f32t[t][:, j * rows:(j + 1) * rows],
                                identity=ident), reads=[R_f32t[t], R_const], writes=[R_ps[b2]])
                        pv = ps[b2][0:rows, 0:128 * nsub].rearrange("p (j d) -> p j d", j=nsub)
                        evac_copy("act", stgA[0:rows, 0:nsub, 128 * c:128 * c + 128], pv, [R_ps[b2]], [R_hb])
                else:
                    t = f32rot.next()
                    evac_copy("act", f32t[t][:, :N], ps[b][:, :N], [R_ps[b]], [R_f32t[t]])
                    b2 = rotA.next()
                    for j in range(nsub):
                        S.issue("pe", lambda e, j=j, b2=b2, t=t: e.transpose(
                            out=ps[b2][0:rows, 128 * j:128 * j + 128], in_=f32t[t][:, j * rows:(j + 1) * rows],
                            identity=ident), reads=[R_f32t[t], R_const], writes=[R_ps[b2]])
                    pv = ps[b2][0:rows, 0:128 * nsub].rearrange("p (j d) -> p j d", j=nsub)
                    if v_out is not None:
                        evac_copy("act", stgB[0:rows, 0:nsub, 128 * c:128 * c + 128], pv, [R_ps[b2]], [R_hb])
                    for j in range(nsub):
                        if v_out is not None:
                            src = stgB[0:rows, j, 128 * c:128 * c + 128].rearrange("p (g d) -> p g d", g=2)
                            rr = R_hb
                        else:
                            src = ps[b2][0:rows, 128 * j:128 * j + 128].rearrange("p (g d) -> p g d", g=2)
                            rr = R_ps[b2]
                        S.issue("dve", lambda e, c=c, j=j, src=src: e.tensor_copy(
                            out=vA[0:rows, j, 2 * c:2 * c + 2, 0:HD], in_=src), reads=[rr], writes=[R_vA])
        if k_out is not None:
            S.dma("sp", C_stA, lambda e: e.dma_start(out=k_out.rearrange("(j p) d -> p j d", p=rows),
                                                     in_=stgA[0:rows, 0:nsub, :]), reads=[R_hb])
        if v_out is not None:
            S.dma("sp", C_stB, lambda e: e.dma_start(out=v_out.rearrange("(j p) d -> p j d", p=rows),
                                                     in_=stgB[0:rows, 0:nsub, :]), reads=[R_hb])

    def logf_cum(N, rows, nsub, lf_out, ncum_dst, R_ncd, carry, R_carry):
        for j in range(nsub):
            b = rotA.next()
            for kc in range(DC):
                S.issue("pe", lambda e, kc=kc, j=j, b=b: e.matmul(
                    ps[b][0:rows, 0:H], lhsT=xn[:, kc, j * rows:(j + 1) * rows], rhs=wf[:, kc, :],
                    start=(kc == 0), stop=(kc == DC - 1)), reads=[R_xn, R_const], writes=[R_ps[b]])
            S.issue("dve", lambda e, j=j, b=b: e.tensor_tensor(out=zt[0:rows, j, :], in0=ps[b][0:rows, 0:H],
                                                               in1=bfb[0:rows, :], op=ALU.add),
                    reads=[R_ps[b], R_const], writes=[R_zt])
        S.issue("act", lambda e: e.activation(out=zt[0:rows, 0:nsub, :], in_=zt[0:rows, 0:nsub, :], func=AF.Exp, scale=-1.0),
                reads=[R_zt], writes=[R_zt])
        S.issue("act", lambda e: e.activation(out=zt[0:rows, 0:nsub, :], in_=zt[0:rows, 0:nsub, :], func=AF.Ln, bias=1.0),
                reads=[R_zt], writes=[R_zt])
        S.issue("dve", lambda e: e.tensor_scalar(out=lft[0:rows, 0:nsub, :], in0=zt[0:rows, 0:nsub, :], scalar1=-1.0,
                                                 scalar2=None, op0=ALU.mult), reads=[R_zt], writes=[R_lft])
        S.dma("sp", C_stL, lambda e: e.dma_start(out=lf_out.rearrange("(j p) h -> p j h", p=rows),
                                                 in_=lft[0:rows, 0:nsub, :]), reads=[R_lft])
        cum_blocks(rows, nsub, lambda j: lft[0:rows, j, :], R_lft, ncum_dst, R_ncd, carry, R_carry)

    def cum_blocks(rows, nsub, lf_fn, R_lf, ncum_dst, R_ncd, carry, R_carry):
        for j in range(nsub):
            b = rotA.next()
            S.issue("pe", lambda e, j=j, b=b: e.matmul(ps[b][0:rows, 0:H], lhsT=tri[0:rows, 0:rows], rhs=lf_fn(j),
                                                       start=True, stop=True),
                    reads=[R_lf, R_const], writes=[R_ps[b]])
            b2 = rotA.next()
            S.issue("pe", lambda e, j=j, b2=b2: e.matmul(ps[b2][:, 0:H], lhsT=ones[0:rows, :], rhs=lf_fn(j),
                                                         start=True, stop=True),
                    reads=[R_lf, R_const], writes=[R_ps[b2]])
            S.issue("dve", lambda e, j=j, b=b: e.scalar_tensor_tensor(
                out=ncum_dst(j), in0=ps[b][0:rows, 0:H], scalar=-1.0, in1=carry[0:rows, :],
                op0=ALU.mult, op1=ALU.subtract), reads=[R_ps[b], R_carry], writes=[R_ncd])
            S.issue("dve", lambda e, b2=b2: e.tensor_tensor(out=carry, in0=carry, in1=ps[b2][:, 0:H], op=ALU.add),
                    reads=[R_ps[b2], R_carry], writes=[R_carry])

    def spill_kv(l, t0, blk0, nsub):
        S.dma("pool", C_spk, lambda e: e.dma_start(out=ksp[l][:, :, t0:t0 + 512].rearrange("c p t -> p c t"),
                                                 in_=kT[:, :, :]), reads=[R_kT], writes=[R_ksp[l]])
        for c in range(DC):
            S.dma("pool", C_spv, lambda e, c=c: e.dma_start(
                out=vsp[l][c, :, blk0:blk0 + nsub, :],
                in_=vA[:, 0:nsub, 2 * c:2 * c + 2, :].rearrange("p b g e -> p b (g e)")),
                reads=[R_vA], writes=([R_vsp[l]] if c == DC - 1 else []))

    def attention(kind, N, rows, nsub, segs, ncum_cur, R_ncur):
        S.issue("dve", lambda e: e.memset(junk2[:], 0.0), writes=[R_hb, R_at])
        for c in range(DC):
            accs = [rotO.next(), rotO.next()]
            started = [False, False]
            cqs = [None, None]
            rss = [None, None]
            for g in range(2):
                h = 2 * c + g
                if kind == "band":
                    r = rskrot.next()
                    S.dma("sp", C_rsk[r], lambda e, r=r, h=h: e.dma_start(out=rsk[r][:], in_=rsk_d[h]),
                          writes=[R_rsk[r]])
                    rss[g] = r
                else:
                    q = cqrot.next()
                    b = rotS.next()
                    for j in range(nsub):
                        dg = dgrot.next()
                        S.issue("dve", lambda e, j=j, dg=dg, h=h: e.tensor_scalar(
                            out=dgt[dg][0:rows, 0:rows], in0=ident[0:rows, 0:rows],
                            scalar1=ncum_cur(j)[:, h:h + 1], scalar2=-1.0, op0=ALU.mult, op1=ALU.mult),
                            reads=[R_const, R_ncur], writes=[R_dgt[dg]])
                        S.issue("pe", lambda e, j=j, dg=dg, b=b: e.matmul(
                            ps[b][:, j * rows:(j + 1) * rows], lhsT=ones[0:rows, :], rhs=dgt[dg][0:rows, 0:rows],
                            start=True, stop=True), reads=[R_dgt[dg], R_const], writes=[R_ps[b]])
                    evac_copy("act", cq[q][:, :N], ps[b][:, :N], [R_ps[b]], [R_cq[q]])
                    cqs[g] = q

            pending = []

            def block(g, kslab, vslab, Rk, Rv, krows, col_lo, col_hi, addend, bias):
                h = 2 * c + g
                base = 64 * g
                ncol = col_hi - col_lo
                bs = rotS.next()
                S.issue("pe", lambda e: e.matmul(ps[bs][0:krows, 0:ncol], lhsT=kslab,
                                                 rhs=qT2[g][:, c, col_lo:col_hi], start=True, stop=True),
                        reads=[Rk, R_qT], writes=[R_ps[bs]])
                t = f32rot.next()
                addend(t, bs, krows, ncol)
                p = ptrot.next()
                if bias is None:
                    S.issue("act", lambda e: e.activation(out=pt[p][0:krows, 0:ncol], in_=f32t[t][0:krows, 0:ncol],
                                                          func=AF.Exp), reads=[R_f32t[t]], writes=[R_pt[p]])
                else:
                    bap, Rb = bias
                    S.issue("act", lambda e: e.activation(out=pt[p][0:krows, 0:ncol], in_=f32t[t][0:krows, 0:ncol],
                                                          func=AF.Exp, bias=bap), reads=[R_f32t[t], Rb],
                            writes=[R_pt[p]])
                def stage2():
                    first = not started[g]
                    started[g] = True
                    S.issue("pe", lambda e: e.matmul(ps[accs[g]][0:HD + 1, col_lo:col_hi], lhsT=vslab,
                                                     rhs=pt[p][0:krows, 0:ncol], start=first, stop=True,
                                                     skip_group_check=True),
                            reads=[Rv, R_pt[p]], writes=[R_ps[accs[g]]])
                pending.append(stage2)
                if len(pending) > LOOKAHEAD:
                    pending.pop(0)()

            def flush():
                while pending:
                    pending.pop(0)()

            def add_plain(src_ap, Rsrc):
                def f(t, bs, krows, ncol):
                    S.issue("dve", lambda e: e.tensor_tensor(out=f32t[t][0:krows, 0:ncol], in0=ps[bs][0:krows, 0:ncol],
                                                             in1=src_ap(krows, ncol), op=ALU.add),
                            reads=[R_ps[bs], Rsrc], writes=[R_f32t[t]])
                return f

            def add_masked(src_ap, Rsrc, mask_ap):
                def f(t, bs, krows, ncol):
                    S.issue("dve", lambda e: e.tensor_tensor(out=f32t[t][0:krows, 0:ncol], in0=src_ap(krows, ncol),
                                                              in1=mask_ap(krows, ncol), op=ALU.add),
                            reads=[Rsrc, R_const], writes=[R_f32t[t]])
                    S.issue("dve", lambda e: e.tensor_tensor(out=f32t[t][0:krows, 0:ncol], in0=ps[bs][0:krows, 0:ncol],
                                                             in1=f32t[t][0:krows, 0:ncol], op=ALU.add),
                            reads=[R_ps[bs], R_f32t[t]], writes=[R_f32t[t]])
                return f

            for seg in segs:
              nk = seg["nk"]
              ks_ap, vs_ap, Rk_d, Rv_d, k0 = seg["ks"], seg["vs"], seg["Rk"], seg["Rv"], seg["k0"]
              k_done = 0
              while k_done < nk:
                n = min(1024, nk - k_done)
                sl = hsrot.next()
                kb0 = (k0 + k_done) // 128
                S.dma("sp", C_hk[sl], lambda e, sl=sl, n=n, kd=k_done: e.dma_start(
                    out=hk[sl][:, 0:n], in_=ks_ap[c, :, k0 + kd:k0 + kd + n]), reads=[Rk_d], writes=[R_hs[sl]])
                S.dma("sp", C_hv[sl], lambda e, sl=sl, n=n, kb0=kb0: e.dma_start(
                    out=hv[sl][:, 0:n // 128, :], in_=vs_ap[c, :, kb0:kb0 + n // 128, :]),
                    reads=[Rv_d], writes=[R_hs[sl]])
                for g in range(2):
                    h = 2 * c + g
                    for bl in range(n // 128):
                        kb = k_done // 128 + bl
                        kslab = hk[sl][:, 128 * bl:128 * bl + 128]
                        vslab = hv[sl][:, bl, VW * g:VW * g + HD + 1]
                        if kind == "band":
                            b = kb
                            hb_ = None if seg.get("hbias") is None else (seg["hbias"], R_const)
                            if rows == 128:
                                hi = 128 * (b + 1)
                                r = rss[g]
                                block(g, kslab, vslab, R_hs[sl], R_hs[sl], 128, 0, hi,
                                      add_masked(lambda kr, ncn, r=r, b=b: rsk[r][0:kr, 512 - 128 * b:512 - 128 * b + ncn],
                                                 R_rsk[r], lambda kr, ncn, b=b: msk[0:kr, b, 0:ncn]), hb_)
                            else:
                                r = rss[g]
                                block(g, kslab, vslab, R_hs[sl], R_hs[sl], 128, 0, N,
                                      add_plain(lambda kr, ncn, r=r, b=b: rsk[r][0:kr, 512 - 128 * b:512 - 128 * b + ncn],
                                                R_rsk[r]), hb_)
                        else:
                            q = cqs[g]
                            block(g, kslab, vslab, R_hs[sl], R_hs[sl], 128, 0, N,
                                  add_plain(lambda kr, ncn, q=q: cq[q][0:kr, 0:ncn], R_cq[q]),
                                  (seg["ncum"](kb)[:, h:h + 1], seg["Rnc"]))
                k_done += n
            for g in range(2):
                h = 2 * c + g
                for j in range(nsub):
                    kslab = kT[:, c, j * rows:(j + 1) * rows]
                    vslab = vA[0:rows, j, h, 0:HD + 1]
                    lo = j * rows
                    if kind == "band":
                        r = rss[g]
                        b = 4 + j
                        if rows == 128:
                            block(g, kslab, vslab, R_kT, R_vA, rows, lo, N,
                                  add_masked(lambda kr, ncn, r=r, b=b, lo=lo: rsk[r][0:kr, 512 - 128 * b + lo:512 - 128 * b + lo + ncn],
                                             R_rsk[r], lambda kr, ncn, b=b, lo=lo: msk[0:kr, b, lo:lo + ncn]), None)
                        else:
                            block(g, kslab, vslab, R_kT, R_vA, rows, 0, N,
                                  add_plain(lambda kr, ncn, r=r: rsk[r][0:kr, 0:ncn], R_rsk[r]), None)
                    else:
                        q = cqs[g]

                        def add_diag(t, bs, krows, ncol, q=q, lo=lo):
                            S.issue("dve", lambda e: e.tensor_tensor(
                                out=f32t[t][0:krows, 0:krows], in0=cq[q][0:krows, lo:lo + krows],
                                in1=trineg[0:krows, 0:krows], op=ALU.add),
                                reads=[R_cq[q], R_const], writes=[R_f32t[t]])
                            S.issue("dve", lambda e: e.tensor_tensor(
                                out=f32t[t][0:krows, 0:krows], in0=ps[bs][0:krows, 0:krows],
                                in1=f32t[t][0:krows, 0:krows], op=ALU.add),
                                reads=[R_ps[bs], R_f32t[t]], writes=[R_f32t[t]])
                            if ncol > krows:
                                S.issue("dve", lambda e: e.tensor_tensor(
                                    out=f32t[t][0:krows, krows:ncol], in0=ps[bs][0:krows, krows:ncol],
                                    in1=cq[q][0:krows, lo + krows:lo + ncol], op=ALU.add),
                                    reads=[R_ps[bs], R_cq[q]], writes=[R_f32t[t]])
                        block(g, kslab, vslab, R_kT, R_vA, rows, lo, N, add_diag,
                              (ncum_cur(j)[:, h:h + 1], R_ncur))
            flush()
            for g in range(2):
                h = 2 * c + g
                a = accs[g]
                evac_copy("act", oT[0:HD + 1, :N], ps[a][0:HD + 1, :N], [R_ps[a]], [R_oT])
                S.issue("act", lambda e: e.activation(out=rd[64:65, :N], in_=oT[64:65, :N], func=AF.Ln),
                        reads=[R_oT], writes=[R_rd])
                S.issue("act", lambda e: e.activation(out=rd[64:65, :N], in_=rd[64:65, :N], func=AF.Exp, scale=-1.0),
                        reads=[R_rd], writes=[R_rd])
                b = rotS.next()
                S.issue("pe", lambda e, b=b: e.matmul(ps[b][0:HD, :N], lhsT=ones[64:65, 0:HD], rhs=rd[64:65, :N],
                                                      start=True, stop=True), reads=[R_rd, R_const], writes=[R_ps[b]])
                if g == 0:
                    S.issue("dve", lambda e, b=b: e.tensor_tensor(out=attnT2[0:HD, c, :N], in0=oT[0:HD, :N],
                                                                  in1=ps[b][0:HD, :N], op=ALU.mult),
                            reads=[R_oT, R_ps[b], R_at])
                else:
                    o = oddrot.next()
                    S.issue("dve", lambda e, b=b, o=o: e.tensor_tensor(out=oddst[o][:, :N], in0=oT[0:HD, :N],
                                                                       in1=ps[b][0:HD, :N], op=ALU.mult),
                            reads=[R_oT, R_ps[b]], writes=[R_odd[o]])
                    S.dma("pool", C_odd[o], lambda e, o=o: e.dma_start(out=attnT2[HD:128, c, :N], in_=oddst[o][:, :N]),
                          reads=[R_odd[o], R_at])

    def wo_proj(N, l):
        wv = wo_b[l].rearrange("(kc p) d -> p kc d", p=128)
        for s in range(2):
            i = ring_load(wv[:, :, 512 * s:512 * s + 512], R_w[("o", l)])
            for ml in range(4):
                m = 4 * s + ml
                b = rotA.next()
                for kc in range(DC):
                    S.issue("pe", lambda e, kc=kc, ml=ml, b=b, i=i: e.matmul(
                        ps[b][:, :N], lhsT=ring[i][:, kc, 128 * ml:128 * ml + 128], rhs=attnT2[:, kc, :N],
                        start=(kc == 0), stop=(kc == DC - 1)), reads=[R_ring[i]], writes=[R_ps[b], R_at])
                S.issue("dve", lambda e, m=m, b=b: e.tensor_tensor(out=xT[:, m, :N], in0=xT[:, m, :N], in1=ps[b][:, :N],
                                                                   op=ALU.add), reads=[R_ps[b], R_xT], writes=[R_xT])
        S.issue("dve", lambda e: e.memset(junk2[:], 0.0), writes=[R_hb, R_at])

    def final_out(N, rows, nsub, dst_rows):
        tl = {}

        def out_ap(c):
            t2 = f32rot.next()
            tl[c] = t2
            S.issue("dve", lambda e: e.scalar_tensor_tensor(
                out=f32t[t2][:, :N], in0=xT[:, c, :N], scalar=gT[:, 6 * DC + c:6 * DC + c + 1], in1=rstd[:, :N],
                op0=ALU.mult, op1=ALU.mult), reads=[R_xT, R_rstd, R_const], writes=[R_f32t[t2]])
            b = rotB.next()
            for j in range(nsub):
                S.issue("pe", lambda e, j=j: e.transpose(
                    out=ps[b][0:rows, 128 * j:128 * j + 128], in_=f32t[t2][:, j * rows:(j + 1) * rows], identity=ident),
                    reads=[R_f32t[t2], R_const], writes=[R_ps[b]])
            pv = ps[b][0:rows, 0:128 * nsub].rearrange("p (j d) -> p j d", j=nsub)
            evac_copy("act", stgA[0:rows, 0:nsub, 128 * c:128 * c + 128], pv, [R_ps[b]], [R_hb])
        rmsnorm(N, 6, out_ap=out_ap)
        S.dma("sp", C_stA, lambda e: e.dma_start(out=dst_rows.rearrange("(j p) d -> p j d", p=rows),
                                                 in_=stgA[0:rows, 0:nsub, :]), reads=[R_hb])

    S.issue("dve", lambda e: e.memset(qT2[0][64:128, :, :], 0.0), writes=[R_qT])
    S.issue("dve", lambda e: e.memset(qT2[1][0:64, :, :], 0.0), writes=[R_qT])
    S.issue("dve", lambda e: e.memset(vA[:, :, :, HD:VW], 1.0), writes=[R_vA])

    def cache_prep(u):
        for l, (ck, cv, nkc) in enumerate(((cbk_d, cbv_d, 512), (cfk_d, cfv_d, 1024))):
            for half in range(nkc // 512):
                r0 = 512 * half
                S.dma("sp", C_ld, lambda e, ck=ck, r0=r0: e.dma_start(
                    out=stgA[:, :, :], in_=ck[u, r0:r0 + 512, :].rearrange("(j p) d -> p j d", p=128)),
                    writes=[R_hb])
                for c in range(DC):
                    b = rotA.next()
                    for j in range(4):
                        S.issue("pe", lambda e, c=c, j=j, b=b: e.transpose(
                            out=ps[b][:, 128 * j:128 * j + 128], in_=stgA[:, j, 128 * c:128 * c + 128], identity=ident),
                            reads=[R_hb, R_const], writes=[R_ps[b]])
                    evac_copy("act" if c % 2 == 0 else "dve", kT[:, c, :], ps[b][:, :], [R_ps[b]], [R_kT])
                S.dma("pool", C_spk, lambda e, l=l, r0=r0: e.dma_start(
                    out=kss[l][u, :, :, r0:r0 + 512].rearrange("c p t -> p c t"), in_=kT[:, :, :]),
                    reads=[R_kT], writes=[R_kss[l]])
                S.dma("sp", C_ld, lambda e, cv=cv, r0=r0: e.dma_start(
                    out=stgB[:, :, :], in_=cv[u, r0:r0 + 512, :].rearrange("(j p) d -> p j d", p=128)),
                    writes=[R_hb])
                S.issue("dve", lambda e: e.tensor_copy(out=vA[:, :, :, 0:HD],
                                                       in_=stgB[:, :, :].rearrange("p j (h d) -> p j h d", h=H)),
                        reads=[R_hb], writes=[R_vA])
                for c in range(DC):
                    S.dma("pool", C_spv, lambda e, l=l, half=half, c=c: e.dma_start(
                        out=vss[l][u, c, :, 4 * half:4 * half + 4, :],
                        in_=vA[:, :, 2 * c:2 * c + 2, :].rearrange("p b g e -> p b (g e)")),
                        reads=[R_vA], writes=([R_vss[l]] if c == DC - 1 else []))
        S.dma("sp", C_ld, lambda e: e.dma_start(out=zt[:, :, :], in_=cfl_d[u, 0:512, :].rearrange("(j p) h -> p j h", p=128)),
              writes=[R_zt])
        S.dma("sp", C_ld2, lambda e: e.dma_start(out=lft[:, :, :], in_=cfl_d[u, 512:1024, :].rearrange("(j p) h -> p j h", p=128)),
              writes=[R_lft])
        S.issue("dve", lambda e: e.memset(carry_s[:, u, :], 0.0), writes=[R_cs])
        cum_blocks(128, 4, lambda j: zt[:, j, :], R_zt, lambda j: ncum_s[:, u, j, :], R_ncs, carry_s[:, u, :], R_cs)
        cum_blocks(128, 4, lambda j: lft[:, j, :], R_lft, lambda j: ncum_s[:, u, 4 + j, :], R_ncs, carry_s[:, u, :], R_cs)

    def tile(N, rows, nsub, x_src, seq):
        import os
        KSTOP = int(os.environ.get("KSTOP", "99"))
        load_x(x_src, rows, nsub)
        if KSTOP <= 1:
            return final_out(N, rows, nsub, seq["y_out"])
        ffn(N, 0, 0)
        if KSTOP <= 2:
          if os.environ.get("KDBG", "0") == "1":
            S.dma("sp", C_ld, lambda e: e.dma_start(out=dbg_d[:, 0:8, :], in_=xn[:, :, :]), reads=[R_xn])
            S.dma("sp", C_ld, lambda e: e.dma_start(out=dbg_d[:, 8:12, :], in_=h_v[:, 0:4, :]), reads=[R_hb])
            return final_out(N, rows, nsub, seq["y_out"])
        qkv(N, rows, nsub, 0, seq["bk_out"], seq["bv_out"], None)
        if seq["spill0"] is not None:
            spill_kv(0, *seq["spill0"])
        if KSTOP <= 3:
            return final_out(N, rows, nsub, seq["y_out"])
        attention("band", N, rows, nsub, seq["hist0"], None, None)
        if KSTOP <= 4:
            return final_out(N, rows, nsub, seq["y_out"])
        wo_proj(N, 0)
        ffn(N, 0, 1)
        if KSTOP <= 5:
            return final_out(N, rows, nsub, seq["y_out"])
        if L1:
            ffn(N, 1, 0)
            qkv(N, rows, nsub, 1, seq["fk_out"], seq["fv_out"], None)
            logf_cum(N, rows, nsub, seq["fl_out"], seq["ncum_cur"], seq["R_ncur"], seq["carry"], seq["R_carry"])
            if seq["spill1"] is not None:
                spill_kv(1, *seq["spill1"])
            attention("fox", N, rows, nsub, seq["hist1"], seq["ncum_cur"], seq["R_ncur"])
            wo_proj(N, 1)
            ffn(N, 1, 1)
        final_out(N, rows, nsub, seq["y_out"])

    def sample_tiles():
        for u in range(NS):
            cache_prep(u)
            seq = dict(
                bk_out=bks_d[u], bv_out=bvs_d[u], spill0=None,
                hist0=[dict(ks=kss[0][u], vs=vss[0][u], Rk=R_kss[0], Rv=R_vss[0], k0=0, nk=512)],
                fk_out=fks_d[u], fv_out=fvs_d[u], fl_out=fls_d[u],
                ncum_cur=(lambda j: ncum_c[0:16, :]), R_ncur=R_ncc, carry=carry_s[:, u, :], R_carry=R_cs,
                spill1=None,
                hist1=[dict(ks=kss[1][u], vs=vss[1][u], Rk=R_kss[1], Rv=R_vss[1], k0=0, nk=1024,
                            ncum=(lambda kb, u=u: ncum_s[:, u, kb, :]), Rnc=R_ncs)],
                y_out=ys_d[u],
            )
            tile(16, 16, 1, xs_d[u], seq)

    if not PAIR:
        for i in range(NT):
            t0 = 512 * i
            last = (i == NT - 1)
            seq = dict(
                bk_out=bkp_d if last else None, bv_out=bvp_d if last else None,
                spill0=(t0, 4 * i, 4) if not last else None,
                hist0=[] if i == 0 else [dict(ks=ksp[0], vs=vsp[0], Rk=R_ksp[0], Rv=R_vsp[0], k0=t0 - 512, nk=512)],
                fk_out=fkp_d[t0:t0 + 512, :], fv_out=fvp_d[t0:t0 + 512, :], fl_out=flp_d[t0:t0 + 512, :],
                ncum_cur=(lambda j, i=i: ncum_p[:, 4 * i + j, :]), R_ncur=R_ncp, carry=carry_p[:, :], R_carry=R_cp,
                spill1=(t0, 4 * i, 4) if not last else None,
                hist1=[] if i == 0 else [dict(ks=ksp[1], vs=vsp[1], Rk=R_ksp[1], Rv=R_vsp[1], k0=0, nk=t0,
                                              ncum=(lambda kb: ncum_p[:, kb, :]), Rnc=R_ncp)],
                y_out=y_d[t0:t0 + 512, :],
            )
            tile(512, 128, 4, x_d[t0:t0 + 512, :], seq)
    else:
        hmask = sb("hmask", [128, 1], F32)
        ncum_prev = sb("ncum_prev", [128, NBLK + 1, H], F32); R_ncprev = Res("ncum_prev")
        nprev = sb("nprev", [128, NBLK, H], F32); R_nprev = Res("nprev")
        totm = sb("totm", [128, H], F32)
        C_xs, C_qs, C_nc = S.chan("xs"), S.chan("qs"), S.chan("ncst")
        C_xl, C_ql, C_kl, C_vl, C_ncl = S.chan("xl"), S.chan("ql"), S.chan("kl"), S.chan("vl"), S.chan("ncl")
        C_cc = S.chan("cc")
        S.dma("sp", C_ld, lambda e: e.dma_start(out=hmask[:], in_=hm_d), writes=[R_const])

        load_x(xh_d, 128, 4)
        ffn(512, 0, 0)
        qkv(512, 128, 4, 0, None, None, None)
        spill_kv(0, 0, 0, 4)

        for i in range(NT):
            t0 = 512 * i
            last = (i == NT - 1)
            load_x(x_d[t0:t0 + 512, :], 128, 4)
            ffn(512, 0, 0)
            qkv(512, 128, 4, 0, bkp_d if last else None, bvp_d if last else None, None)
            if not last:
                spill_kv(0, t0 + 512, 4 * (i + 1), 4)
            attention("band", 512, 128, 4,
                      [dict(ks=ksp[0], vs=vsp[0], Rk=R_ksp[0], Rv=R_vsp[0], k0=t0, nk=512,
                            hbias=(hmask[:, 0:1] if i == 0 else None))], None, None)
            wo_proj(512, 0)
            ffn(512, 0, 1)
            ffn(512, 1, 0)
            qkv(512, 128, 4, 1, fkp_d[t0:t0 + 512, :], fvp_d[t0:t0 + 512, :], None)
            logf_cum(512, 128, 4, flp_d[t0:t0 + 512, :], (lambda j, i=i: ncum_p[:, 4 * i + j, :]), R_ncp,
                     carry_p[:, :], R_cp)
            spill_kv(1, t0, 4 * i, 4)
            S.dma("pool", C_xs, lambda e, i=i: e.dma_start(out=xsp[i], in_=xT[:, :, :].rearrange("p c t -> p (c t)")),
                  reads=[R_xT], writes=[R_xsp])
            for g in range(2):
                S.dma("pool", C_qs, lambda e, i=i, g=g: e.dma_start(
                    out=qsp[i, g], in_=qT2[g][:, :, :].rearrange("p c t -> p (c t)")),
                    reads=[R_qT], writes=([R_qsp] if g == 1 else []))

        S.dma("pool", C_nc, lambda e: e.dma_start(out=ncd[:, 0:NBLK * H], in_=ncum_p[:, :, :].rearrange("p b h -> p (b h)")),
              reads=[R_ncp])
        S.dma("pool", C_nc, lambda e: e.dma_start(out=ncd[:, NBLK * H:(NBLK + 1) * H], in_=carry_p[:, :]),
              reads=[R_cp], writes=[R_ncd])
        PAIRS = [[0, 1], [2, 3], [4, 5], [6, 7]]
        def gather(src_ap, dst_ap, Rs, Rd, last):
            S.dma("pool", C_cc, lambda e: e.collective_compute(
                "AllGather", ALU.bypass, replica_groups=PAIRS, ins=[src_ap.opt()], outs=[dst_ap.opt()]),
                reads=[Rs], writes=([Rd] if last else []), coll=True)
        gather(ncd, ncg, R_ncd, R_ncg, True)
        for c in range(DC):
            gather(ksp1_2d[128 * c:128 * c + 128, :], ksg_2d[256 * c:256 * c + 256, :], R_ksp[1], R_ksg, c == DC - 1)
            gather(vsp1_2d[128 * c:128 * c + 128, :], vsg_2d[256 * c:256 * c + 256, :], R_vsp[1], R_vsg, c == DC - 1)
        sample_tiles()
        S.dma("sp", C_ncl, lambda e: e.dma_start(out=ncum_prev[:, :, :].rearrange("p b h -> p (b h)"), in_=ncg[0:128, :]),
              reads=[R_ncg], writes=[R_ncprev])
        S.issue("dve", lambda e: e.tensor_scalar(out=totm[:, :], in0=ncum_prev[:, NBLK, :], scalar1=hmask[:, 0:1],
                                                 scalar2=None, op0=ALU.add), reads=[R_ncprev, R_const], writes=[R_nprev])
        for kb in range(NBLK):
            S.issue("dve", lambda e, kb=kb: e.tensor_tensor(out=nprev[:, kb, :], in0=ncum_prev[:, kb, :], in1=totm[:, :],
                                                            op=ALU.add), reads=[R_ncprev, R_nprev], writes=[R_nprev])

        for i in range(NT):
            t0 = 512 * i
            S.dma("sp", C_xl, lambda e, i=i: e.dma_start(out=xT[:, :, :].rearrange("p c t -> p (c t)"), in_=xsp[i]),
                  reads=[R_xsp], writes=[R_xT])
            for g in range(2):
                S.dma("sp", C_ql, lambda e, i=i, g=g: e.dma_start(
                    out=qT2[g][:, :, :].rearrange("p c t -> p (c t)"), in_=qsp[i, g]),
                    reads=[R_qsp], writes=([R_qT] if g == 1 else []), wars=[R_qT])
            S.dma("sp", C_kl, lambda e, t0=t0: e.dma_start(
                out=kT[:, :, :], in_=ksp[1][:, :, t0:t0 + 512].rearrange("c p t -> p c t")),
                reads=[R_ksp[1]], writes=[R_kT])
            for c in range(DC):
                S.dma("sp", C_vl, lambda e, i=i, c=c: e.dma_start(
                    out=vA[:, 0:4, 2 * c:2 * c + 2, :].rearrange("p b g e -> p b (g e)"),
                    in_=vsp[1][c, :, 4 * i:4 * i + 4, :]),
                    reads=[R_vsp[1]], writes=([R_vA] if c == DC - 1 else []), wars=[R_vA])
            segs = [dict(ks=ksg[0], vs=vsg[0], Rk=R_ksg, Rv=R_vsg, k0=0, nk=T,
                         ncum=(lambda kb: nprev[:, kb, :]), Rnc=R_nprev)]
            if i > 0:
                segs.append(dict(ks=ksp[1], vs=vsp[1], Rk=R_ksp[1], Rv=R_vsp[1], k0=0, nk=t0,
                                 ncum=(lambda kb: ncum_p[:, kb, :]), Rnc=R_ncp))
            attention("fox", 512, 128, 4, segs, (lambda j, i=i: ncum_p[:, 4 * i + j, :]), R_ncp)
            wo_proj(512, 1)
            ffn(512, 1, 1)
            final_out(512, 128, 4, y_d[t0:t0 + 512, :])

    if not PAIR:
        sample_tiles()

    S.emit(nc, es)
    es.close()
    return nc


def _consts():
    ident = np.eye(128, dtype=np.float32)
    ones = np.ones((128, 128), np.float32)
    p = np.arange(128)[:, None]
    c = np.arange(128)[None, :]
    tri = (p <= c).astype(np.float32)
    trineg = np.where(c >= p, 0.0, NEG).astype(np.float32)
    cst = np.concatenate([ident, ones, tri, trineg], axis=1)
    msk = np.zeros((128, 8, 512), np.float32)
    for b in range(8):
        kc = 2 * b + (np.arange(128) >= 64).astype(np.int64)
        qc = 8 + np.arange(512) // 64
        valid = (kc[:, None] <= qc[None, :]) & (kc[:, None] >= qc[None, :] - 8)
        msk[:, b, :] = np.where(valid, 0.0, NEG)
    return cst, msk.reshape(128, 8 * 512).astype(ml_dtypes.bfloat16)


def _prep_shared(norm_g, w_qkv, w_o, w_ffn_gate, w_ffn_up, w_ffn_down, rel_bias, w_forget, b_forget, final_norm_g):
    g7 = np.concatenate([norm_g.reshape(6, D), final_norm_g.reshape(1, D)], axis=0)
    gT = np.ascontiguousarray(g7.reshape(7, DC, 128).transpose(2, 0, 1).reshape(128, 7 * DC))
    rb = rel_bias[0]
    pp = np.arange(128)[:, None]
    cc = np.arange(640)[None, :]
    idx = np.clip(cc - pp, -256, 256) + 256
    rsk = np.ascontiguousarray(rb[idx].transpose(2, 0, 1))
    wf = np.ascontiguousarray(w_forget[0].reshape(DC, 128, H).transpose(1, 0, 2).reshape(128, DC * H))
    bfb = np.ascontiguousarray(np.broadcast_to(b_forget[0][None, :], (128, H)))
    cst, msk = _consts()
    return {
        "gT": gT.astype(np.float32), "w_qkv": np.ascontiguousarray(w_qkv), "w_o": np.ascontiguousarray(w_o),
        "w_g": np.ascontiguousarray(w_ffn_gate.reshape(4, D, FF)), "w_u": np.ascontiguousarray(w_ffn_up.reshape(4, D, FF)),
        "w_d": np.ascontiguousarray(w_ffn_down.reshape(4, FF, D)), "rsk": rsk.astype(np.float32),
        "wf": wf.astype(np.float32), "bfb": bfb.astype(np.float32), "cst": cst, "msk": msk,
    }


_NC_CACHE = {}


def kernel(x_prompt, x_sample, cache_band_k, cache_band_v, cache_fox_k, cache_fox_v, cache_fox_logf,
           norm_g, w_qkv, w_o, w_ffn_gate, w_ffn_up, w_ffn_down, rel_bias, w_forget, b_forget, final_norm_g):
    f = lambda a: np.asarray(a, dtype=np.float32)
    x_prompt, x_sample = f(x_prompt), f(x_sample)
    B, T, _ = x_prompt.shape
    SB = x_sample.shape[0]
    NCORE = 8
    NS = SB // NCORE
    shared = _prep_shared(f(norm_g), f(w_qkv), f(w_o), f(w_ffn_gate), f(w_ffn_up), f(w_ffn_down), f(rel_bias),
                          f(w_forget), f(b_forget), f(final_norm_g))
    cbk, cbv = f(cache_band_k)[0].reshape(SB, 512, D), f(cache_band_v)[0].reshape(SB, 512, D)
    cfk, cfv = f(cache_fox_k)[0].reshape(SB, 1024, D), f(cache_fox_v)[0].reshape(SB, 1024, D)
    cfl = f(cache_fox_logf)[0]
    TH = T // 2
    key = (TH, NS, True)
    if key not in _NC_CACHE:
        _NC_CACHE[key] = build(TH, NS, PAIR=True)
    nc = _NC_CACHE[key]
    in_maps = []
    for c in range(NCORE):
        b, half = c // 2, c % 2
        sl = slice(c * NS, (c + 1) * NS)
        m = dict(shared)
        xh = x_prompt[b, TH - 512:TH] if half == 1 else np.zeros((512, D), np.float32)
        m.update({"x": np.ascontiguousarray(x_prompt[b, half * TH:(half + 1) * TH]),
                  "xh": np.ascontiguousarray(xh),
                  "hmask": np.full((128, 1), 0.0 if half == 1 else NEG, np.float32),
                  "xs": np.ascontiguousarray(x_sample[sl]),
                  "cbk": np.ascontiguousarray(cbk[sl]), "cbv": np.ascontiguousarray(cbv[sl]),
                  "cfk": np.ascontiguousarray(cfk[sl]), "cfv": np.ascontiguousarray(cfv[sl]),
                  "cfl": np.ascontiguousarray(cfl[sl])})
        in_maps.append(m)
    res = run_bass_kernel_spmd(nc, in_maps, core_ids=list(range(NCORE))).results
    cat = lambda k: np.concatenate([res[c][k] for c in range(NCORE)], axis=0)
    seqcat = lambda k: np.stack([np.concatenate([res[2 * b][k], res[2 * b + 1][k]], axis=0) for b in range(B)], axis=0)
    last = lambda k: np.stack([res[2 * b + 1][k] for b in range(B)], axis=0)
    y_prompt = seqcat("y")
    y_sample = cat("ys")
    return (y_prompt, y_sample,
            last("bkp").reshape(1, B, 512, H, HD), last("bvp").reshape(1, B, 512, H, HD),
            cat("bks").reshape(1, SB, 16, H, HD), cat("bvs").reshape(1, SB, 16, H, HD),
            seqcat("fkp").reshape(1, B, T, H, HD), seqcat("fvp").reshape(1, B, T, H, HD), seqcat("flp").reshape(1, B, T, H),
            cat("fks").reshape(1, SB, 16, H, HD), cat("fvs").reshape(1, SB, 16, H, HD), cat("fls").reshape(1, SB, 16, H))
```

```python
import numpy as np
import ml_dtypes
from contextlib import ExitStack
import concourse.bass as bass
import concourse.mybir as mybir
from concourse.bass_utils import run_bass_kernel_spmd

F32 = mybir.dt.float32
BF16 = mybir.dt.bfloat16
AF = mybir.ActivationFunctionType
ALU = mybir.AluOpType

D = 1024
DC = 8
FF = 4096
FC = 32
H = 16
HD = 64
NEG = -30000.0
VW = 66
SAME_ENG_SYNC = True
LOOKAHEAD = 4


class Res:
    __slots__ = ("name", "w", "r")

    def __init__(self, name):
        self.name = name
        self.w = None
        self.r = {}


class Op:
    __slots__ = ("eng", "fn", "waits", "signaled", "sigval", "dma", "coll")

    def __init__(self, eng, fn):
        self.eng = eng
        self.fn = fn
        self.waits = []
        self.signaled = False
        self.sigval = None
        self.dma = None
        self.coll = False


class Rec:
    def __init__(self):
        self.call = None

    def __getattr__(self, name):
        def f(*a, **k):
            self.call = (name, a, k)
            return self
        return f


def _rec(fn):
    r = Rec()
    fn(r)
    assert r.call is not None
    return r.call


class Chan:
    def __init__(self, name, strict=False):
        self.name = name
        self.count = 0
        self.sem = None
        self.strict = strict


class Sched:
    ENG = ("pe", "act", "dve", "pool", "sp")

    def __init__(self):
        self.ops = {k: [] for k in self.ENG}
        self.chans = []

    def chan(self, name, strict=False):
        c = Chan(name, strict)
        self.chans.append(c)
        return c

    def _dep(self, op, ev):
        if ev is None:
            return
        if ev[0] == "op":
            src = ev[1]
            if src.eng == op.eng and (op.eng == "pe" or not SAME_ENG_SYNC):
                return
            if src is op:
                return
            src.signaled = True
        op.waits.append(ev)

    def issue(self, eng, fn, reads=(), writes=()):
        op = Op(eng, _rec(fn) if fn is not None else None)
        ev = ("op", op)
        for r in reads:
            self._dep(op, r.w)
        for w in writes:
            self._dep(op, w.w)
            for e in w.r.values():
                self._dep(op, e)
        for r in reads:
            r.r[eng] = ev
        for w in writes:
            w.w = ev
            w.r = {}
        self.ops[eng].append(op)
        return op

    def dma(self, queue, chan, fn, reads=(), writes=(), coll=False, wars=()):
        op = Op(queue, _rec(fn))
        op.coll = coll
        if chan.strict and chan.count > 0:
            op.waits.append(("dma", chan, chan.count))
        chan.count += 1 if coll else 16
        op.dma = (chan, chan.count)
        ev = ("dma", chan, chan.count)
        for w in wars:
            self._dep(op, w.w)
            for e in w.r.values():
                self._dep(op, e)
        for r in reads:
            self._dep(op, r.w)
        for w in writes:
            self._dep(op, w.w)
            for e in w.r.values():
                self._dep(op, e)
        for r in reads:
            r.r[("dma", chan.name)] = ev
        for w in writes:
            w.w = ev
            w.r = {}
        self.ops[queue].append(op)
        return op

    def emit(self, nc, es):
        sems = {k: es.enter_context(nc.semaphore("sem_" + k)) for k in self.ENG}
        for c in self.chans:
            c.sem = es.enter_context(nc.semaphore("ch_" + c.name))
        for k in self.ENG:
            n = 0
            for op in self.ops[k]:
                if op.signaled:
                    n += 1
                    op.sigval = n
        block = es.enter_context(nc.Block())

        def run(k, eng):
            waited = {}
            for op in self.ops[k]:
                for ev in op.waits:
                    if ev[0] == "op":
                        sem, val, key = sems[ev[1].eng], ev[1].sigval, ev[1].eng
                    else:
                        sem, val, key = ev[1].sem, ev[2], ev[1].name
                    if waited.get(key, 0) >= val:
                        continue
                    waited[key] = val
                    eng.wait_ge(sem, val)
                if op.fn is None:
                    continue
                name, a, kw = op.fn
                ins = getattr(eng, name)(*a, **kw)
                if op.dma is not None and op.coll:
                    ins.then_inc(op.dma[0].sem)
                elif op.dma is not None:
                    ins.then_inc(op.dma[0].sem, 16)
                elif op.signaled:
                    ins.then_inc(sems[k], 1)

        @block.tensor
        def _(e):
            run("pe", e)

        @block.scalar
        def _(e):
            run("act", e)

        @block.vector
        def _(e):
            run("dve", e)

        @block.gpsimd
        def _(e):
            run("pool", e)

        @block.sync
        def _(e):
            run("sp", e)
            for c in self.chans:
                if c.count > 0:
                    e.wait_ge(c.sem, c.count)


class Rot:
    def __init__(self, items):
        self.items = items
        self.i = 0

    def next(self):
        it = self.items[self.i % len(self.items)]
        self.i += 1
        return it


def build(T, NS, PAIR=False, L1=True):
    nc = bass.Bass("TRN2", target_bir_lowering=False)
    S = Sched()
    es = ExitStack()
    NT = T // 512
    NBLK = T // 128

    def din(name, shape, dt=F32):
        return nc.dram_tensor(name, list(shape), dt, kind="ExternalInput").ap()

    def dout(name, shape, dt=F32):
        return nc.dram_tensor(name, list(shape), dt, kind="ExternalOutput").ap()

    def dscr(name, shape, dt):
        return nc.dram_tensor(name, list(shape), dt, kind="Internal").ap()

    x_d = din("x", [T, D])
    xs_d = din("xs", [max(NS, 1), 16, D])
    cbk_d = din("cbk", [max(NS, 1), 512, D])
    cbv_d = din("cbv", [max(NS, 1), 512, D])
    cfk_d = din("cfk", [max(NS, 1), 1024, D])
    cfv_d = din("cfv", [max(NS, 1), 1024, D])
    cfl_d = din("cfl", [max(NS, 1), 1024, H])
    gT_d = din("gT", [128, 7 * DC])
    wqkv_d = din("w_qkv", [2, D, 3 * D])
    wo_d = din("w_o", [2, D, D])
    wg_d = din("w_g", [4, D, FF])
    wu_d = din("w_u", [4, D, FF])
    wd_d = din("w_d", [4, FF, D])
    rsk_d = din("rsk", [H, 128, 640])
    wf_d = din("wf", [128, DC * H])
    bf_d = din("bfb", [128, H])
    cst_d = din("cst", [128, 4 * 128])
    msk_d = din("msk", [128, 8 * 512], BF16)

    if PAIR:
        xh_d = din("xh", [512, D])
        hm_d = din("hmask", [128, 1])
    y_d = dout("y", [T, D])
    ys_d = dout("ys", [max(NS, 1), 16, D])
    bkp_d = dout("bkp", [512, D])
    bvp_d = dout("bvp", [512, D])
    bks_d = dout("bks", [max(NS, 1), 16, D])
    bvs_d = dout("bvs", [max(NS, 1), 16, D])
    fkp_d = dout("fkp", [T, D])
    fvp_d = dout("fvp", [T, D])
    flp_d = dout("flp", [T, H])
    fks_d = dout("fks", [max(NS, 1), 16, D])
    fvs_d = dout("fvs", [max(NS, 1), 16, D])
    fls_d = dout("fls", [max(NS, 1), 16, H])

    import os
    dbg_d = dout("dbg", [128, 12, 512], BF16) if (os.environ.get("KDBG", "0") == "1") else None
    wqkv_b = dscr("wqkv_b", [2, D, 3 * D], BF16)
    wo_b = dscr("wo_b", [2, D, D], BF16)
    wg_b = dscr("wg_b", [4, D, FF], BF16)
    wu_b = dscr("wu_b", [4, D, FF], BF16)
    wd_b = dscr("wd_b", [4, FF, D], BF16)
    def dcol(name, shape, dt):
        return nc.dram_tensor(name, list(shape), dt).ap()

    HB = 4 if PAIR else 0
    ksp1_2d = dcol("ksp1", [DC * 128, T], BF16)
    vsp1_2d = dcol("vsp1", [DC * 128, NBLK * 2 * VW], BF16)
    ksp = [dscr("ksp0", [DC, 128, T + 128 * HB], BF16), ksp1_2d.rearrange("(c p) t -> c p t", p=128)]
    vsp = [dscr("vsp0", [DC, 128, NBLK + HB, 2 * VW], BF16),
           vsp1_2d.rearrange("(c p) (b e) -> c p b e", p=128, e=2 * VW)]
    if PAIR:
        ksg_2d = dcol("ksg", [2 * DC * 128, T], BF16)
        vsg_2d = dcol("vsg", [2 * DC * 128, NBLK * 2 * VW], BF16)
        ksg = ksg_2d.rearrange("(c r p) t -> r c p t", r=2, p=128)
        vsg = vsg_2d.rearrange("(c r p) (b e) -> r c p b e", r=2, p=128, e=2 * VW)
        ncd = dcol("ncd", [128, (NBLK + 1) * H], F32)
        ncg = dcol("ncg", [2 * 128, (NBLK + 1) * H], F32)
        xsp = dscr("xsp", [NT, 128, DC * 512], F32)
        qsp = dscr("qsp", [NT, 2, 128, DC * 512], BF16)
        R_xsp, R_qsp, R_ksg, R_vsg, R_ncd, R_ncg = (Res(n) for n in ("xsp", "qsp", "ksg", "vsg", "ncd", "ncg"))
    kss = [dscr(f"kss{l}", [max(NS, 1), DC, 128, 1024], BF16) for l in range(2)]
    vss = [dscr(f"vss{l}", [max(NS, 1), DC, 128, 8, 2 * VW], BF16) for l in range(2)]
    R_ksp = [Res(f"ksp{l}") for l in range(2)]
    R_vsp = [Res(f"vsp{l}") for l in range(2)]
    R_kss = [Res(f"kss{l}") for l in range(2)]
    R_vss = [Res(f"vss{l}") for l in range(2)]
    R_w = {}

    def sb(name, shape, dt):
        return es.enter_context(nc.sbuf_tensor("s_" + name, list(shape), dt))

    xT = sb("xT", [128, DC, 512], F32); R_xT = Res("xT")
    xn = sb("xn", [128, DC, 512], BF16); R_xn = Res("xn")
    hb = sb("hb", [128, FC * 512], BF16); R_hb = Res("hb")
    h_v = hb[:, :].rearrange("p (f t) -> p f t", f=FC)
    stgA = hb[:, 0:8192].bitcast(F32).rearrange("p (j d) -> p j d", j=4)
    stgB = hb[:, 8192:16384].bitcast(F32).rearrange("p (j d) -> p j d", j=4)
    attnT2 = hb[:, 0:4096].rearrange("p (c t) -> p c t", c=DC)
    R_at = Res("attnT2")
    junk2 = sb("junk2", [128, 1], F32)
    oddst = [sb(f"oddst{i}", [64, 512], BF16) for i in range(2)]
    R_odd = [Res(f"oddst{i}") for i in range(2)]
    C_odd = [S.chan(f"odd{i}") for i in range(2)]
    oddrot = Rot([0, 1])
    qT2 = [sb(f"qT{g}", [128, DC, 512], BF16) for g in range(2)]; R_qT = Res("qT")
    kT = sb("kT", [128, DC, 512], BF16); R_kT = Res("kT")
    vA = sb("vA", [128, 4, H, VW], BF16); R_vA = Res("vA")
    NRING = 4
    ring = [sb(f"ring{i}", [128, 8, 512], BF16) for i in range(NRING)]
    R_ring = [Res(f"ring{i}") for i in range(NRING)]
    C_ring = [S.chan(f"ring{i}") for i in range(NRING)]
    ringrot = Rot(list(range(NRING)))
    hk = [sb(f"hk{i}", [128, 1024], BF16) for i in range(2)]
    hv = [sb(f"hv{i}", [128, 8, 2 * VW], BF16) for i in range(2)]
    R_hs = [Res(f"hs{i}") for i in range(2)]
    C_hk = [S.chan(f"hk{i}") for i in range(2)]
    C_hv = [S.chan(f"hv{i}") for i in range(2)]
    hsrot = Rot([0, 1])
    rsk = [sb(f"rsk{i}", [128, 640], F32) for i in range(2)]
    R_rsk = [Res(f"rsk{i}") for i in range(2)]
    C_rsk = [S.chan(f"rsk{i}") for i in range(2)]
    rskrot = Rot([0, 1])
    cst = sb("cst", [128, 512], F32); R_cst = Res("cst")
    ident = cst[:, 0:128]
    ones = cst[:, 128:256]
    tri = cst[:, 256:384]
    trineg = cst[:, 384:512]
    msk = sb("msk", [128, 8, 512], BF16)
    gT = sb("gT", [128, 7 * DC], F32)
    wf = sb("wf", [128, DC, H], BF16)
    wf32 = sb("wf32", [128, DC * H], F32)
    bfb = sb("bfb", [128, H], F32)
    ncum_p = sb("ncum_p", [128, NBLK, H], F32); R_ncp = Res("ncum_p")
    ncum_s = sb("ncum_s", [128, max(NS, 1), 8, H], F32); R_ncs = Res("ncum_s")
    ncum_c = sb("ncum_c", [128, max(NS, 1), H], F32); R_ncc = Res("ncum_c")
    carry_p = sb("carry_p", [128, H], F32); R_cp = Res("carry_p")
    carry_s = sb("carry_s", [128, max(NS, 1), H], F32); R_cs = Res("carry_s")
    lft = sb("lft", [128, 4, H], F32); R_lft = Res("lft")
    zt = sb("zt", [128, 4, H], F32); R_zt = Res("zt")
    f32t = [sb(f"f32t{i}", [128, 512], F32) for i in range(4)]
    R_f32t = [Res(f"f32t{i}") for i in range(4)]
    f32rot = Rot(list(range(4)))
    pt = [sb(f"pt{i}", [128, 512], BF16) for i in range(6)]
    R_pt = [Res(f"pt{i}") for i in range(6)]
    ptrot = Rot([0, 1, 2, 3, 4, 5])
    cq = [sb(f"cq{i}", [128, 512], F32) for i in range(2)]
    R_cq = [Res(f"cq{i}") for i in range(2)]
    cqrot = Rot([0, 1])
    dgt = [sb(f"dgt{i}", [128, 128], F32) for i in range(2)]
    R_dgt = [Res(f"dgt{i}") for i in range(2)]
    dgrot = Rot([0, 1])
    oT = sb("oT", [HD + 1, 512], F32); R_oT = Res("oT")
    rd = sb("rd", [128, 512], F32); R_rd = Res("rd")
    C_ld = S.chan("ld", strict=True)
    C_ld2 = S.chan("ld2", strict=True)
    C_stA = S.chan("stA")
    C_stB = S.chan("stB")
    C_stL = S.chan("stL")
    C_spk = S.chan("spk")
    C_spv = S.chan("spv")
    R_const = Res("const")

    ps = [es.enter_context(nc.psum_tensor(f"ps{i}", [128, 512], F32)) for i in range(8)]
    R_ps = [Res(f"ps{i}") for i in range(8)]
    rotA = Rot([0, 1, 2, 3])
    rotB = Rot([4, 5, 6, 7])
    rotS = Rot([0, 1, 2, 3, 4])
    rotO = Rot([5, 6, 7])

    stg32 = [hb[:, 0:8192].bitcast(F32), hb[:, 8192:16384].bitcast(F32)]
    R_stg = [Res("stg0"), Res("stg1")]
    C_stg = [S.chan("stg0"), S.chan("stg1")]
    C_wst = [S.chan(f"wst{i}") for i in range(NRING)]
    ncast = [0]

    def cast_slab(src3, dst3, rw, d0, d1):
        k = ncast[0] % 2
        i = ringrot.next()
        eng = "dve"
        ncast[0] += 1
        sv = stg32[k].rearrange("p (a b) -> p a b", a=d0)
        rv = ring[i][:, :, :].rearrange("p a b -> p (a b)").rearrange("p (a b) -> p a b", a=d0)
        S.dma("sp", C_stg[k], lambda e: e.dma_start(out=sv, in_=src3), writes=[R_stg[k]])
        if eng == "act":
            S.issue("act", lambda e: e.activation(out=rv, in_=sv, func=AF.Copy), reads=[R_stg[k]], writes=[R_ring[i]])
        else:
            S.issue(eng, lambda e: e.tensor_copy(out=rv, in_=sv), reads=[R_stg[k]], writes=[R_ring[i]])
        S.dma("pool", C_wst[i], lambda e: e.dma_start(out=dst3, in_=rv), reads=[R_ring[i]], writes=[rw])

    def cast_w(nm, dst, src, i, kind):
        rw = R_w[(nm, i)] = Res(f"w_{nm}{i}")
        if kind == "kf":
            sv = src[i].rearrange("(kc p) f -> p kc f", p=128)
            dv = dst[i].rearrange("(kc p) f -> p kc f", p=128)
            for fb in range(sv.shape[2] // 512):
                cast_slab(sv[:, :, 512 * fb:512 * fb + 512], dv[:, :, 512 * fb:512 * fb + 512], rw, 8, 512)
        elif kind == "fd":
            sv = src[i].rearrange("(fc p) d -> p fc d", p=128)
            dv = dst[i].rearrange("(fc p) d -> p fc d", p=128)
            for fb in range(8):
                cast_slab(sv[:, 4 * fb:4 * fb + 4, :], dv[:, 4 * fb:4 * fb + 4, :], rw, 4, 1024)
        else:
            sv = src[i].rearrange("(kc p) d -> p kc d", p=128)
            dv = dst[i].rearrange("(kc p) d -> p kc d", p=128)
            for fb in range(2):
                cast_slab(sv[:, 4 * fb:4 * fb + 4, :], dv[:, 4 * fb:4 * fb + 4, :], rw, 4, 1024)

    for l in range(2):
        cast_w("g", wg_b, wg_d, 2 * l, "kf")
        cast_w("u", wu_b, wu_d, 2 * l, "kf")
        cast_w("d", wd_b, wd_d, 2 * l, "fd")
        cast_w("qkv", wqkv_b, wqkv_d, l, "kf")
        cast_w("o", wo_b, wo_d, l, "o")
        cast_w("g", wg_b, wg_d, 2 * l + 1, "kf")
        cast_w("u", wu_b, wu_d, 2 * l + 1, "kf")
        cast_w("d", wd_b, wd_d, 2 * l + 1, "fd")

    junk = sb("junk", [128, 1], F32)
    S.issue("dve", lambda e: e.memset(junk[:], 0.0), reads=[R_stg[0], R_stg[1]], writes=[R_hb])
    for dst, src in ((cst, cst_d), (gT, gT_d), (wf32, wf_d), (bfb, bf_d)):
        S.dma("sp", C_ld, lambda e, d=dst, s=src: e.dma_start(out=d[:], in_=s), writes=[R_const])
    S.dma("sp", C_ld, lambda e: e.dma_start(out=msk[:], in_=msk_d.rearrange("p (b t) -> p b t", b=8)),
          writes=[R_const])
    S.issue("dve", lambda e: e.tensor_copy(out=wf[:].rearrange("p c h -> p (c h)"), in_=wf32[:]),
            reads=[R_const], writes=[R_const])
    S.issue("dve", lambda e: e.memset(carry_p[:], 0.0), writes=[R_cp])

    def ring_load(src_ap, rw):
        i = ringrot.next()
        S.dma("sp", C_ring[i], lambda e: e.dma_start(out=ring[i][:], in_=src_ap),
              reads=[rw], writes=[R_ring[i]])
        return i

    def evac_copy(eng, out, in_, reads, writes):
        if eng == "act":
            S.issue("act", lambda e: e.activation(out=out, in_=in_, func=AF.Copy), reads=reads, writes=writes)
        else:
            S.issue(eng, lambda e: e.tensor_copy(out=out, in_=in_), reads=reads, writes=writes)

    def rmsnorm(N, gidx, out_bf=True, out_ap=None):
        b = rotA.next()
        for c in range(DC):
            t = f32rot.next()
            S.issue("act", lambda e, c=c, t=t: e.activation(out=f32t[t][:, :N], in_=xT[:, c, :N], func=AF.Square),
                    reads=[R_xT], writes=[R_f32t[t]])
            S.issue("pe", lambda e, c=c, t=t: e.matmul(ps[b][:, :N], lhsT=ones, rhs=f32t[t][:, :N],
                                                      start=(c == 0), stop=(c == DC - 1)),
                    reads=[R_f32t[t], R_const], writes=[R_ps[b]])
        S.issue("act", lambda e: e.activation(out=rstd[:, :N], in_=ps[b][:, :N], func=AF.Sqrt,
                                              scale=1.0 / D, bias=epsb[:, 0:1]),
                reads=[R_ps[b], R_const], writes=[R_rstd])
        S.issue("dve", lambda e: e.reciprocal(out=rstd[:, :N], in_=rstd[:, :N]),
                reads=[R_rstd], writes=[R_rstd])
        for c in range(DC):
            if out_ap is None:
                S.issue("dve", lambda e, c=c: e.scalar_tensor_tensor(
                    out=xn[:, c, :N], in0=xT[:, c, :N], scalar=gT[:, gidx * DC + c: gidx * DC + c + 1],
                    in1=rstd[:, :N], op0=ALU.mult, op1=ALU.mult),
                    reads=[R_xT, R_rstd, R_const], writes=[R_xn])
            else:
                out_ap(c)

    epsb = sb("epsb", [128, 1], F32)
    rstd = sb("rstd", [128, 512], F32); R_rstd = Res("rstd")
    S.issue("dve", lambda e: e.memset(epsb[:], 1e-6), writes=[R_const])

    def ffn(N, l, a):
        wi = 2 * l + a
        wgv = wg_b[wi].rearrange("(kc p) f -> p kc f", p=128)
        wuv = wu_b[wi].rearrange("(kc p) f -> p kc f", p=128)
        wdv = wd_b[wi].rearrange("(fc p) d -> p fc d", p=128)
        rmsnorm(N, 3 * l + (0 if a == 0 else 2))
        for fb in range(8):
            sg = ring_load(wgv[:, :, 512 * fb: 512 * fb + 512], R_w[("g", wi)])
            su = ring_load(wuv[:, :, 512 * fb: 512 * fb + 512], R_w[("u", wi)])
            for fcl in range(4):
                fc = 4 * fb + fcl
                bg = rotA.next()
                bu = rotA.next()
                for kc in range(DC):
                    S.issue("pe", lambda e, kc=kc, fcl=fcl, bg=bg, sg=sg: e.matmul(
                        ps[bg][:, :N], lhsT=ring[sg][:, kc, 128 * fcl: 128 * fcl + 128], rhs=xn[:, kc, :N],
                        start=(kc == 0), stop=(kc == DC - 1)),
                        reads=[R_ring[sg], R_xn], writes=[R_ps[bg]])
                for kc in range(DC):
                    S.issue("pe", lambda e, kc=kc, fcl=fcl, bu=bu, su=su: e.matmul(
                        ps[bu][:, :N], lhsT=ring[su][:, kc, 128 * fcl: 128 * fcl + 128], rhs=xn[:, kc, :N],
                        start=(kc == 0), stop=(kc == DC - 1)),
                        reads=[R_ring[su], R_xn], writes=[R_ps[bu]])
                t = f32rot.next()
                S.issue("act", lambda e, bg=bg, t=t: e.activation(out=f32t[t][:, :N], in_=ps[bg][:, :N], func=AF.Silu),
                        reads=[R_ps[bg]], writes=[R_f32t[t]])
                S.issue("dve", lambda e, bu=bu, t=t, fc=fc: e.tensor_tensor(
                    out=h_v[:, fc, :N], in0=f32t[t][:, :N], in1=ps[bu][:, :N], op=ALU.mult),
                    reads=[R_f32t[t], R_ps[bu]], writes=[R_hb])
        for dp in range(2):
            banks = [4, 5, 6, 7]
            for s in range(4):
                sd = ring_load(wdv[:, 8 * s: 8 * s + 8, 512 * dp: 512 * dp + 512], R_w[("d", wi)])
                for m in range(4):
                    for fcl in range(8):
                        S.issue("pe", lambda e, m=m, fcl=fcl, s=s, sd=sd: e.matmul(
                            ps[banks[m]][:, :N], lhsT=ring[sd][:, fcl, 128 * m: 128 * m + 128],
                            rhs=h_v[:, 8 * s + fcl, :N], start=(s == 0 and fcl == 0), stop=(s == 3 and fcl == 7)),
                            reads=[R_ring[sd], R_hb], writes=[R_ps[banks[m]]])
            for m in range(4):
                c = 4 * dp + m
                S.issue("dve", lambda e, m=m, c=c: e.scalar_tensor_tensor(
                    out=xT[:, c, :N], in0=ps[banks[m]][:, :N], scalar=0.5, in1=xT[:, c, :N],
                    op0=ALU.mult, op1=ALU.add),
                    reads=[R_ps[banks[m]], R_xT], writes=[R_xT])

    def load_x(src_rows, rows, nsub):
        N = rows * nsub
        S.dma("sp", C_ld, lambda e: e.dma_start(out=stgA[0:rows, 0:nsub, :],
                                                in_=src_rows.rearrange("(j p) d -> p j d", p=rows)),
              writes=[R_hb])
        for c in range(DC):
            b = rotA.next()
            for j in range(nsub):
                S.issue("pe", lambda e, c=c, j=j, b=b: e.transpose(
                    out=ps[b][:, j * rows:(j + 1) * rows], in_=stgA[0:rows, j, 128 * c:128 * c + 128],
                    identity=ident[0:rows, 0:rows]),
                    reads=[R_hb, R_const], writes=[R_ps[b]])
            evac_copy("act" if c % 2 == 0 else "dve", xT[:, c, :N], ps[b][:, :N], [R_ps[b]], [R_xT])

    def qkv(N, rows, nsub, l, k_out, v_out, lf_out):
        wv = wqkv_b[l].rearrange("(kc p) f -> p kc f", p=128)
        rmsnorm(N, 3 * l + 1)
        KQ = int(os.environ.get("KQ", "9"))
        if KQ <= 2:
            k_out = None
        if KQ <= 4:
            v_out = None
        for s in range(6):
            if (KQ <= 1 and s >= 2) or (KQ <= 3 and s >= 4):
                break
            sl = ring_load(wv[:, :, 512 * s:512 * s + 512], R_w[("qkv", l)])
            for ml in range(4):
                m = 4 * s + ml
                b = rotB.next()
                for kc in range(DC):
                    S.issue("pe", lambda e, kc=kc, ml=ml, b=b, sl=sl: e.matmul(
                        ps[b][:, :N], lhsT=ring[sl][:, kc, 128 * ml:128 * ml + 128], rhs=xn[:, kc, :N],
                        start=(kc == 0), stop=(kc == DC - 1)),
                        reads=[R_ring[sl], R_xn], writes=[R_ps[b]])
                c = m % DC
                if m < 8:
                    for g in range(2):
                        S.issue("dve", lambda e, b=b, c=c, g=g: e.tensor_scalar(
                            out=qT2[g][64 * g:64 * g + 64, c, :N], in0=ps[b][64 * g:64 * g + 64, :N], scalar1=0.125,
                            scalar2=None, op0=ALU.mult), reads=[R_ps[b]], writes=[R_qT])
                elif m < 16:
                    if k_out is None:
                        S.issue("dve", lambda e, b=b, c=c: e.tensor_copy(out=kT[:, c, :N], in_=ps[b][:, :N]),
                                reads=[R_ps[b]], writes=[R_kT])
                    else:
                        t = f32rot.next()
                        evac_copy("act", f32t[t][:, :N], ps[b][:, :N], [R_ps[b]], [R_f32t[t]])
                        S.issue("dve", lambda e, t=t, c=c: e.tensor_copy(out=kT[:, c, :N], in_=f32t[t][:, :N]),
                                reads=[R_f32t[t]], writes=[R_kT])
                        b2 = rotA.next()
                        for j in range(nsub):
                            S.issue("pe", lambda e, j=j, b2=b2, t=t: e.transpose(
                                out=ps[b2][0:rows, 128 * j:128 * j + 128], in_=f32t[t][:, j * rows:(j + 1) * rows],
                                identity=ident), reads=[R_f32t[t], R_const], writes=[R_ps[b2]])
                        pv = ps[b2][0:rows, 0:128 * nsub].rearrange("p (j d) -> p j d", j=nsub)
                        evac_copy("act", stgA[0:rows, 0:nsub, 128 * c:128 * c + 128], pv, [R_ps[b2]], [R_hb])
                else:
                    t = f32rot.next()
                    evac_copy("act", f32t[t][:, :N], ps[b][:, :N], [R_ps[b]], [R_f32t[t]])
                    b2 = rotA.next()
                    for j in range(nsub):
                        S.issue("pe", lambda e, j=j, b2=b2, t=t: e.transpose(
                            out=ps[b2][0:rows, 128 * j:128 * j + 128], in_=f32t[t][:, j * rows:(j + 1) * rows],
                            identity=ident), reads=[R_f32t[t], R_const], writes=[R_ps[b2]])
                    pv = ps[b2][0:rows, 0:128 * nsub].rearrange("p (j d) -> p j d", j=nsub)
                    if v_out is not None:
                        evac_copy("act", stgB[0:rows, 0:nsub, 128 * c:128 * c + 128], pv, [R_ps[b2]], [R_hb])
                    for j in range(nsub):
                        if v_out is not None:
                            src = stgB[0:rows, j, 128 * c:128 * c + 128].rearrange("p (g d) -> p g d", g=2)
                            rr = R_hb
                        else:
                            src = ps[b2][0:rows, 128 * j:128 * j + 128].rearrange("p (g d) -> p g d", g=2)
                            rr = R_ps[b2]
                        S.issue("dve", lambda e, c=c, j=j, src=src: e.tensor_copy(
                            out=vA[0:rows, j, 2 * c:2 * c + 2, 0:HD], in_=src), reads=[rr], writes=[R_vA])
        if k_out is not None:
            S.dma("sp", C_stA, lambda e: e.dma_start(out=k_out.rearrange("(j p) d -> p j d", p=rows),
                                                     in_=stgA[0:rows, 0:nsub, :]), reads=[R_hb])
        if v_out is not None:
            S.dma("sp", C_stB, lambda e: e.dma_start(out=v_out.rearrange("(j p) d -> p j d", p=rows),
                                                     in_=stgB[0:rows, 0:nsub, :]), reads=[R_hb])

    def logf_cum(N, rows, nsub, lf_out, ncum_dst, R_ncd, carry, R_carry, carries=None):
        for j in range(nsub):
            b = rotA.next()
            for kc in range(DC):
                S.issue("pe", lambda e, kc=kc, j=j, b=b: e.matmul(
                    ps[b][0:rows, 0:H], lhsT=xn[:, kc, j * rows:(j + 1) * rows], rhs=wf[:, kc, :],
                    start=(kc == 0), stop=(kc == DC - 1)), reads=[R_xn, R_const], writes=[R_ps[b]])
            S.issue("dve", lambda e, j=j, b=b: e.tensor_tensor(out=zt[0:rows, j, :], in0=ps[b][0:rows, 0:H],
                                                               in1=bfb[0:rows, :], op=ALU.add),
                    reads=[R_ps[b], R_const], writes=[R_zt])
        S.issue("act", lambda e: e.activation(out=zt[0:rows, 0:nsub, :], in_=zt[0:rows, 0:nsub, :], func=AF.Exp, scale=-1.0),
                reads=[R_zt], writes=[R_zt])
        S.issue("act", lambda e: e.activation(out=zt[0:rows, 0:nsub, :], in_=zt[0:rows, 0:nsub, :], func=AF.Ln, bias=1.0),
                reads=[R_zt], writes=[R_zt])
        S.issue("dve", lambda e: e.tensor_scalar(out=lft[0:rows, 0:nsub, :], in0=zt[0:rows, 0:nsub, :], scalar1=-1.0,
                                                 scalar2=None, op0=ALU.mult), reads=[R_zt], writes=[R_lft])
        S.dma("sp", C_stL, lambda e: e.dma_start(out=lf_out.rearrange("(j p) h -> p j h", p=rows),
                                                 in_=lft[0:rows, 0:nsub, :]), reads=[R_lft])
        if carries is None:
            cum_blocks(rows, nsub, lambda j: lft[0:rows, j, :], R_lft, ncum_dst, R_ncd, carry, R_carry)
        else:
            for jj in range(nsub):
                cum_blocks(rows, 1, lambda j, jj=jj: lft[0:rows, jj, :], R_lft, lambda j, jj=jj: ncum_dst(jj), R_ncd,
                           carries[jj], R_carry)

    def cum_blocks(rows, nsub, lf_fn, R_lf, ncum_dst, R_ncd, carry, R_carry):
        for j in range(nsub):
            b = rotA.next()
            S.issue("pe", lambda e, j=j, b=b: e.matmul(ps[b][0:rows, 0:H], lhsT=tri[0:rows, 0:rows], rhs=lf_fn(j),
                                                       start=True, stop=True),
                    reads=[R_lf, R_const], writes=[R_ps[b]])
            b2 = rotA.next()
            S.issue("pe", lambda e, j=j, b2=b2: e.matmul(ps[b2][:, 0:H], lhsT=ones[0:rows, :], rhs=lf_fn(j),
                                                         start=True, stop=True),
                    reads=[R_lf, R_const], writes=[R_ps[b2]])
            S.issue("dve", lambda e, j=j, b=b: e.scalar_tensor_tensor(
                out=ncum_dst(j), in0=ps[b][0:rows, 0:H], scalar=-1.0, in1=carry[0:rows, :],
                op0=ALU.mult, op1=ALU.subtract), reads=[R_ps[b], R_carry], writes=[R_ncd])
            S.issue("dve", lambda e, b2=b2: e.tensor_tensor(out=carry, in0=carry, in1=ps[b2][:, 0:H], op=ALU.add),
                    reads=[R_ps[b2], R_carry], writes=[R_carry])

    def spill_kv(l, t0, blk0, nsub):
        S.dma("pool", C_spk, lambda e: e.dma_start(out=ksp[l][:, :, t0:t0 + 512].rearrange("c p t -> p c t"),
                                                 in_=kT[:, :, :]), reads=[R_kT], writes=[R_ksp[l]])
        for c in range(DC):
            S.dma("pool", C_spv, lambda e, c=c: e.dma_start(
                out=vsp[l][c, :, blk0:blk0 + nsub, :],
                in_=vA[:, 0:nsub, 2 * c:2 * c + 2, :].rearrange("p b g e -> p b (g e)")),
                reads=[R_vA], writes=([R_vsp[l]] if c == DC - 1 else []))

    def attention(kind, N, rows, nsub, segs, ncum_cur, R_ncur, col0=0, jd=None):
        S.issue("dve", lambda e: e.memset(junk2[:], 0.0), writes=[R_hb, R_at])
        for c in range(DC):
            accs = [rotO.next(), rotO.next()]
            started = [False, False]
            cqs = [None, None]
            rss = [None, None]
            for g in range(2):
                h = 2 * c + g
                if kind == "band":
                    r = rskrot.next()
                    S.dma("sp", C_rsk[r], lambda e, r=r, h=h: e.dma_start(out=rsk[r][:], in_=rsk_d[h]),
                          writes=[R_rsk[r]])
                    rss[g] = r
                else:
                    q = cqrot.next()
                    b = rotS.next()
                    for j in range(nsub):
                        dg = dgrot.next()
                        S.issue("dve", lambda e, j=j, dg=dg, h=h: e.tensor_scalar(
                            out=dgt[dg][0:rows, 0:rows], in0=ident[0:rows, 0:rows],
                            scalar1=ncum_cur(j)[:, h:h + 1], scalar2=-1.0, op0=ALU.mult, op1=ALU.mult),
                            reads=[R_const, R_ncur], writes=[R_dgt[dg]])
                        S.issue("pe", lambda e, j=j, dg=dg, b=b: e.matmul(
                            ps[b][:, j * rows:(j + 1) * rows], lhsT=ones[0:rows, :], rhs=dgt[dg][0:rows, 0:rows],
                            start=True, stop=True), reads=[R_dgt[dg], R_const], writes=[R_ps[b]])
                    evac_copy("act", cq[q][:, :N], ps[b][:, :N], [R_ps[b]], [R_cq[q]])
                    cqs[g] = q

            pending = []

            def block(g, kslab, vslab, Rk, Rv, krows, col_lo, col_hi, addend, bias):
                h = 2 * c + g
                base = 64 * g
                ncol = col_hi - col_lo
                bs = rotS.next()
                S.issue("pe", lambda e: e.matmul(ps[bs][0:krows, 0:ncol], lhsT=kslab,
                                                 rhs=qT2[g][:, c, col0 + col_lo:col0 + col_hi], start=True, stop=True),
                        reads=[Rk, R_qT], writes=[R_ps[bs]])
                t = f32rot.next()
                addend(t, bs, krows, ncol)
                p = ptrot.next()
                if bias is None:
                    S.issue("act", lambda e: e.activation(out=pt[p][0:krows, 0:ncol], in_=f32t[t][0:krows, 0:ncol],
                                                          func=AF.Exp), reads=[R_f32t[t]], writes=[R_pt[p]])
                else:
                    bap, Rb = bias
                    S.issue("act", lambda e: e.activation(out=pt[p][0:krows, 0:ncol], in_=f32t[t][0:krows, 0:ncol],
                                                          func=AF.Exp, bias=bap), reads=[R_f32t[t], Rb],
                            writes=[R_pt[p]])
                def stage2():
                    first = not started[g]
                    started[g] = True
                    S.issue("pe", lambda e: e.matmul(ps[accs[g]][0:HD + 1, col_lo:col_hi], lhsT=vslab,
                                                     rhs=pt[p][0:krows, 0:ncol], start=first, stop=True,
                                                     skip_group_check=True),
                            reads=[Rv, R_pt[p]], writes=[R_ps[accs[g]]])
                pending.append(stage2)
                if len(pending) > LOOKAHEAD:
                    pending.pop(0)()

            def flush():
                while pending:
                    pending.pop(0)()

            def add_plain(src_ap, Rsrc):
                def f(t, bs, krows, ncol):
                    S.issue("dve", lambda e: e.tensor_tensor(out=f32t[t][0:krows, 0:ncol], in0=ps[bs][0:krows, 0:ncol],
                                                             in1=src_ap(krows, ncol), op=ALU.add),
                            reads=[R_ps[bs], Rsrc], writes=[R_f32t[t]])
                return f

            def add_masked(src_ap, Rsrc, mask_ap):
                def f(t, bs, krows, ncol):
                    S.issue("dve", lambda e: e.tensor_tensor(out=f32t[t][0:krows, 0:ncol], in0=src_ap(krows, ncol),
                                                              in1=mask_ap(krows, ncol), op=ALU.add),
                            reads=[Rsrc, R_const], writes=[R_f32t[t]])
                    S.issue("dve", lambda e: e.tensor_tensor(out=f32t[t][0:krows, 0:ncol], in0=ps[bs][0:krows, 0:ncol],
                                                             in1=f32t[t][0:krows, 0:ncol], op=ALU.add),
                            reads=[R_ps[bs], R_f32t[t]], writes=[R_f32t[t]])
                return f

            for seg in segs:
              nk = seg["nk"]
              ks_ap, vs_ap, Rk_d, Rv_d, k0 = seg["ks"], seg["vs"], seg["Rk"], seg["Rv"], seg["k0"]
              k_done = 0
              while k_done < nk:
                n = min(1024, nk - k_done)
                sl = hsrot.next()
                kb0 = (k0 + k_done) // 128
                S.dma("sp", C_hk[sl], lambda e, sl=sl, n=n, kd=k_done: e.dma_start(
                    out=hk[sl][:, 0:n], in_=ks_ap[c, :, k0 + kd:k0 + kd + n]), reads=[Rk_d], writes=[R_hs[sl]])
                S.dma("sp", C_hv[sl], lambda e, sl=sl, n=n, kb0=kb0: e.dma_start(
                    out=hv[sl][:, 0:n // 128, :], in_=vs_ap[c, :, kb0:kb0 + n // 128, :]),
                    reads=[Rv_d], writes=[R_hs[sl]])
                for g in range(2):
                    h = 2 * c + g
                    for bl in range(n // 128):
                        kb = k_done // 128 + bl
                        kslab = hk[sl][:, 128 * bl:128 * bl + 128]
                        vslab = hv[sl][:, bl, VW * g:VW * g + HD + 1]
                        if kind == "band":
                            b = kb
                            hb_ = None if seg.get("hbias") is None else (seg["hbias"], R_const)
                            if rows == 128:
                                hi = 128 * (b + 1)
                                r = rss[g]
                                block(g, kslab, vslab, R_hs[sl], R_hs[sl], 128, 0, hi,
                                      add_masked(lambda kr, ncn, r=r, b=b: rsk[r][0:kr, 512 - 128 * b:512 - 128 * b + ncn],
                                                 R_rsk[r], lambda kr, ncn, b=b: msk[0:kr, b, 0:ncn]), hb_)
                            else:
                                r = rss[g]
                                block(g, kslab, vslab, R_hs[sl], R_hs[sl], 128, 0, N,
                                      add_plain(lambda kr, ncn, r=r, b=b: rsk[r][0:kr, 512 - 128 * b:512 - 128 * b + ncn],
                                                R_rsk[r]), hb_)
                        else:
                            q = cqs[g]
                            block(g, kslab, vslab, R_hs[sl], R_hs[sl], 128, 0, N,
                                  add_plain(lambda kr, ncn, q=q: cq[q][0:kr, 0:ncn], R_cq[q]),
                                  (seg["ncum"](kb)[:, h:h + 1], seg["Rnc"]))
                k_done += n
            for g in range(2):
                h = 2 * c + g
                for j in range(nsub):
                    jj = j if jd is None else jd
                    kslab = kT[:, c, jj * rows:(jj + 1) * rows]
                    vslab = vA[0:rows, jj, h, 0:HD + 1]
                    lo = j * rows
                    if kind == "band":
                        r = rss[g]
                        b = 4 + j
                        if rows == 128:
                            block(g, kslab, vslab, R_kT, R_vA, rows, lo, N,
                                  add_masked(lambda kr, ncn, r=r, b=b, lo=lo: rsk[r][0:kr, 512 - 128 * b + lo:512 - 128 * b + lo + ncn],
                                             R_rsk[r], lambda kr, ncn, b=b, lo=lo: msk[0:kr, b, lo:lo + ncn]), None)
                        else:
                            block(g, kslab, vslab, R_kT, R_vA, rows, 0, N,
                                  add_plain(lambda kr, ncn, r=r: rsk[r][0:kr, 0:ncn], R_rsk[r]), None)
                    else:
                        q = cqs[g]

                        def add_diag(t, bs, krows, ncol, q=q, lo=lo):
                            S.issue("dve", lambda e: e.tensor_tensor(
                                out=f32t[t][0:krows, 0:krows], in0=cq[q][0:krows, lo:lo + krows],
                                in1=trineg[0:krows, 0:krows], op=ALU.add),
                                reads=[R_cq[q], R_const], writes=[R_f32t[t]])
                            S.issue("dve", lambda e: e.tensor_tensor(
                                out=f32t[t][0:krows, 0:krows], in0=ps[bs][0:krows, 0:krows],
                                in1=f32t[t][0:krows, 0:krows], op=ALU.add),
                                reads=[R_ps[bs], R_f32t[t]], writes=[R_f32t[t]])
                            if ncol > krows:
                                S.issue("dve", lambda e: e.tensor_tensor(
                                    out=f32t[t][0:krows, krows:ncol], in0=ps[bs][0:krows, krows:ncol],
                                    in1=cq[q][0:krows, lo + krows:lo + ncol], op=ALU.add),
                                    reads=[R_ps[bs], R_cq[q]], writes=[R_f32t[t]])
                        block(g, kslab, vslab, R_kT, R_vA, rows, lo, N, add_diag,
                              (ncum_cur(j)[:, h:h + 1], R_ncur))
            flush()
            for g in range(2):
                h = 2 * c + g
                a = accs[g]
                evac_copy("act", oT[0:HD + 1, :N], ps[a][0:HD + 1, :N], [R_ps[a]], [R_oT])
                S.issue("act", lambda e: e.activation(out=rd[64:65, :N], in_=oT[64:65, :N], func=AF.Ln),
                        reads=[R_oT], writes=[R_rd])
                S.issue("act", lambda e: e.activation(out=rd[64:65, :N], in_=rd[64:65, :N], func=AF.Exp, scale=-1.0),
                        reads=[R_rd], writes=[R_rd])
                b = rotS.next()
                S.issue("pe", lambda e, b=b: e.matmul(ps[b][0:HD, :N], lhsT=ones[64:65, 0:HD], rhs=rd[64:65, :N],
                                                      start=True, stop=True), reads=[R_rd, R_const], writes=[R_ps[b]])
                if g == 0:
                    S.issue("dve", lambda e, b=b: e.tensor_tensor(out=attnT2[0:HD, c, col0:col0 + N], in0=oT[0:HD, :N],
                                                                  in1=ps[b][0:HD, :N], op=ALU.mult),
                            reads=[R_oT, R_ps[b], R_at])
                else:
                    o = oddrot.next()
                    S.issue("dve", lambda e, b=b, o=o: e.tensor_tensor(out=oddst[o][:, :N], in0=oT[0:HD, :N],
                                                                       in1=ps[b][0:HD, :N], op=ALU.mult),
                            reads=[R_oT, R_ps[b]], writes=[R_odd[o]])
                    S.dma("pool", C_odd[o], lambda e, o=o: e.dma_start(out=attnT2[HD:128, c, col0:col0 + N], in_=oddst[o][:, :N]),
                          reads=[R_odd[o], R_at])

    def wo_proj(N, l):
        wv = wo_b[l].rearrange("(kc p) d -> p kc d", p=128)
        for s in range(2):
            i = ring_load(wv[:, :, 512 * s:512 * s + 512], R_w[("o", l)])
            for ml in range(4):
                m = 4 * s + ml
                b = rotA.next()
                for kc in range(DC):
                    S.issue("pe", lambda e, kc=kc, ml=ml, b=b, i=i: e.matmul(
                        ps[b][:, :N], lhsT=ring[i][:, kc, 128 * ml:128 * ml + 128], rhs=attnT2[:, kc, :N],
                        start=(kc == 0), stop=(kc == DC - 1)), reads=[R_ring[i]], writes=[R_ps[b], R_at])
                S.issue("dve", lambda e, m=m, b=b: e.tensor_tensor(out=xT[:, m, :N], in0=xT[:, m, :N], in1=ps[b][:, :N],
                                                                   op=ALU.add), reads=[R_ps[b], R_xT], writes=[R_xT])
        S.issue("dve", lambda e: e.memset(junk2[:], 0.0), writes=[R_hb, R_at])

    def final_out(N, rows, nsub, dst_rows):
        tl = {}

        def out_ap(c):
            t2 = f32rot.next()
            tl[c] = t2
            S.issue("dve", lambda e: e.scalar_tensor_tensor(
                out=f32t[t2][:, :N], in0=xT[:, c, :N], scalar=gT[:, 6 * DC + c:6 * DC + c + 1], in1=rstd[:, :N],
                op0=ALU.mult, op1=ALU.mult), reads=[R_xT, R_rstd, R_const], writes=[R_f32t[t2]])
            b = rotB.next()
            for j in range(nsub):
                S.issue("pe", lambda e, j=j: e.transpose(
                    out=ps[b][0:rows, 128 * j:128 * j + 128], in_=f32t[t2][:, j * rows:(j + 1) * rows], identity=ident),
                    reads=[R_f32t[t2], R_const], writes=[R_ps[b]])
            pv = ps[b][0:rows, 0:128 * nsub].rearrange("p (j d) -> p j d", j=nsub)
            evac_copy("act", stgA[0:rows, 0:nsub, 128 * c:128 * c + 128], pv, [R_ps[b]], [R_hb])
        rmsnorm(N, 6, out_ap=out_ap)
        S.dma("sp", C_stA, lambda e: e.dma_start(out=dst_rows.rearrange("(j p) d -> p j d", p=rows),
                                                 in_=stgA[0:rows, 0:nsub, :]), reads=[R_hb])

    S.issue("dve", lambda e: e.memset(qT2[0][64:128, :, :], 0.0), writes=[R_qT])
    S.issue("dve", lambda e: e.memset(qT2[1][0:64, :, :], 0.0), writes=[R_qT])
    S.issue("dve", lambda e: e.memset(vA[:, :, :, HD:VW], 1.0), writes=[R_vA])

    def cache_prep(u):
        for l, (ck, cv, nkc) in enumerate(((cbk_d, cbv_d, 512), (cfk_d, cfv_d, 1024))):
            for half in range(nkc // 512):
                r0 = 512 * half
                S.dma("sp", C_ld, lambda e, ck=ck, r0=r0: e.dma_start(
                    out=stgA[:, :, :], in_=ck[u, r0:r0 + 512, :].rearrange("(j p) d -> p j d", p=128)),
                    writes=[R_hb])
                for c in range(DC):
                    b = rotA.next()
                    for j in range(4):
                        S.issue("pe", lambda e, c=c, j=j, b=b: e.transpose(
                            out=ps[b][:, 128 * j:128 * j + 128], in_=stgA[:, j, 128 * c:128 * c + 128], identity=ident),
                            reads=[R_hb, R_const], writes=[R_ps[b]])
                    evac_copy("act" if c % 2 == 0 else "dve", kT[:, c, :], ps[b][:, :], [R_ps[b]], [R_kT])
                S.dma("pool", C_spk, lambda e, l=l, r0=r0: e.dma_start(
                    out=kss[l][u, :, :, r0:r0 + 512].rearrange("c p t -> p c t"), in_=kT[:, :, :]),
                    reads=[R_kT], writes=[R_kss[l]])
                S.dma("sp", C_ld, lambda e, cv=cv, r0=r0: e.dma_start(
                    out=stgB[:, :, :], in_=cv[u, r0:r0 + 512, :].rearrange("(j p) d -> p j d", p=128)),
                    writes=[R_hb])
                S.issue("dve", lambda e: e.tensor_copy(out=vA[:, :, :, 0:HD],
                                                       in_=stgB[:, :, :].rearrange("p j (h d) -> p j h d", h=H)),
                        reads=[R_hb], writes=[R_vA])
                for c in range(DC):
                    S.dma("pool", C_spv, lambda e, l=l, half=half, c=c: e.dma_start(
                        out=vss[l][u, c, :, 4 * half:4 * half + 4, :],
                        in_=vA[:, :, 2 * c:2 * c + 2, :].rearrange("p b g e -> p b (g e)")),
                        reads=[R_vA], writes=([R_vss[l]] if c == DC - 1 else []))
        S.dma("sp", C_ld, lambda e: e.dma_start(out=zt[:, :, :], in_=cfl_d[u, 0:512, :].rearrange("(j p) h -> p j h", p=128)),
              writes=[R_zt])
        S.dma("sp", C_ld2, lambda e: e.dma_start(out=lft[:, :, :], in_=cfl_d[u, 512:1024, :].rearrange("(j p) h -> p j h", p=128)),
              writes=[R_lft])
        S.issue("dve", lambda e: e.memset(carry_s[:, u, :], 0.0), writes=[R_cs])
        cum_blocks(128, 4, lambda j: zt[:, j, :], R_zt, lambda j: ncum_s[:, u, j, :], R_ncs, carry_s[:, u, :], R_cs)
        cum_blocks(128, 4, lambda j: lft[:, j, :], R_lft, lambda j: ncum_s[:, u, 4 + j, :], R_ncs, carry_s[:, u, :], R_cs)

    def tile(N, rows, nsub, x_src, seq):
        import os
        KSTOP = int(os.environ.get("KSTOP", "99"))
        load_x(x_src, rows, nsub)
        if KSTOP <= 1:
            return final_out(N, rows, nsub, seq["y_out"])
        ffn(N, 0, 0)
        if KSTOP <= 2:
          if os.environ.get("KDBG", "0") == "1":
            S.dma("sp", C_ld, lambda e: e.dma_start(out=dbg_d[:, 0:8, :], in_=xn[:, :, :]), reads=[R_xn])
            S.dma("sp", C_ld, lambda e: e.dma_start(out=dbg_d[:, 8:12, :], in_=h_v[:, 0:4, :]), reads=[R_hb])
            return final_out(N, rows, nsub, seq["y_out"])
        qkv(N, rows, nsub, 0, seq["bk_out"], seq["bv_out"], None)
        if seq["spill0"] is not None:
            spill_kv(0, *seq["spill0"])
        if KSTOP <= 3:
            return final_out(N, rows, nsub, seq["y_out"])
        attention("band", N, rows, nsub, seq["hist0"], None, None)
        if KSTOP <= 4:
            return final_out(N, rows, nsub, seq["y_out"])
        wo_proj(N, 0)
        ffn(N, 0, 1)
        if KSTOP <= 5:
            return final_out(N, rows, nsub, seq["y_out"])
        if L1:
            ffn(N, 1, 0)
            qkv(N, rows, nsub, 1, seq["fk_out"], seq["fv_out"], None)
            logf_cum(N, rows, nsub, seq["fl_out"], seq["ncum_cur"], seq["R_ncur"], seq["carry"], seq["R_carry"])
            if seq["spill1"] is not None:
                spill_kv(1, *seq["spill1"])
            attention("fox", N, rows, nsub, seq["hist1"], seq["ncum_cur"], seq["R_ncur"])
            wo_proj(N, 1)
            ffn(N, 1, 1)
        final_out(N, rows, nsub, seq["y_out"])

    def sample_tiles():
        if NS == 0:
            return
        for u in range(NS):
            cache_prep(u)
        N = 16 * NS
        flat = lambda ap: ap.rearrange("u p d -> (u p) d")
        load_x(flat(xs_d), 16, NS)
        ffn(N, 0, 0)
        qkv(N, 16, NS, 0, flat(bks_d), flat(bvs_d), None)
        for u in range(NS):
            attention("band", 16, 16, 1,
                      [dict(ks=kss[0][u], vs=vss[0][u], Rk=R_kss[0], Rv=R_vss[0], k0=0, nk=512)],
                      None, None, col0=16 * u, jd=u)
        wo_proj(N, 0)
        ffn(N, 0, 1)
        ffn(N, 1, 0)
        qkv(N, 16, NS, 1, flat(fks_d), flat(fvs_d), None)
        logf_cum(N, 16, NS, flat(fls_d), (lambda j: ncum_c[0:16, j, :]), R_ncc, None, R_cs,
                 carries=[carry_s[:, u, :] for u in range(NS)])
        for u in range(NS):
            attention("fox", 16, 16, 1,
                      [dict(ks=kss[1][u], vs=vss[1][u], Rk=R_kss[1], Rv=R_vss[1], k0=0, nk=1024,
                            ncum=(lambda kb, u=u: ncum_s[:, u, kb, :]), Rnc=R_ncs)],
                      (lambda j, u=u: ncum_c[0:16, u, :]), R_ncc, col0=16 * u, jd=u)
        wo_proj(N, 1)
        ffn(N, 1, 1)
        final_out(N, 16, NS, flat(ys_d))

    if not PAIR:
        for i in range(NT):
            t0 = 512 * i
            last = (i == NT - 1)
            seq = dict(
                bk_out=bkp_d if last else None, bv_out=bvp_d if last else None,
                spill0=(t0, 4 * i, 4) if not last else None,
                hist0=[] if i == 0 else [dict(ks=ksp[0], vs=vsp[0], Rk=R_ksp[0], Rv=R_vsp[0], k0=t0 - 512, nk=512)],
                fk_out=fkp_d[t0:t0 + 512, :], fv_out=fvp_d[t0:t0 + 512, :], fl_out=flp_d[t0:t0 + 512, :],
                ncum_cur=(lambda j, i=i: ncum_p[:, 4 * i + j, :]), R_ncur=R_ncp, carry=carry_p[:, :], R_carry=R_cp,
                spill1=(t0, 4 * i, 4) if not last else None,
                hist1=[] if i == 0 else [dict(ks=ksp[1], vs=vsp[1], Rk=R_ksp[1], Rv=R_vsp[1], k0=0, nk=t0,
                                              ncum=(lambda kb: ncum_p[:, kb, :]), Rnc=R_ncp)],
                y_out=y_d[t0:t0 + 512, :],
            )
            tile(512, 128, 4, x_d[t0:t0 + 512, :], seq)
    else:
        hmask = sb("hmask", [128, 1], F32)
        ncum_prev = sb("ncum_prev", [128, NBLK + 1, H], F32); R_ncprev = Res("ncum_prev")
        nprev = sb("nprev", [128, NBLK, H], F32); R_nprev = Res("nprev")
        totm = sb("totm", [128, H], F32)
        C_xs, C_qs, C_nc = S.chan("xs"), S.chan("qs"), S.chan("ncst")
        C_xl, C_ql, C_kl, C_vl, C_ncl = S.chan("xl"), S.chan("ql"), S.chan("kl"), S.chan("vl"), S.chan("ncl")
        C_cc = S.chan("cc")
        S.dma("sp", C_ld, lambda e: e.dma_start(out=hmask[:], in_=hm_d), writes=[R_const])

        load_x(xh_d, 128, 4)
        ffn(512, 0, 0)
        qkv(512, 128, 4, 0, None, None, None)
        spill_kv(0, 0, 0, 4)

        for i in range(NT):
            t0 = 512 * i
            last = (i == NT - 1)
            load_x(x_d[t0:t0 + 512, :], 128, 4)
            ffn(512, 0, 0)
            qkv(512, 128, 4, 0, bkp_d if last else None, bvp_d if last else None, None)
            if not last:
                spill_kv(0, t0 + 512, 4 * (i + 1), 4)
            attention("band", 512, 128, 4,
                      [dict(ks=ksp[0], vs=vsp[0], Rk=R_ksp[0], Rv=R_vsp[0], k0=t0, nk=512,
                            hbias=(hmask[:, 0:1] if i == 0 else None))], None, None)
            wo_proj(512, 0)
            ffn(512, 0, 1)
            ffn(512, 1, 0)
            qkv(512, 128, 4, 1, fkp_d[t0:t0 + 512, :], fvp_d[t0:t0 + 512, :], None)
            logf_cum(512, 128, 4, flp_d[t0:t0 + 512, :], (lambda j, i=i: ncum_p[:, 4 * i + j, :]), R_ncp,
                     carry_p[:, :], R_cp)
            spill_kv(1, t0, 4 * i, 4)
            S.dma("pool", C_xs, lambda e, i=i: e.dma_start(out=xsp[i], in_=xT[:, :, :].rearrange("p c t -> p (c t)")),
                  reads=[R_xT], writes=[R_xsp])
            for g in range(2):
                S.dma("pool", C_qs, lambda e, i=i, g=g: e.dma_start(
                    out=qsp[i, g], in_=qT2[g][:, :, :].rearrange("p c t -> p (c t)")),
                    reads=[R_qT], writes=([R_qsp] if g == 1 else []))

        S.dma("pool", C_nc, lambda e: e.dma_start(out=ncd[:, 0:NBLK * H], in_=ncum_p[:, :, :].rearrange("p b h -> p (b h)")),
              reads=[R_ncp])
        S.dma("pool", C_nc, lambda e: e.dma_start(out=ncd[:, NBLK * H:(NBLK + 1) * H], in_=carry_p[:, :]),
              reads=[R_cp], writes=[R_ncd])
        PAIRS = [[0, 1], [2, 3], [4, 5], [6, 7]]
        def gather(src_ap, dst_ap, Rs, Rd, last):
            S.dma("pool", C_cc, lambda e: e.collective_compute(
                "AllGather", ALU.bypass, replica_groups=PAIRS, ins=[src_ap.opt()], outs=[dst_ap.opt()]),
                reads=[Rs], writes=([Rd] if last else []), coll=True)
        gather(ncd, ncg, R_ncd, R_ncg, True)
        for c in range(DC):
            gather(ksp1_2d[128 * c:128 * c + 128, :], ksg_2d[256 * c:256 * c + 256, :], R_ksp[1], R_ksg, c == DC - 1)
            gather(vsp1_2d[128 * c:128 * c + 128, :], vsg_2d[256 * c:256 * c + 256, :], R_vsp[1], R_vsg, c == DC - 1)
        sample_tiles()
        S.dma("sp", C_ncl, lambda e: e.dma_start(out=ncum_prev[:, :, :].rearrange("p b h -> p (b h)"), in_=ncg[0:128, :]),
              reads=[R_ncg], writes=[R_ncprev])
        S.issue("dve", lambda e: e.tensor_scalar(out=totm[:, :], in0=ncum_prev[:, NBLK, :], scalar1=hmask[:, 0:1],
                                                 scalar2=None, op0=ALU.add), reads=[R_ncprev, R_const], writes=[R_nprev])
        for kb in range(NBLK):
            S.issue("dve", lambda e, kb=kb: e.tensor_tensor(out=nprev[:, kb, :], in0=ncum_prev[:, kb, :], in1=totm[:, :],
                                                            op=ALU.add), reads=[R_ncprev, R_nprev], writes=[R_nprev])

        for i in range(NT):
            t0 = 512 * i
            S.dma("sp", C_xl, lambda e, i=i: e.dma_start(out=xT[:, :, :].rearrange("p c t -> p (c t)"), in_=xsp[i]),
                  reads=[R_xsp], writes=[R_xT])
            for g in range(2):
                S.dma("sp", C_ql, lambda e, i=i, g=g: e.dma_start(
                    out=qT2[g][:, :, :].rearrange("p c t -> p (c t)"), in_=qsp[i, g]),
                    reads=[R_qsp], writes=([R_qT] if g == 1 else []), wars=[R_qT])
            S.dma("sp", C_kl, lambda e, t0=t0: e.dma_start(
                out=kT[:, :, :], in_=ksp[1][:, :, t0:t0 + 512].rearrange("c p t -> p c t")),
                reads=[R_ksp[1]], writes=[R_kT])
            for c in range(DC):
                S.dma("sp", C_vl, lambda e, i=i, c=c: e.dma_start(
                    out=vA[:, 0:4, 2 * c:2 * c + 2, :].rearrange("p b g e -> p b (g e)"),
                    in_=vsp[1][c, :, 4 * i:4 * i + 4, :]),
                    reads=[R_vsp[1]], writes=([R_vA] if c == DC - 1 else []), wars=[R_vA])
            segs = [dict(ks=ksg[0], vs=vsg[0], Rk=R_ksg, Rv=R_vsg, k0=0, nk=T,
                         ncum=(lambda kb: nprev[:, kb, :]), Rnc=R_nprev)]
            if i > 0:
                segs.append(dict(ks=ksp[1], vs=vsp[1], Rk=R_ksp[1], Rv=R_vsp[1], k0=0, nk=t0,
                                 ncum=(lambda kb: ncum_p[:, kb, :]), Rnc=R_ncp))
            attention("fox", 512, 128, 4, segs, (lambda j, i=i: ncum_p[:, 4 * i + j, :]), R_ncp)
            wo_proj(512, 1)
            ffn(512, 1, 1)
            final_out(512, 128, 4, y_d[t0:t0 + 512, :])

    if not PAIR:
        sample_tiles()

    S.emit(nc, es)
    es.close()
    return nc


def _consts():
    ident = np.eye(128, dtype=np.float32)
    ones = np.ones((128, 128), np.float32)
    p = np.arange(128)[:, None]
    c = np.arange(128)[None, :]
    tri = (p <= c).astype(np.float32)
    trineg = np.where(c >= p, 0.0, NEG).astype(np.float32)
    cst = np.concatenate([ident, ones, tri, trineg], axis=1)
    msk = np.zeros((128, 8, 512), np.float32)
    for b in range(8):
        kc = 2 * b + (np.arange(128) >= 64).astype(np.int64)
        qc = 8 + np.arange(512) // 64
        valid = (kc[:, None] <= qc[None, :]) & (kc[:, None] >= qc[None, :] - 8)
        msk[:, b, :] = np.where(valid, 0.0, NEG)
    return cst, msk.reshape(128, 8 * 512).astype(ml_dtypes.bfloat16)


def _prep_shared(norm_g, w_qkv, w_o, w_ffn_gate, w_ffn_up, w_ffn_down, rel_bias, w_forget, b_forget, final_norm_g):
    g7 = np.concatenate([norm_g.reshape(6, D), final_norm_g.reshape(1, D)], axis=0)
    gT = np.ascontiguousarray(g7.reshape(7, DC, 128).transpose(2, 0, 1).reshape(128, 7 * DC))
    rb = rel_bias[0]
    pp = np.arange(128)[:, None]
    cc = np.arange(640)[None, :]
    idx = np.clip(cc - pp, -256, 256) + 256
    rsk = np.ascontiguousarray(rb[idx].transpose(2, 0, 1))
    wf = np.ascontiguousarray(w_forget[0].reshape(DC, 128, H).transpose(1, 0, 2).reshape(128, DC * H))
    bfb = np.ascontiguousarray(np.broadcast_to(b_forget[0][None, :], (128, H)))
    cst, msk = _consts()
    return {
        "gT": gT.astype(np.float32), "w_qkv": np.ascontiguousarray(w_qkv), "w_o": np.ascontiguousarray(w_o),
        "w_g": np.ascontiguousarray(w_ffn_gate.reshape(4, D, FF)), "w_u": np.ascontiguousarray(w_ffn_up.reshape(4, D, FF)),
        "w_d": np.ascontiguousarray(w_ffn_down.reshape(4, FF, D)), "rsk": rsk.astype(np.float32),
        "wf": wf.astype(np.float32), "bfb": bfb.astype(np.float32), "cst": cst, "msk": msk,
    }


_NC_CACHE = {}


def kernel(x_prompt, x_sample, cache_band_k, cache_band_v, cache_fox_k, cache_fox_v, cache_fox_logf,
           norm_g, w_qkv, w_o, w_ffn_gate, w_ffn_up, w_ffn_down, rel_bias, w_forget, b_forget, final_norm_g):
    f = lambda a: np.asarray(a, dtype=np.float32)
    x_prompt, x_sample = f(x_prompt), f(x_sample)
    B, T, _ = x_prompt.shape
    SB = x_sample.shape[0]
    NCORE = 8
    NS = SB // NCORE
    shared = _prep_shared(f(norm_g), f(w_qkv), f(w_o), f(w_ffn_gate), f(w_ffn_up), f(w_ffn_down), f(rel_bias),
                          f(w_forget), f(b_forget), f(final_norm_g))
    cbk, cbv = f(cache_band_k)[0].reshape(SB, 512, D), f(cache_band_v)[0].reshape(SB, 512, D)
    cfk, cfv = f(cache_fox_k)[0].reshape(SB, 1024, D), f(cache_fox_v)[0].reshape(SB, 1024, D)
    cfl = f(cache_fox_logf)[0]
    TH = T // 2
    key = (TH, NS, True)
    if key not in _NC_CACHE:
        _NC_CACHE[key] = build(TH, NS, PAIR=True)
    nc = _NC_CACHE[key]
    in_maps = []
    for c in range(NCORE):
        b, half = c // 2, c % 2
        sl = slice(c * NS, (c + 1) * NS)
        m = dict(shared)
        xh = x_prompt[b, TH - 512:TH] if half == 1 else np.zeros((512, D), np.float32)
        m.update({"x": np.ascontiguousarray(x_prompt[b, half * TH:(half + 1) * TH]),
                  "xh": np.ascontiguousarray(xh),
                  "hmask": np.full((128, 1), 0.0 if half == 1 else NEG, np.float32),
                  "xs": np.ascontiguousarray(x_sample[sl]),
                  "cbk": np.ascontiguousarray(cbk[sl]), "cbv": np.ascontiguousarray(cbv[sl]),
                  "cfk": np.ascontiguousarray(cfk[sl]), "cfv": np.ascontiguousarray(cfv[sl]),
                  "cfl": np.ascontiguousarray(cfl[sl])})
        in_maps.append(m)
    res = run_bass_kernel_spmd(nc, in_maps, core_ids=list(range(NCORE))).results
    cat = lambda k: np.concatenate([res[c][k] for c in range(NCORE)], axis=0)
    seqcat = lambda k: np.stack([np.concatenate([res[2 * b][k], res[2 * b + 1][k]], axis=0) for b in range(B)], axis=0)
    last = lambda k: np.stack([res[2 * b + 1][k] for b in range(B)], axis=0)
    y_prompt = seqcat("y")
    y_sample = cat("ys")
    return (y_prompt, y_sample,
            last("bkp").reshape(1, B, 512, H, HD), last("bvp").reshape(1, B, 512, H, HD),
            cat("bks").reshape(1, SB, 16, H, HD), cat("bvs").reshape(1, SB, 16, H, HD),
            seqcat("fkp").reshape(1, B, T, H, HD), seqcat("fvp").reshape(1, B, T, H, HD), seqcat("flp").reshape(1, B, T, H),
            cat("fks").reshape(1, SB, 16, H, HD), cat("fvs").reshape(1, SB, 16, H, HD), cat("fls").reshape(1, SB, 16, H))
```

```python
import numpy as np
import ml_dtypes
from contextlib import ExitStack
import concourse.bass as bass
import concourse.mybir as mybir
from concourse.bass_utils import run_bass_kernel_spmd

F32 = mybir.dt.float32
BF16 = mybir.dt.bfloat16
AF = mybir.ActivationFunctionType
ALU = mybir.AluOpType

D = 1024
DC = 8
FF = 4096
FC = 32
H = 16
HD = 64
NEG = -30000.0
VW = 66
SAME_ENG_SYNC = True
LOOKAHEAD = 5


class Res:
    __slots__ = ("name", "w", "r")

    def __init__(self, name):
        self.name = name
        self.w = None
        self.r = {}


class Op:
    __slots__ = ("eng", "fn", "waits", "signaled", "sigval", "dma", "coll")

    def __init__(self, eng, fn):
        self.eng = eng
        self.fn = fn
        self.waits = []
        self.signaled = False
        self.sigval = None
        self.dma = None
        self.coll = False


class Rec:
    def __init__(self):
        self.call = None

    def __getattr__(self, name):
        def f(*a, **k):
            self.call = (name, a, k)
            return self
        return f


def _rec(fn):
    r = Rec()
    fn(r)
    assert r.call is not None
    return r.call


class Chan:
    def __init__(self, name, strict=False):
        self.name = name
        self.count = 0
        self.sem = None
        self.strict = strict


class Sched:
    ENG = ("pe", "act", "dve", "pool", "sp")

    def __init__(self):
        self.ops = {k: [] for k in self.ENG}
        self.chans = []

    def chan(self, name, strict=False):
        c = Chan(name, strict)
        self.chans.append(c)
        return c

    def _dep(self, op, ev):
        if ev is None:
            return
        if ev[0] == "op":
            src = ev[1]
            if src.eng == op.eng and (op.eng == "pe" or not SAME_ENG_SYNC):
                return
            if src is op:
                return
            src.signaled = True
        op.waits.append(ev)

    def issue(self, eng, fn, reads=(), writes=()):
        op = Op(eng, _rec(fn) if fn is not None else None)
        ev = ("op", op)
        for r in reads:
            self._dep(op, r.w)
        for w in writes:
            self._dep(op, w.w)
            for e in w.r.values():
                self._dep(op, e)
        for r in reads:
            r.r[eng] = ev
        for w in writes:
            w.w = ev
            w.r = {}
        self.ops[eng].append(op)
        return op

    def dma(self, queue, chan, fn, reads=(), writes=(), coll=False, wars=()):
        op = Op(queue, _rec(fn))
        op.coll = coll
        if chan.strict and chan.count > 0:
            op.waits.append(("dma", chan, chan.count))
        chan.count += 1 if coll else 16
        op.dma = (chan, chan.count)
        ev = ("dma", chan, chan.count)
        for w in wars:
            self._dep(op, w.w)
            for e in w.r.values():
                self._dep(op, e)
        for r in reads:
            self._dep(op, r.w)
        for w in writes:
            self._dep(op, w.w)
            for e in w.r.values():
                self._dep(op, e)
        for r in reads:
            r.r[("dma", chan.name)] = ev
        for w in writes:
            w.w = ev
            w.r = {}
        self.ops[queue].append(op)
        return op

    def emit(self, nc, es):
        sems = {k: es.enter_context(nc.semaphore("sem_" + k)) for k in self.ENG}
        for c in self.chans:
            c.sem = es.enter_context(nc.semaphore("ch_" + c.name))
        for k in self.ENG:
            n = 0
            for op in self.ops[k]:
                if op.signaled:
                    n += 1
                    op.sigval = n
        block = es.enter_context(nc.Block())

        def run(k, eng):
            waited = {}
            for op in self.ops[k]:
                for ev in op.waits:
                    if ev[0] == "op":
                        sem, val, key = sems[ev[1].eng], ev[1].sigval, ev[1].eng
                    else:
                        sem, val, key = ev[1].sem, ev[2], ev[1].name
                    if waited.get(key, 0) >= val:
                        continue
                    waited[key] = val
                    eng.wait_ge(sem, val)
                if op.fn is None:
                    continue
                name, a, kw = op.fn
                ins = getattr(eng, name)(*a, **kw)
                if op.dma is not None and op.coll:
                    ins.then_inc(op.dma[0].sem)
                elif op.dma is not None:
                    ins.then_inc(op.dma[0].sem, 16)
                elif op.signaled:
                    ins.then_inc(sems[k], 1)

        @block.tensor
        def _(e):
            run("pe", e)

        @block.scalar
        def _(e):
            run("act", e)

        @block.vector
        def _(e):
            run("dve", e)

        @block.gpsimd
        def _(e):
            run("pool", e)

        @block.sync
        def _(e):
            run("sp", e)
            for c in self.chans:
                if c.count > 0:
                    e.wait_ge(c.sem, c.count)


class Rot:
    def __init__(self, items):
        self.items = items
        self.i = 0

    def next(self):
        it = self.items[self.i % len(self.items)]
        self.i += 1
        return it


def build(T, NS, PAIR=False, L1=True):
    nc = bass.Bass("TRN2", target_bir_lowering=False)
    S = Sched()
    es = ExitStack()
    NT = T // 512
    NBLK = T // 128

    def din(name, shape, dt=F32):
        return nc.dram_tensor(name, list(shape), dt, kind="ExternalInput").ap()

    def dout(name, shape, dt=F32):
        return nc.dram_tensor(name, list(shape), dt, kind="ExternalOutput").ap()

    def dscr(name, shape, dt):
        return nc.dram_tensor(name, list(shape), dt, kind="Internal").ap()

    x_d = din("x", [T, D])
    xs_d = din("xs", [max(NS, 1), 16, D])
    cbk_d = din("cbk", [max(NS, 1), 512, D])
    cbv_d = din("cbv", [max(NS, 1), 512, D])
    cfk_d = din("cfk", [max(NS, 1), 1024, D])
    cfv_d = din("cfv", [max(NS, 1), 1024, D])
    cfl_d = din("cfl", [max(NS, 1), 1024, H])
    gT_d = din("gT", [128, 7 * DC])
    wqkv_d = din("w_qkv", [2, D, 3 * D])
    wo_d = din("w_o", [2, D, D])
    wg_d = din("w_g", [4, D, FF])
    wu_d = din("w_u", [4, D, FF])
    wd_d = din("w_d", [4, FF, D])
    rsk_d = din("rsk", [H, 128, 640])
    wf_d = din("wf", [128, DC * H])
    bf_d = din("bfb", [128, H])
    cst_d = din("cst", [128, 4 * 128])
    msk_d = din("msk", [128, 8 * 512], BF16)

    if PAIR:
        xh_d = din("xh", [512, D])
        hm_d = din("hmask", [128, 1])
    y_d = dout("y", [T, D])
    ys_d = dout("ys", [max(NS, 1), 16, D])
    bkp_d = dout("bkp", [512, D])
    bvp_d = dout("bvp", [512, D])
    bks_d = dout("bks", [max(NS, 1), 16, D])
    bvs_d = dout("bvs", [max(NS, 1), 16, D])
    fkp_d = dout("fkp", [T, D])
    fvp_d = dout("fvp", [T, D])
    flp_d = dout("flp", [T, H])
    fks_d = dout("fks", [max(NS, 1), 16, D])
    fvs_d = dout("fvs", [max(NS, 1), 16, D])
    fls_d = dout("fls", [max(NS, 1), 16, H])

    import os
    dbg_d = dout("dbg", [128, 12, 512], BF16) if (os.environ.get("KDBG", "0") == "1") else None
    wqkv_b = dscr("wqkv_b", [2, D, 3 * D], BF16)
    wo_b = dscr("wo_b", [2, D, D], BF16)
    wg_b = dscr("wg_b", [4, D, FF], BF16)
    wu_b = dscr("wu_b", [4, D, FF], BF16)
    wd_b = dscr("wd_b", [4, FF, D], BF16)
    def dcol(name, shape, dt):
        return nc.dram_tensor(name, list(shape), dt).ap()

    HB = 4 if PAIR else 0
    ksp1_2d = dcol("ksp1", [DC * 128, T], BF16)
    vsp1_2d = dcol("vsp1", [DC * 128, NBLK * 2 * VW], BF16)
    ksp = [dscr("ksp0", [DC, 128, T + 128 * HB], BF16), ksp1_2d.rearrange("(c p) t -> c p t", p=128)]
    vsp = [dscr("vsp0", [DC, 128, NBLK + HB, 2 * VW], BF16),
           vsp1_2d.rearrange("(c p) (b e) -> c p b e", p=128, e=2 * VW)]
    if PAIR:
        ksg_2d = dcol("ksg", [2 * DC * 128, T], BF16)
        vsg_2d = dcol("vsg", [2 * DC * 128, NBLK * 2 * VW], BF16)
        ksg = ksg_2d.rearrange("(c r p) t -> r c p t", r=2, p=128)
        vsg = vsg_2d.rearrange("(c r p) (b e) -> r c p b e", r=2, p=128, e=2 * VW)
        ncd = dcol("ncd", [128, (NBLK + 1) * H], F32)
        ncg = dcol("ncg", [2 * 128, (NBLK + 1) * H], F32)
        xsp = dscr("xsp", [NT, 128, DC * 512], F32)
        qsp = dscr("qsp", [NT, 2, 128, DC * 512], BF16)
        R_xsp, R_qsp, R_ksg, R_vsg, R_ncd, R_ncg = (Res(n) for n in ("xsp", "qsp", "ksg", "vsg", "ncd", "ncg"))
    kss = [dscr(f"kss{l}", [max(NS, 1), DC, 128, 1024], BF16) for l in range(2)]
    vss = [dscr(f"vss{l}", [max(NS, 1), DC, 128, 8, 2 * VW], BF16) for l in range(2)]
    R_ksp = [Res(f"ksp{l}") for l in range(2)]
    R_vsp = [Res(f"vsp{l}") for l in range(2)]
    R_kss = [Res(f"kss{l}") for l in range(2)]
    R_vss = [Res(f"vss{l}") for l in range(2)]
    R_w = {}

    def sb(name, shape, dt):
        return es.enter_context(nc.sbuf_tensor("s_" + name, list(shape), dt))

    xT = sb("xT", [128, DC, 512], F32); R_xT = Res("xT")
    xn = sb("xn", [128, DC, 512], BF16); R_xn = Res("xn")
    hb = sb("hb", [128, FC * 512], BF16); R_hb = Res("hb")
    h_v = hb[:, :].rearrange("p (f t) -> p f t", f=FC)
    stgA = hb[:, 0:8192].bitcast(F32).rearrange("p (j d) -> p j d", j=4)
    stgB = hb[:, 8192:16384].bitcast(F32).rearrange("p (j d) -> p j d", j=4)
    attnT2 = hb[:, 0:4096].rearrange("p (c t) -> p c t", c=DC)
    R_at = Res("attnT2")
    junk2 = sb("junk2", [128, 1], F32)
    oddst = [sb(f"oddst{i}", [64, 512], BF16) for i in range(2)]
    R_odd = [Res(f"oddst{i}") for i in range(2)]
    C_odd = [S.chan(f"odd{i}") for i in range(2)]
    oddrot = Rot([0, 1])
    qT2 = [sb(f"qT{g}", [128, DC, 512], BF16) for g in range(2)]; R_qT = Res("qT")
    kT = sb("kT", [128, DC, 512], BF16); R_kT = Res("kT")
    vA = sb("vA", [128, 4, H, VW], BF16); R_vA = Res("vA")
    NRING = 4
    ring = [sb(f"ring{i}", [128, 8, 512], BF16) for i in range(NRING)]
    R_ring = [Res(f"ring{i}") for i in range(NRING)]
    C_ring = [S.chan(f"ring{i}") for i in range(NRING)]
    ringrot = Rot(list(range(NRING)))
    hk = [sb(f"hk{i}", [128, 1024], BF16) for i in range(2)]
    hv = [sb(f"hv{i}", [128, 8, 2 * VW], BF16) for i in range(2)]
    R_hs = [Res(f"hs{i}") for i in range(2)]
    C_hk = [S.chan(f"hk{i}") for i in range(2)]
    C_hv = [S.chan(f"hv{i}") for i in range(2)]
    hsrot = Rot([0, 1])
    rsk = [sb(f"rsk{i}", [128, 640], F32) for i in range(2)]
    R_rsk = [Res(f"rsk{i}") for i in range(2)]
    C_rsk = [S.chan(f"rsk{i}") for i in range(2)]
    rskrot = Rot([0, 1])
    cst = sb("cst", [128, 512], F32); R_cst = Res("cst")
    ident = cst[:, 0:128]
    ones = cst[:, 128:256]
    tri = cst[:, 256:384]
    trineg = cst[:, 384:512]
    msk = sb("msk", [128, 8, 512], BF16)
    gT = sb("gT", [128, 7 * DC], F32)
    wf = sb("wf", [128, DC, H], BF16)
    wf32 = sb("wf32", [128, DC * H], F32)
    bfb = sb("bfb", [128, H], F32)
    ncum_p = sb("ncum_p", [128, NBLK, H], F32); R_ncp = Res("ncum_p")
    ncum_s = sb("ncum_s", [128, max(NS, 1), 8, H], F32); R_ncs = Res("ncum_s")
    ncum_c = sb("ncum_c", [128, max(NS, 1), H], F32); R_ncc = Res("ncum_c")
    carry_p = sb("carry_p", [128, H], F32); R_cp = Res("carry_p")
    carry_s = sb("carry_s", [128, max(NS, 1), H], F32); R_cs = Res("carry_s")
    lft = sb("lft", [128, 4, H], F32); R_lft = Res("lft")
    zt = sb("zt", [128, 4, H], F32); R_zt = Res("zt")
    f32t = [sb(f"f32t{i}", [128, 512], F32) for i in range(4)]
    R_f32t = [Res(f"f32t{i}") for i in range(4)]
    f32rot = Rot(list(range(4)))
    pt = [sb(f"pt{i}", [128, 512], BF16) for i in range(6)]
    R_pt = [Res(f"pt{i}") for i in range(6)]
    ptrot = Rot([0, 1, 2, 3, 4, 5])
    cq = [sb(f"cq{i}", [128, 512], F32) for i in range(2)]
    R_cq = [Res(f"cq{i}") for i in range(2)]
    cqrot = Rot([0, 1])
    dgt = [sb(f"dgt{i}", [128, 128], F32) for i in range(2)]
    R_dgt = [Res(f"dgt{i}") for i in range(2)]
    dgrot = Rot([0, 1])
    oT = sb("oT", [HD + 1, 512], F32); R_oT = Res("oT")
    rd = sb("rd", [128, 512], F32); R_rd = Res("rd")
    C_ld = S.chan("ld", strict=True)
    C_ld2 = S.chan("ld2", strict=True)
    C_stA = S.chan("stA")
    C_stB = S.chan("stB")
    C_stL = S.chan("stL")
    C_spk = S.chan("spk")
    C_spv = S.chan("spv")
    R_const = Res("const")

    ps = [es.enter_context(nc.psum_tensor(f"ps{i}", [128, 512], F32)) for i in range(8)]
    R_ps = [Res(f"ps{i}") for i in range(8)]
    rotA = Rot([0, 1, 2, 3])
    rotB = Rot([4, 5, 6, 7])
    rotS = Rot([0, 1, 2, 3, 4])
    rotO = Rot([5, 6, 7])

    stg32 = [hb[:, 0:8192].bitcast(F32), hb[:, 8192:16384].bitcast(F32)]
    R_stg = [Res("stg0"), Res("stg1")]
    C_stg = [S.chan("stg0"), S.chan("stg1")]
    C_wst = [S.chan(f"wst{i}") for i in range(NRING)]
    ncast = [0]

    def cast_slab(src3, dst3, rw, d0, d1):
        k = ncast[0] % 2
        i = ringrot.next()
        eng = "dve"
        ncast[0] += 1
        sv = stg32[k].rearrange("p (a b) -> p a b", a=d0)
        rv = ring[i][:, :, :].rearrange("p a b -> p (a b)").rearrange("p (a b) -> p a b", a=d0)
        S.dma("sp", C_stg[k], lambda e: e.dma_start(out=sv, in_=src3), writes=[R_stg[k]])
        if eng == "act":
            S.issue("act", lambda e: e.activation(out=rv, in_=sv, func=AF.Copy), reads=[R_stg[k]], writes=[R_ring[i]])
        else:
            S.issue(eng, lambda e: e.tensor_copy(out=rv, in_=sv), reads=[R_stg[k]], writes=[R_ring[i]])
        S.dma("pool", C_wst[i], lambda e: e.dma_start(out=dst3, in_=rv), reads=[R_ring[i]], writes=[rw])

    def cast_w(nm, dst, src, i, kind):
        rw = R_w[(nm, i)] = Res(f"w_{nm}{i}")
        if kind == "kf":
            sv = src[i].rearrange("(kc p) f -> p kc f", p=128)
            dv = dst[i].rearrange("(kc p) f -> p kc f", p=128)
            for fb in range(sv.shape[2] // 512):
                cast_slab(sv[:, :, 512 * fb:512 * fb + 512], dv[:, :, 512 * fb:512 * fb + 512], rw, 8, 512)
        elif kind == "fd":
            sv = src[i].rearrange("(fc p) d -> p fc d", p=128)
            dv = dst[i].rearrange("(fc p) d -> p fc d", p=128)
            for fb in range(8):
                cast_slab(sv[:, 4 * fb:4 * fb + 4, :], dv[:, 4 * fb:4 * fb + 4, :], rw, 4, 1024)
        else:
            sv = src[i].rearrange("(kc p) d -> p kc d", p=128)
            dv = dst[i].rearrange("(kc p) d -> p kc d", p=128)
            for fb in range(2):
                cast_slab(sv[:, 4 * fb:4 * fb + 4, :], dv[:, 4 * fb:4 * fb + 4, :], rw, 4, 1024)

    for l in range(2):
        cast_w("g", wg_b, wg_d, 2 * l, "kf")
        cast_w("u", wu_b, wu_d, 2 * l, "kf")
        cast_w("d", wd_b, wd_d, 2 * l, "fd")
        cast_w("qkv", wqkv_b, wqkv_d, l, "kf")
        cast_w("o", wo_b, wo_d, l, "o")
        cast_w("g", wg_b, wg_d, 2 * l + 1, "kf")
        cast_w("u", wu_b, wu_d, 2 * l + 1, "kf")
        cast_w("d", wd_b, wd_d, 2 * l + 1, "fd")

    junk = sb("junk", [128, 1], F32)
    S.issue("dve", lambda e: e.memset(junk[:], 0.0), reads=[R_stg[0], R_stg[1]], writes=[R_hb])
    for dst, src in ((cst, cst_d), (gT, gT_d), (wf32, wf_d), (bfb, bf_d)):
        S.dma("sp", C_ld, lambda e, d=dst, s=src: e.dma_start(out=d[:], in_=s), writes=[R_const])
    S.dma("sp", C_ld, lambda e: e.dma_start(out=msk[:], in_=msk_d.rearrange("p (b t) -> p b t", b=8)),
          writes=[R_const])
    S.issue("dve", lambda e: e.tensor_copy(out=wf[:].rearrange("p c h -> p (c h)"), in_=wf32[:]),
            reads=[R_const], writes=[R_const])
    S.issue("dve", lambda e: e.memset(carry_p[:], 0.0), writes=[R_cp])

    def ring_load(src_ap, rw):
        i = ringrot.next()
        S.dma("sp", C_ring[i], lambda e: e.dma_start(out=ring[i][:], in_=src_ap),
              reads=[rw], writes=[R_ring[i]])
        return i

    def evac_copy(eng, out, in_, reads, writes):
        if eng == "act":
            S.issue("act", lambda e: e.activation(out=out, in_=in_, func=AF.Copy), reads=reads, writes=writes)
        else:
            S.issue(eng, lambda e: e.tensor_copy(out=out, in_=in_), reads=reads, writes=writes)

    def rmsnorm(N, gidx, out_bf=True, out_ap=None):
        b = rotA.next()
        for c in range(DC):
            t = f32rot.next()
            S.issue("act", lambda e, c=c, t=t: e.activation(out=f32t[t][:, :N], in_=xT[:, c, :N], func=AF.Square),
                    reads=[R_xT], writes=[R_f32t[t]])
            S.issue("pe", lambda e, c=c, t=t: e.matmul(ps[b][:, :N], lhsT=ones, rhs=f32t[t][:, :N],
                                                      start=(c == 0), stop=(c == DC - 1)),
                    reads=[R_f32t[t], R_const], writes=[R_ps[b]])
        S.issue("act", lambda e: e.activation(out=rstd[:, :N], in_=ps[b][:, :N], func=AF.Sqrt,
                                              scale=1.0 / D, bias=epsb[:, 0:1]),
                reads=[R_ps[b], R_const], writes=[R_rstd])
        S.issue("dve", lambda e: e.reciprocal(out=rstd[:, :N], in_=rstd[:, :N]),
                reads=[R_rstd], writes=[R_rstd])
        for c in range(DC):
            if out_ap is None:
                S.issue("dve", lambda e, c=c: e.scalar_tensor_tensor(
                    out=xn[:, c, :N], in0=xT[:, c, :N], scalar=gT[:, gidx * DC + c: gidx * DC + c + 1],
                    in1=rstd[:, :N], op0=ALU.mult, op1=ALU.mult),
                    reads=[R_xT, R_rstd, R_const], writes=[R_xn])
            else:
                out_ap(c)

    epsb = sb("epsb", [128, 1], F32)
    rstd = sb("rstd", [128, 512], F32); R_rstd = Res("rstd")
    S.issue("dve", lambda e: e.memset(epsb[:], 1e-6), writes=[R_const])

    def ffn(N, l, a):
        wi = 2 * l + a
        wgv = wg_b[wi].rearrange("(kc p) f -> p kc f", p=128)
        wuv = wu_b[wi].rearrange("(kc p) f -> p kc f", p=128)
        wdv = wd_b[wi].rearrange("(fc p) d -> p fc d", p=128)
        rmsnorm(N, 3 * l + (0 if a == 0 else 2))
        for fb in range(8):
            sg = ring_load(wgv[:, :, 512 * fb: 512 * fb + 512], R_w[("g", wi)])
            su = ring_load(wuv[:, :, 512 * fb: 512 * fb + 512], R_w[("u", wi)])
            for fcl in range(4):
                fc = 4 * fb + fcl
                bg = rotA.next()
                bu = rotA.next()
                for kc in range(DC):
                    S.issue("pe", lambda e, kc=kc, fcl=fcl, bg=bg, sg=sg: e.matmul(
                        ps[bg][:, :N], lhsT=ring[sg][:, kc, 128 * fcl: 128 * fcl + 128], rhs=xn[:, kc, :N],
                        start=(kc == 0), stop=(kc == DC - 1)),
                        reads=[R_ring[sg], R_xn], writes=[R_ps[bg]])
                for kc in range(DC):
                    S.issue("pe", lambda e, kc=kc, fcl=fcl, bu=bu, su=su: e.matmul(
                        ps[bu][:, :N], lhsT=ring[su][:, kc, 128 * fcl: 128 * fcl + 128], rhs=xn[:, kc, :N],
                        start=(kc == 0), stop=(kc == DC - 1)),
                        reads=[R_ring[su], R_xn], writes=[R_ps[bu]])
                t = f32rot.next()
                S.issue("act", lambda e, bg=bg, t=t: e.activation(out=f32t[t][:, :N], in_=ps[bg][:, :N], func=AF.Silu),
                        reads=[R_ps[bg]], writes=[R_f32t[t]])
                S.issue("dve", lambda e, bu=bu, t=t, fc=fc: e.tensor_tensor(
                    out=h_v[:, fc, :N], in0=f32t[t][:, :N], in1=ps[bu][:, :N], op=ALU.mult),
                    reads=[R_f32t[t], R_ps[bu]], writes=[R_hb])
        for dp in range(2):
            banks = [4, 5, 6, 7]
            for s in range(4):
                sd = ring_load(wdv[:, 8 * s: 8 * s + 8, 512 * dp: 512 * dp + 512], R_w[("d", wi)])
                for m in range(4):
                    for fcl in range(8):
                        S.issue("pe", lambda e, m=m, fcl=fcl, s=s, sd=sd: e.matmul(
                            ps[banks[m]][:, :N], lhsT=ring[sd][:, fcl, 128 * m: 128 * m + 128],
                            rhs=h_v[:, 8 * s + fcl, :N], start=(s == 0 and fcl == 0), stop=(s == 3 and fcl == 7)),
                            reads=[R_ring[sd], R_hb], writes=[R_ps[banks[m]]])
            for m in range(4):
                c = 4 * dp + m
                S.issue("dve", lambda e, m=m, c=c: e.scalar_tensor_tensor(
                    out=xT[:, c, :N], in0=ps[banks[m]][:, :N], scalar=0.5, in1=xT[:, c, :N],
                    op0=ALU.mult, op1=ALU.add),
                    reads=[R_ps[banks[m]], R_xT], writes=[R_xT])

    def load_x(src_rows, rows, nsub):
        N = rows * nsub
        S.dma("sp", C_ld, lambda e: e.dma_start(out=stgA[0:rows, 0:nsub, :],
                                                in_=src_rows.rearrange("(j p) d -> p j d", p=rows)),
              writes=[R_hb])
        for c in range(DC):
            b = rotA.next()
            for j in range(nsub):
                S.issue("pe", lambda e, c=c, j=j, b=b: e.transpose(
                    out=ps[b][:, j * rows:(j + 1) * rows], in_=stgA[0:rows, j, 128 * c:128 * c + 128],
                    identity=ident[0:rows, 0:rows]),
                    reads=[R_hb, R_const], writes=[R_ps[b]])
            evac_copy("act" if c % 2 == 0 else "dve", xT[:, c, :N], ps[b][:, :N], [R_ps[b]], [R_xT])

    def qkv(N, rows, nsub, l, k_out, v_out, lf_out):
        wv = wqkv_b[l].rearrange("(kc p) f -> p kc f", p=128)
        rmsnorm(N, 3 * l + 1)
        KQ = int(os.environ.get("KQ", "9"))
        if KQ <= 2:
            k_out = None
        if KQ <= 4:
            v_out = None
        for s in range(6):
            if (KQ <= 1 and s >= 2) or (KQ <= 3 and s >= 4):
                break
            sl = ring_load(wv[:, :, 512 * s:512 * s + 512], R_w[("qkv", l)])
            for ml in range(4):
                m = 4 * s + ml
                b = rotB.next()
                for kc in range(DC):
                    S.issue("pe", lambda e, kc=kc, ml=ml, b=b, sl=sl: e.matmul(
                        ps[b][:, :N], lhsT=ring[sl][:, kc, 128 * ml:128 * ml + 128], rhs=xn[:, kc, :N],
                        start=(kc == 0), stop=(kc == DC - 1)),
                        reads=[R_ring[sl], R_xn], writes=[R_ps[b]])
                c = m % DC
                if m < 8:
                    for g in range(2):
                        S.issue("dve", lambda e, b=b, c=c, g=g: e.tensor_scalar(
                            out=qT2[g][64 * g:64 * g + 64, c, :N], in0=ps[b][64 * g:64 * g + 64, :N], scalar1=0.125,
                            scalar2=None, op0=ALU.mult), reads=[R_ps[b]], writes=[R_qT])
                elif m < 16:
                    if k_out is None:
                        S.issue("dve", lambda e, b=b, c=c: e.tensor_copy(out=kT[:, c, :N], in_=ps[b][:, :N]),
                                reads=[R_ps[b]], writes=[R_kT])
                    else:
                        t = f32rot.next()
                        evac_copy("act", f32t[t][:, :N], ps[b][:, :N], [R_ps[b]], [R_f32t[t]])
                        S.issue("dve", lambda e, t=t, c=c: e.tensor_copy(out=kT[:, c, :N], in_=f32t[t][:, :N]),
                                reads=[R_f32t[t]], writes=[R_kT])
                        b2 = rotA.next()
                        for j in range(nsub):
                            S.issue("pe", lambda e, j=j, b2=b2, t=t: e.transpose(
                                out=ps[b2][0:rows, 128 * j:128 * j + 128], in_=f32t[t][:, j * rows:(j + 1) * rows],
                                identity=ident), reads=[R_f32t[t], R_const], writes=[R_ps[b2]])
                        pv = ps[b2][0:rows, 0:128 * nsub].rearrange("p (j d) -> p j d", j=nsub)
                        evac_copy("act", stgA[0:rows, 0:nsub, 128 * c:128 * c + 128], pv, [R_ps[b2]], [R_hb])
                else:
                    t = f32rot.next()
                    evac_copy("act", f32t[t][:, :N], ps[b][:, :N], [R_ps[b]], [R_f32t[t]])
                    b2 = rotA.next()
                    for j in range(nsub):
                        S.issue("pe", lambda e, j=j, b2=b2, t=t: e.transpose(
                            out=ps[b2][0:rows, 128 * j:128 * j + 128], in_=f32t[t][:, j * rows:(j + 1) * rows],
                            identity=ident), reads=[R_f32t[t], R_const], writes=[R_ps[b2]])
                    pv = ps[b2][0:rows, 0:128 * nsub].rearrange("p (j d) -> p j d", j=nsub)
                    if v_out is not None:
                        evac_copy("act", stgB[0:rows, 0:nsub, 128 * c:128 * c + 128], pv, [R_ps[b2]], [R_hb])
                    for j in range(nsub):
                        if v_out is not None:
                            src = stgB[0:rows, j, 128 * c:128 * c + 128].rearrange("p (g d) -> p g d", g=2)
                            rr = R_hb
                        else:
                            src = ps[b2][0:rows, 128 * j:128 * j + 128].rearrange("p (g d) -> p g d", g=2)
                            rr = R_ps[b2]
                        S.issue("dve", lambda e, c=c, j=j, src=src: e.tensor_copy(
                            out=vA[0:rows, j, 2 * c:2 * c + 2, 0:HD], in_=src), reads=[rr], writes=[R_vA])
        if k_out is not None:
            S.dma("sp", C_stA, lambda e: e.dma_start(out=k_out.rearrange("(j p) d -> p j d", p=rows),
                                                     in_=stgA[0:rows, 0:nsub, :]), reads=[R_hb])
        if v_out is not None:
            S.dma("sp", C_stB, lambda e: e.dma_start(out=v_out.rearrange("(j p) d -> p j d", p=rows),
                                                     in_=stgB[0:rows, 0:nsub, :]), reads=[R_hb])

    def logf_cum(N, rows, nsub, lf_out, ncum_dst, R_ncd, carry, R_carry, carries=None):
        for j in range(nsub):
            b = rotA.next()
            for kc in range(DC):
                S.issue("pe", lambda e, kc=kc, j=j, b=b: e.matmul(
                    ps[b][0:rows, 0:H], lhsT=xn[:, kc, j * rows:(j + 1) * rows], rhs=wf[:, kc, :],
                    start=(kc == 0), stop=(kc == DC - 1)), reads=[R_xn, R_const], writes=[R_ps[b]])
            S.issue("dve", lambda e, j=j, b=b: e.tensor_tensor(out=zt[0:rows, j, :], in0=ps[b][0:rows, 0:H],
                                                               in1=bfb[0:rows, :], op=ALU.add),
                    reads=[R_ps[b], R_const], writes=[R_zt])
        S.issue("act", lambda e: e.activation(out=zt[0:rows, 0:nsub, :], in_=zt[0:rows, 0:nsub, :], func=AF.Exp, scale=-1.0),
                reads=[R_zt], writes=[R_zt])
        S.issue("act", lambda e: e.activation(out=zt[0:rows, 0:nsub, :], in_=zt[0:rows, 0:nsub, :], func=AF.Ln, bias=1.0),
                reads=[R_zt], writes=[R_zt])
        S.issue("dve", lambda e: e.tensor_scalar(out=lft[0:rows, 0:nsub, :], in0=zt[0:rows, 0:nsub, :], scalar1=-1.0,
                                                 scalar2=None, op0=ALU.mult), reads=[R_zt], writes=[R_lft])
        S.dma("sp", C_stL, lambda e: e.dma_start(out=lf_out.rearrange("(j p) h -> p j h", p=rows),
                                                 in_=lft[0:rows, 0:nsub, :]), reads=[R_lft])
        if carries is None:
            cum_blocks(rows, nsub, lambda j: lft[0:rows, j, :], R_lft, ncum_dst, R_ncd, carry, R_carry)
        else:
            for jj in range(nsub):
                cum_blocks(rows, 1, lambda j, jj=jj: lft[0:rows, jj, :], R_lft, lambda j, jj=jj: ncum_dst(jj), R_ncd,
                           carries[jj], R_carry)

    def cum_blocks(rows, nsub, lf_fn, R_lf, ncum_dst, R_ncd, carry, R_carry):
        for j in range(nsub):
            b = rotA.next()
            S.issue("pe", lambda e, j=j, b=b: e.matmul(ps[b][0:rows, 0:H], lhsT=tri[0:rows, 0:rows], rhs=lf_fn(j),
                                                       start=True, stop=True),
                    reads=[R_lf, R_const], writes=[R_ps[b]])
            b2 = rotA.next()
            S.issue("pe", lambda e, j=j, b2=b2: e.matmul(ps[b2][:, 0:H], lhsT=ones[0:rows, :], rhs=lf_fn(j),
                                                         start=True, stop=True),
                    reads=[R_lf, R_const], writes=[R_ps[b2]])
            S.issue("dve", lambda e, j=j, b=b: e.scalar_tensor_tensor(
                out=ncum_dst(j), in0=ps[b][0:rows, 0:H], scalar=-1.0, in1=carry[0:rows, :],
                op0=ALU.mult, op1=ALU.subtract), reads=[R_ps[b], R_carry], writes=[R_ncd])
            S.issue("dve", lambda e, b2=b2: e.tensor_tensor(out=carry, in0=carry, in1=ps[b2][:, 0:H], op=ALU.add),
                    reads=[R_ps[b2], R_carry], writes=[R_carry])

    def spill_kv(l, t0, blk0, nsub):
        S.dma("pool", C_spk, lambda e: e.dma_start(out=ksp[l][:, :, t0:t0 + 512].rearrange("c p t -> p c t"),
                                                 in_=kT[:, :, :]), reads=[R_kT], writes=[R_ksp[l]])
        for c in range(DC):
            S.dma("pool", C_spv, lambda e, c=c: e.dma_start(
                out=vsp[l][c, :, blk0:blk0 + nsub, :],
                in_=vA[:, 0:nsub, 2 * c:2 * c + 2, :].rearrange("p b g e -> p b (g e)")),
                reads=[R_vA], writes=([R_vsp[l]] if c == DC - 1 else []))

    wo_slots = {}

    def attention(kind, N, rows, nsub, segs, ncum_cur, R_ncur, col0=0, jd=None):
        S.issue("dve", lambda e: e.memset(junk2[:], 0.0), writes=[R_hb, R_at])
        for c in range(DC):
            accs = [rotO.next(), rotO.next()]
            started = [False, False]
            cqs = [None, None]
            rss = [None, None]
            for g in range(2):
                h = 2 * c + g
                if kind == "band":
                    r = rskrot.next()
                    S.dma("sp", C_rsk[r], lambda e, r=r, h=h: e.dma_start(out=rsk[r][:], in_=rsk_d[h]),
                          writes=[R_rsk[r]])
                    rss[g] = r
                else:
                    q = cqrot.next()
                    b = rotS.next()
                    for j in range(nsub):
                        dg = dgrot.next()
                        S.issue("dve", lambda e, j=j, dg=dg, h=h: e.tensor_scalar(
                            out=dgt[dg][0:rows, 0:rows], in0=ident[0:rows, 0:rows],
                            scalar1=ncum_cur(j)[:, h:h + 1], scalar2=-1.0, op0=ALU.mult, op1=ALU.mult),
                            reads=[R_const, R_ncur], writes=[R_dgt[dg]])
                        S.issue("pe", lambda e, j=j, dg=dg, b=b: e.matmul(
                            ps[b][:, j * rows:(j + 1) * rows], lhsT=ones[0:rows, :], rhs=dgt[dg][0:rows, 0:rows],
                            start=True, stop=True), reads=[R_dgt[dg], R_const], writes=[R_ps[b]])
                    evac_copy("act", cq[q][:, :N], ps[b][:, :N], [R_ps[b]], [R_cq[q]])
                    cqs[g] = q

            pending = []

            def block(g, kslab, vslab, Rk, Rv, krows, col_lo, col_hi, addend, bias):
                h = 2 * c + g
                base = 64 * g
                ncol = col_hi - col_lo
                bs = rotS.next()
                S.issue("pe", lambda e: e.matmul(ps[bs][0:krows, 0:ncol], lhsT=kslab,
                                                 rhs=qT2[g][:, c, col0 + col_lo:col0 + col_hi], start=True, stop=True),
                        reads=[Rk, R_qT], writes=[R_ps[bs]])
                t = f32rot.next()
                addend(t, bs, krows, ncol)
                p = ptrot.next()
                if bias is None:
                    S.issue("act", lambda e: e.activation(out=pt[p][0:krows, 0:ncol], in_=f32t[t][0:krows, 0:ncol],
                                                          func=AF.Exp), reads=[R_f32t[t]], writes=[R_pt[p]])
                else:
                    bap, Rb = bias
                    S.issue("act", lambda e: e.activation(out=pt[p][0:krows, 0:ncol], in_=f32t[t][0:krows, 0:ncol],
                                                          func=AF.Exp, bias=bap), reads=[R_f32t[t], Rb],
                            writes=[R_pt[p]])
                def stage2():
                    first = not started[g]
                    started[g] = True
                    S.issue("pe", lambda e: e.matmul(ps[accs[g]][0:HD + 1, col_lo:col_hi], lhsT=vslab,
                                                     rhs=pt[p][0:krows, 0:ncol], start=first, stop=True,
                                                     skip_group_check=True),
                            reads=[Rv, R_pt[p]], writes=[R_ps[accs[g]]])
                pending.append(stage2)
                if len(pending) > LOOKAHEAD:
                    pending.pop(0)()

            def flush():
                while pending:
                    pending.pop(0)()

            def add_plain(src_ap, Rsrc):
                def f(t, bs, krows, ncol):
                    S.issue("dve", lambda e: e.tensor_tensor(out=f32t[t][0:krows, 0:ncol], in0=ps[bs][0:krows, 0:ncol],
                                                             in1=src_ap(krows, ncol), op=ALU.add),
                            reads=[R_ps[bs], Rsrc], writes=[R_f32t[t]])
                return f

            def add_masked(src_ap, Rsrc, mask_ap):
                def f(t, bs, krows, ncol):
                    S.issue("dve", lambda e: e.tensor_tensor(out=f32t[t][0:krows, 0:ncol], in0=src_ap(krows, ncol),
                                                              in1=mask_ap(krows, ncol), op=ALU.add),
                            reads=[Rsrc, R_const], writes=[R_f32t[t]])
                    S.issue("dve", lambda e: e.tensor_tensor(out=f32t[t][0:krows, 0:ncol], in0=ps[bs][0:krows, 0:ncol],
                                                             in1=f32t[t][0:krows, 0:ncol], op=ALU.add),
                            reads=[R_ps[bs], R_f32t[t]], writes=[R_f32t[t]])
                return f

            for seg in segs:
              nk = seg["nk"]
              ks_ap, vs_ap, Rk_d, Rv_d, k0 = seg["ks"], seg["vs"], seg["Rk"], seg["Rv"], seg["k0"]
              k_done = 0
              while k_done < nk:
                n = min(1024, nk - k_done)
                sl = hsrot.next()
                kb0 = (k0 + k_done) // 128
                S.dma("sp", C_hk[sl], lambda e, sl=sl, n=n, kd=k_done: e.dma_start(
                    out=hk[sl][:, 0:n], in_=ks_ap[c, :, k0 + kd:k0 + kd + n]), reads=[Rk_d], writes=[R_hs[sl]])
                S.dma("sp", C_hv[sl], lambda e, sl=sl, n=n, kb0=kb0: e.dma_start(
                    out=hv[sl][:, 0:n // 128, :], in_=vs_ap[c, :, kb0:kb0 + n // 128, :]),
                    reads=[Rv_d], writes=[R_hs[sl]])
                for g in range(2):
                    h = 2 * c + g
                    for bl in range(n // 128):
                        kb = k_done // 128 + bl
                        kslab = hk[sl][:, 128 * bl:128 * bl + 128]
                        vslab = hv[sl][:, bl, VW * g:VW * g + HD + 1]
                        if kind == "band":
                            b = kb
                            hb_ = None if seg.get("hbias") is None else (seg["hbias"], R_const)
                            if rows == 128:
                                hi = 128 * (b + 1)
                                r = rss[g]
                                block(g, kslab, vslab, R_hs[sl], R_hs[sl], 128, 0, hi,
                                      add_masked(lambda kr, ncn, r=r, b=b: rsk[r][0:kr, 512 - 128 * b:512 - 128 * b + ncn],
                                                 R_rsk[r], lambda kr, ncn, b=b: msk[0:kr, b, 0:ncn]), hb_)
                            else:
                                r = rss[g]
                                block(g, kslab, vslab, R_hs[sl], R_hs[sl], 128, 0, N,
                                      add_plain(lambda kr, ncn, r=r, b=b: rsk[r][0:kr, 512 - 128 * b:512 - 128 * b + ncn],
                                                R_rsk[r]), hb_)
                        else:
                            q = cqs[g]
                            block(g, kslab, vslab, R_hs[sl], R_hs[sl], 128, 0, N,
                                  add_plain(lambda kr, ncn, q=q: cq[q][0:kr, 0:ncn], R_cq[q]),
                                  (seg["ncum"](kb)[:, h:h + 1], seg["Rnc"]))
                k_done += n
            for g in range(2):
                h = 2 * c + g
                for j in range(nsub):
                    jj = j if jd is None else jd
                    kslab = kT[:, c, jj * rows:(jj + 1) * rows]
                    vslab = vA[0:rows, jj, h, 0:HD + 1]
                    lo = j * rows
                    if kind == "band":
                        r = rss[g]
                        b = 4 + j
                        if rows == 128:
                            block(g, kslab, vslab, R_kT, R_vA, rows, lo, N,
                                  add_masked(lambda kr, ncn, r=r, b=b, lo=lo: rsk[r][0:kr, 512 - 128 * b + lo:512 - 128 * b + lo + ncn],
                                             R_rsk[r], lambda kr, ncn, b=b, lo=lo: msk[0:kr, b, lo:lo + ncn]), None)
                        else:
                            block(g, kslab, vslab, R_kT, R_vA, rows, 0, N,
                                  add_plain(lambda kr, ncn, r=r: rsk[r][0:kr, 0:ncn], R_rsk[r]), None)
                    else:
                        q = cqs[g]

                        def add_diag(t, bs, krows, ncol, q=q, lo=lo):
                            S.issue("dve", lambda e: e.tensor_tensor(
                                out=f32t[t][0:krows, 0:krows], in0=cq[q][0:krows, lo:lo + krows],
                                in1=trineg[0:krows, 0:krows], op=ALU.add),
                                reads=[R_cq[q], R_const], writes=[R_f32t[t]])
                            S.issue("dve", lambda e: e.tensor_tensor(
                                out=f32t[t][0:krows, 0:krows], in0=ps[bs][0:krows, 0:krows],
                                in1=f32t[t][0:krows, 0:krows], op=ALU.add),
                                reads=[R_ps[bs], R_f32t[t]], writes=[R_f32t[t]])
                            if ncol > krows:
                                S.issue("dve", lambda e: e.tensor_tensor(
                                    out=f32t[t][0:krows, krows:ncol], in0=ps[bs][0:krows, krows:ncol],
                                    in1=cq[q][0:krows, lo + krows:lo + ncol], op=ALU.add),
                                    reads=[R_ps[bs], R_cq[q]], writes=[R_f32t[t]])
                        block(g, kslab, vslab, R_kT, R_vA, rows, lo, N, add_diag,
                              (ncum_cur(j)[:, h:h + 1], R_ncur))
            flush()
            for g in range(2):
                h = 2 * c + g
                a = accs[g]
                evac_copy("act", oT[0:HD + 1, :N], ps[a][0:HD + 1, :N], [R_ps[a]], [R_oT])
                S.issue("act", lambda e: e.activation(out=rd[64:65, :N], in_=oT[64:65, :N], func=AF.Ln),
                        reads=[R_oT], writes=[R_rd])
                S.issue("act", lambda e: e.activation(out=rd[64:65, :N], in_=rd[64:65, :N], func=AF.Exp, scale=-1.0),
                        reads=[R_rd], writes=[R_rd])
                b = rotS.next()
                S.issue("pe", lambda e, b=b: e.matmul(ps[b][0:HD, :N], lhsT=ones[64:65, 0:HD], rhs=rd[64:65, :N],
                                                      start=True, stop=True), reads=[R_rd, R_const], writes=[R_ps[b]])
                if g == 0:
                    S.issue("dve", lambda e, b=b: e.tensor_tensor(out=attnT2[0:HD, c, col0:col0 + N], in0=oT[0:HD, :N],
                                                                  in1=ps[b][0:HD, :N], op=ALU.mult),
                            reads=[R_oT, R_ps[b], R_at])
                else:
                    o = oddrot.next()
                    S.issue("dve", lambda e, b=b, o=o: e.tensor_tensor(out=oddst[o][:, :N], in0=oT[0:HD, :N],
                                                                       in1=ps[b][0:HD, :N], op=ALU.mult),
                            reads=[R_oT, R_ps[b]], writes=[R_odd[o]])
                    S.dma("pool", C_odd[o], lambda e, o=o: e.dma_start(out=attnT2[HD:128, c, col0:col0 + N], in_=oddst[o][:, :N]),
                          reads=[R_odd[o], R_at])

    def wo_prefetch(l):
        wv = wo_b[l].rearrange("(kc p) d -> p kc d", p=128)
        wo_slots[l] = [ring_load(wv[:, :, 512 * s:512 * s + 512], R_w[("o", l)]) for s in range(2)]

    def wo_proj(N, l):
        if wo_slots.get(l) is None:
            wo_prefetch(l)
        slots = wo_slots.pop(l)
        for s in range(2):
            i = slots[s]
            for ml in range(4):
                m = 4 * s + ml
                b = rotA.next()
                for kc in range(DC):
                    S.issue("pe", lambda e, kc=kc, ml=ml, b=b, i=i: e.matmul(
                        ps[b][:, :N], lhsT=ring[i][:, kc, 128 * ml:128 * ml + 128], rhs=attnT2[:, kc, :N],
                        start=(kc == 0), stop=(kc == DC - 1)), reads=[R_ring[i]], writes=[R_ps[b], R_at])
                S.issue("dve", lambda e, m=m, b=b: e.tensor_tensor(out=xT[:, m, :N], in0=xT[:, m, :N], in1=ps[b][:, :N],
                                                                   op=ALU.add), reads=[R_ps[b], R_xT], writes=[R_xT])
        S.issue("dve", lambda e: e.memset(junk2[:], 0.0), writes=[R_hb, R_at])

    def final_out(N, rows, nsub, dst_rows):
        tl = {}

        def out_ap(c):
            t2 = f32rot.next()
            tl[c] = t2
            S.issue("dve", lambda e: e.scalar_tensor_tensor(
                out=f32t[t2][:, :N], in0=xT[:, c, :N], scalar=gT[:, 6 * DC + c:6 * DC + c + 1], in1=rstd[:, :N],
                op0=ALU.mult, op1=ALU.mult), reads=[R_xT, R_rstd, R_const], writes=[R_f32t[t2]])
            b = rotB.next()
            for j in range(nsub):
                S.issue("pe", lambda e, j=j: e.transpose(
                    out=ps[b][0:rows, 128 * j:128 * j + 128], in_=f32t[t2][:, j * rows:(j + 1) * rows], identity=ident),
                    reads=[R_f32t[t2], R_const], writes=[R_ps[b]])
            pv = ps[b][0:rows, 0:128 * nsub].rearrange("p (j d) -> p j d", j=nsub)
            evac_copy("act", stgA[0:rows, 0:nsub, 128 * c:128 * c + 128], pv, [R_ps[b]], [R_hb])
        rmsnorm(N, 6, out_ap=out_ap)
        S.dma("sp", C_stA, lambda e: e.dma_start(out=dst_rows.rearrange("(j p) d -> p j d", p=rows),
                                                 in_=stgA[0:rows, 0:nsub, :]), reads=[R_hb])

    S.issue("dve", lambda e: e.memset(qT2[0][64:128, :, :], 0.0), writes=[R_qT])
    S.issue("dve", lambda e: e.memset(qT2[1][0:64, :, :], 0.0), writes=[R_qT])
    S.issue("dve", lambda e: e.memset(vA[:, :, :, HD:VW], 1.0), writes=[R_vA])

    def cache_prep(u):
        for l, (ck, cv, nkc) in enumerate(((cbk_d, cbv_d, 512), (cfk_d, cfv_d, 1024))):
            for half in range(nkc // 512):
                r0 = 512 * half
                S.dma("sp", C_ld, lambda e, ck=ck, r0=r0: e.dma_start(
                    out=stgA[:, :, :], in_=ck[u, r0:r0 + 512, :].rearrange("(j p) d -> p j d", p=128)),
                    writes=[R_hb])
                for c in range(DC):
                    b = rotA.next()
                    for j in range(4):
                        S.issue("pe", lambda e, c=c, j=j, b=b: e.transpose(
                            out=ps[b][:, 128 * j:128 * j + 128], in_=stgA[:, j, 128 * c:128 * c + 128], identity=ident),
                            reads=[R_hb, R_const], writes=[R_ps[b]])
                    evac_copy("act" if c % 2 == 0 else "dve", kT[:, c, :], ps[b][:, :], [R_ps[b]], [R_kT])
                S.dma("pool", C_spk, lambda e, l=l, r0=r0: e.dma_start(
                    out=kss[l][u, :, :, r0:r0 + 512].rearrange("c p t -> p c t"), in_=kT[:, :, :]),
                    reads=[R_kT], writes=[R_kss[l]])
                S.dma("sp", C_ld, lambda e, cv=cv, r0=r0: e.dma_start(
                    out=stgB[:, :, :], in_=cv[u, r0:r0 + 512, :].rearrange("(j p) d -> p j d", p=128)),
                    writes=[R_hb])
                S.issue("dve", lambda e: e.tensor_copy(out=vA[:, :, :, 0:HD],
                                                       in_=stgB[:, :, :].rearrange("p j (h d) -> p j h d", h=H)),
                        reads=[R_hb], writes=[R_vA])
                for c in range(DC):
                    S.dma("pool", C_spv, lambda e, l=l, half=half, c=c: e.dma_start(
                        out=vss[l][u, c, :, 4 * half:4 * half + 4, :],
                        in_=vA[:, :, 2 * c:2 * c + 2, :].rearrange("p b g e -> p b (g e)")),
                        reads=[R_vA], writes=([R_vss[l]] if c == DC - 1 else []))
        S.dma("sp", C_ld, lambda e: e.dma_start(out=zt[:, :, :], in_=cfl_d[u, 0:512, :].rearrange("(j p) h -> p j h", p=128)),
              writes=[R_zt])
        S.dma("sp", C_ld2, lambda e: e.dma_start(out=lft[:, :, :], in_=cfl_d[u, 512:1024, :].rearrange("(j p) h -> p j h", p=128)),
              writes=[R_lft])
        S.issue("dve", lambda e: e.memset(carry_s[:, u, :], 0.0), writes=[R_cs])
        cum_blocks(128, 4, lambda j: zt[:, j, :], R_zt, lambda j: ncum_s[:, u, j, :], R_ncs, carry_s[:, u, :], R_cs)
        cum_blocks(128, 4, lambda j: lft[:, j, :], R_lft, lambda j: ncum_s[:, u, 4 + j, :], R_ncs, carry_s[:, u, :], R_cs)

    def tile(N, rows, nsub, x_src, seq):
        import os
        KSTOP = int(os.environ.get("KSTOP", "99"))
        load_x(x_src, rows, nsub)
        if KSTOP <= 1:
            return final_out(N, rows, nsub, seq["y_out"])
        ffn(N, 0, 0)
        if KSTOP <= 2:
          if os.environ.get("KDBG", "0") == "1":
            S.dma("sp", C_ld, lambda e: e.dma_start(out=dbg_d[:, 0:8, :], in_=xn[:, :, :]), reads=[R_xn])
            S.dma("sp", C_ld, lambda e: e.dma_start(out=dbg_d[:, 8:12, :], in_=h_v[:, 0:4, :]), reads=[R_hb])
            return final_out(N, rows, nsub, seq["y_out"])
        qkv(N, rows, nsub, 0, seq["bk_out"], seq["bv_out"], None)
        if seq["spill0"] is not None:
            spill_kv(0, *seq["spill0"])
        if KSTOP <= 3:
            return final_out(N, rows, nsub, seq["y_out"])
        attention("band", N, rows, nsub, seq["hist0"], None, None)
        if KSTOP <= 4:
            return final_out(N, rows, nsub, seq["y_out"])
        wo_proj(N, 0)
        ffn(N, 0, 1)
        if KSTOP <= 5:
            return final_out(N, rows, nsub, seq["y_out"])
        if L1:
            ffn(N, 1, 0)
            qkv(N, rows, nsub, 1, seq["fk_out"], seq["fv_out"], None)
            logf_cum(N, rows, nsub, seq["fl_out"], seq["ncum_cur"], seq["R_ncur"], seq["carry"], seq["R_carry"])
            if seq["spill1"] is not None:
                spill_kv(1, *seq["spill1"])
            attention("fox", N, rows, nsub, seq["hist1"], seq["ncum_cur"], seq["R_ncur"])
            wo_proj(N, 1)
            ffn(N, 1, 1)
        final_out(N, rows, nsub, seq["y_out"])

    def sample_tiles():
        if NS == 0:
            return
        for u in range(NS):
            cache_prep(u)
        N = 16 * NS
        flat = lambda ap: ap.rearrange("u p d -> (u p) d")
        load_x(flat(xs_d), 16, NS)
        ffn(N, 0, 0)
        qkv(N, 16, NS, 0, flat(bks_d), flat(bvs_d), None)
        for u in range(NS):
            attention("band", 16, 16, 1,
                      [dict(ks=kss[0][u], vs=vss[0][u], Rk=R_kss[0], Rv=R_vss[0], k0=0, nk=512)],
                      None, None, col0=16 * u, jd=u)
        wo_proj(N, 0)
        ffn(N, 0, 1)
        ffn(N, 1, 0)
        qkv(N, 16, NS, 1, flat(fks_d), flat(fvs_d), None)
        logf_cum(N, 16, NS, flat(fls_d), (lambda j: ncum_c[0:16, j, :]), R_ncc, None, R_cs,
                 carries=[carry_s[:, u, :] for u in range(NS)])
        for u in range(NS):
            attention("fox", 16, 16, 1,
                      [dict(ks=kss[1][u], vs=vss[1][u], Rk=R_kss[1], Rv=R_vss[1], k0=0, nk=1024,
                            ncum=(lambda kb, u=u: ncum_s[:, u, kb, :]), Rnc=R_ncs)],
                      (lambda j, u=u: ncum_c[0:16, u, :]), R_ncc, col0=16 * u, jd=u)
        wo_proj(N, 1)
        ffn(N, 1, 1)
        final_out(N, 16, NS, flat(ys_d))

    if not PAIR:
        for i in range(NT):
            t0 = 512 * i
            last = (i == NT - 1)
            seq = dict(
                bk_out=bkp_d if last else None, bv_out=bvp_d if last else None,
                spill0=(t0, 4 * i, 4) if not last else None,
                hist0=[] if i == 0 else [dict(ks=ksp[0], vs=vsp[0], Rk=R_ksp[0], Rv=R_vsp[0], k0=t0 - 512, nk=512)],
                fk_out=fkp_d[t0:t0 + 512, :], fv_out=fvp_d[t0:t0 + 512, :], fl_out=flp_d[t0:t0 + 512, :],
                ncum_cur=(lambda j, i=i: ncum_p[:, 4 * i + j, :]), R_ncur=R_ncp, carry=carry_p[:, :], R_carry=R_cp,
                spill1=(t0, 4 * i, 4) if not last else None,
                hist1=[] if i == 0 else [dict(ks=ksp[1], vs=vsp[1], Rk=R_ksp[1], Rv=R_vsp[1], k0=0, nk=t0,
                                              ncum=(lambda kb: ncum_p[:, kb, :]), Rnc=R_ncp)],
                y_out=y_d[t0:t0 + 512, :],
            )
            tile(512, 128, 4, x_d[t0:t0 + 512, :], seq)
    else:
        hmask = sb("hmask", [128, 1], F32)
        ncum_prev = sb("ncum_prev", [128, NBLK + 1, H], F32); R_ncprev = Res("ncum_prev")
        nprev = sb("nprev", [128, NBLK, H], F32); R_nprev = Res("nprev")
        totm = sb("totm", [128, H], F32)
        C_xs, C_qs, C_nc = S.chan("xs"), S.chan("qs"), S.chan("ncst")
        C_xl, C_ql, C_kl, C_vl, C_ncl = S.chan("xl"), S.chan("ql"), S.chan("kl"), S.chan("vl"), S.chan("ncl")
        C_cc = S.chan("cc")
        S.dma("sp", C_ld, lambda e: e.dma_start(out=hmask[:], in_=hm_d), writes=[R_const])

        load_x(xh_d, 128, 4)
        ffn(512, 0, 0)
        qkv(512, 128, 4, 0, None, None, None)
        spill_kv(0, 0, 0, 4)

        for i in range(NT):
            t0 = 512 * i
            last = (i == NT - 1)
            load_x(x_d[t0:t0 + 512, :], 128, 4)
            ffn(512, 0, 0)
            qkv(512, 128, 4, 0, bkp_d if last else None, bvp_d if last else None, None)
            if not last:
                spill_kv(0, t0 + 512, 4 * (i + 1), 4)
            wo_prefetch(0)
            attention("band", 512, 128, 4,
                      [dict(ks=ksp[0], vs=vsp[0], Rk=R_ksp[0], Rv=R_vsp[0], k0=t0, nk=512,
                            hbias=(hmask[:, 0:1] if i == 0 else None))], None, None)
            wo_proj(512, 0)
            ffn(512, 0, 1)
            ffn(512, 1, 0)
            qkv(512, 128, 4, 1, fkp_d[t0:t0 + 512, :], fvp_d[t0:t0 + 512, :], None)
            logf_cum(512, 128, 4, flp_d[t0:t0 + 512, :], (lambda j, i=i: ncum_p[:, 4 * i + j, :]), R_ncp,
                     carry_p[:, :], R_cp)
            spill_kv(1, t0, 4 * i, 4)
            S.dma("pool", C_xs, lambda e, i=i: e.dma_start(out=xsp[i], in_=xT[:, :, :].rearrange("p c t -> p (c t)")),
                  reads=[R_xT], writes=[R_xsp])
            for g in range(2):
                S.dma("pool", C_qs, lambda e, i=i, g=g: e.dma_start(
                    out=qsp[i, g], in_=qT2[g][:, :, :].rearrange("p c t -> p (c t)")),
                    reads=[R_qT], writes=([R_qsp] if g == 1 else []))

        S.dma("pool", C_nc, lambda e: e.dma_start(out=ncd[:, 0:NBLK * H], in_=ncum_p[:, :, :].rearrange("p b h -> p (b h)")),
              reads=[R_ncp])
        S.dma("pool", C_nc, lambda e: e.dma_start(out=ncd[:, NBLK * H:(NBLK + 1) * H], in_=carry_p[:, :]),
              reads=[R_cp], writes=[R_ncd])
        PAIRS = [[0, 1], [2, 3], [4, 5], [6, 7]]
        def gather(src_ap, dst_ap, Rs, Rd, last):
            S.dma("pool", C_cc, lambda e: e.collective_compute(
                "AllGather", ALU.bypass, replica_groups=PAIRS, ins=[src_ap.opt()], outs=[dst_ap.opt()]),
                reads=[Rs], writes=([Rd] if last else []), coll=True)
        gather(ncd, ncg, R_ncd, R_ncg, True)
        for c in range(DC):
            gather(ksp1_2d[128 * c:128 * c + 128, :], ksg_2d[256 * c:256 * c + 256, :], R_ksp[1], R_ksg, c == DC - 1)
            gather(vsp1_2d[128 * c:128 * c + 128, :], vsg_2d[256 * c:256 * c + 256, :], R_vsp[1], R_vsg, c == DC - 1)
        sample_tiles()
        S.dma("sp", C_ncl, lambda e: e.dma_start(out=ncum_prev[:, :, :].rearrange("p b h -> p (b h)"), in_=ncg[0:128, :]),
              reads=[R_ncg], writes=[R_ncprev])
        S.issue("dve", lambda e: e.tensor_scalar(out=totm[:, :], in0=ncum_prev[:, NBLK, :], scalar1=hmask[:, 0:1],
                                                 scalar2=None, op0=ALU.add), reads=[R_ncprev, R_const], writes=[R_nprev])
        for kb in range(NBLK):
            S.issue("dve", lambda e, kb=kb: e.tensor_tensor(out=nprev[:, kb, :], in0=ncum_prev[:, kb, :], in1=totm[:, :],
                                                            op=ALU.add), reads=[R_ncprev, R_nprev], writes=[R_nprev])

        for i in range(NT):
            t0 = 512 * i
            S.dma("sp", C_xl, lambda e, i=i: e.dma_start(out=xT[:, :, :].rearrange("p c t -> p (c t)"), in_=xsp[i]),
                  reads=[R_xsp], writes=[R_xT])
            for g in range(2):
                S.dma("sp", C_ql, lambda e, i=i, g=g: e.dma_start(
                    out=qT2[g][:, :, :].rearrange("p c t -> p (c t)"), in_=qsp[i, g]),
                    reads=[R_qsp], writes=([R_qT] if g == 1 else []), wars=[R_qT])
            S.dma("sp", C_kl, lambda e, t0=t0: e.dma_start(
                out=kT[:, :, :], in_=ksp[1][:, :, t0:t0 + 512].rearrange("c p t -> p c t")),
                reads=[R_ksp[1]], writes=[R_kT])
            for c in range(DC):
                S.dma("sp", C_vl, lambda e, i=i, c=c: e.dma_start(
                    out=vA[:, 0:4, 2 * c:2 * c + 2, :].rearrange("p b g e -> p b (g e)"),
                    in_=vsp[1][c, :, 4 * i:4 * i + 4, :]),
                    reads=[R_vsp[1]], writes=([R_vA] if c == DC - 1 else []), wars=[R_vA])
            segs = [dict(ks=ksg[0], vs=vsg[0], Rk=R_ksg, Rv=R_vsg, k0=0, nk=T,
                         ncum=(lambda kb: nprev[:, kb, :]), Rnc=R_nprev)]
            if i > 0:
                segs.append(dict(ks=ksp[1], vs=vsp[1], Rk=R_ksp[1], Rv=R_vsp[1], k0=0, nk=t0,
                                 ncum=(lambda kb: ncum_p[:, kb, :]), Rnc=R_ncp))
            wo_prefetch(1)
            attention("fox", 512, 128, 4, segs, (lambda j, i=i: ncum_p[:, 4 * i + j, :]), R_ncp)
            wo_proj(512, 1)
            ffn(512, 1, 1)
            final_out(512, 128, 4, y_d[t0:t0 + 512, :])

    if not PAIR:
        sample_tiles()

    S.emit(nc, es)
    es.close()
    return nc


def _consts():
    ident = np.eye(128, dtype=np.float32)
    ones = np.ones((128, 128), np.float32)
    p = np.arange(128)[:, None]
    c = np.arange(128)[None, :]
    tri = (p <= c).astype(np.float32)
    trineg = np.where(c >= p, 0.0, NEG).astype(np.float32)
    cst = np.concatenate([ident, ones, tri, trineg], axis=1)
    msk = np.zeros((128, 8, 512), np.float32)
    for b in range(8):
        kc = 2 * b + (np.arange(128) >= 64).astype(np.int64)
        qc = 8 + np.arange(512) // 64
        valid = (kc[:, None] <= qc[None, :]) & (kc[:, None] >= qc[None, :] - 8)
        msk[:, b, :] = np.where(valid, 0.0, NEG)
    return cst, msk.reshape(128, 8 * 512).astype(ml_dtypes.bfloat16)


def _prep_shared(norm_g, w_qkv, w_o, w_ffn_gate, w_ffn_up, w_ffn_down, rel_bias, w_forget, b_forget, final_norm_g):
    g7 = np.concatenate([norm_g.reshape(6, D), final_norm_g.reshape(1, D)], axis=0)
    gT = np.ascontiguousarray(g7.reshape(7, DC, 128).transpose(2, 0, 1).reshape(128, 7 * DC))
    rb = rel_bias[0]
    pp = np.arange(128)[:, None]
    cc = np.arange(640)[None, :]
    idx = np.clip(cc - pp, -256, 256) + 256
    rsk = np.ascontiguousarray(rb[idx].transpose(2, 0, 1))
    wf = np.ascontiguousarray(w_forget[0].reshape(DC, 128, H).transpose(1, 0, 2).reshape(128, DC * H))
    bfb = np.ascontiguousarray(np.broadcast_to(b_forget[0][None, :], (128, H)))
    cst, msk = _consts()
    return {
        "gT": gT.astype(np.float32), "w_qkv": np.ascontiguousarray(w_qkv), "w_o": np.ascontiguousarray(w_o),
        "w_g": np.ascontiguousarray(w_ffn_gate.reshape(4, D, FF)), "w_u": np.ascontiguousarray(w_ffn_up.reshape(4, D, FF)),
        "w_d": np.ascontiguousarray(w_ffn_down.reshape(4, FF, D)), "rsk": rsk.astype(np.float32),
        "wf": wf.astype(np.float32), "bfb": bfb.astype(np.float32), "cst": cst, "msk": msk,
    }


_NC_CACHE = {}


def kernel(x_prompt, x_sample, cache_band_k, cache_band_v, cache_fox_k, cache_fox_v, cache_fox_logf,
           norm_g, w_qkv, w_o, w_ffn_gate, w_ffn_up, w_ffn_down, rel_bias, w_forget, b_forget, final_norm_g):
    f = lambda a: np.asarray(a, dtype=np.float32)
    x_prompt, x_sample = f(x_prompt), f(x_sample)
    B, T, _ = x_prompt.shape
    SB = x_sample.shape[0]
    NCORE = 8
    NS = SB // NCORE
    shared = _prep_shared(f(norm_g), f(w_qkv), f(w_o), f(w_ffn_gate), f(w_ffn_up), f(w_ffn_down), f(rel_bias),
                          f(w_forget), f(b_forget), f(final_norm_g))
    cbk, cbv = f(cache_band_k)[0].reshape(SB, 512, D), f(cache_band_v)[0].reshape(SB, 512, D)
    cfk, cfv = f(cache_fox_k)[0].reshape(SB, 1024, D), f(cache_fox_v)[0].reshape(SB, 1024, D)
    cfl = f(cache_fox_logf)[0]
    TH = T // 2
    key = (TH, NS, True)
    if key not in _NC_CACHE:
        _NC_CACHE[key] = build(TH, NS, PAIR=True)
    nc = _NC_CACHE[key]
    in_maps = []
    for c in range(NCORE):
        b, half = c // 2, c % 2
        sl = slice(c * NS, (c + 1) * NS)
        m = dict(shared)
        xh = x_prompt[b, TH - 512:TH] if half == 1 else np.zeros((512, D), np.float32)
        m.update({"x": np.ascontiguousarray(x_prompt[b, half * TH:(half + 1) * TH]),
                  "xh": np.ascontiguousarray(xh),
                  "hmask": np.full((128, 1), 0.0 if half == 1 else NEG, np.float32),
                  "xs": np.ascontiguousarray(x_sample[sl]),
                  "cbk": np.ascontiguousarray(cbk[sl]), "cbv": np.ascontiguousarray(cbv[sl]),
                  "cfk": np.ascontiguousarray(cfk[sl]), "cfv": np.ascontiguousarray(cfv[sl]),
                  "cfl": np.ascontiguousarray(cfl[sl])})
        in_maps.append(m)
    res = run_bass_kernel_spmd(nc, in_maps, core_ids=list(range(NCORE))).results
    cat = lambda k: np.concatenate([res[c][k] for c in range(NCORE)], axis=0)
    seqcat = lambda k: np.stack([np.concatenate([res[2 * b][k], res[2 * b + 1][k]], axis=0) for b in range(B)], axis=0)
    last = lambda k: np.stack([res[2 * b + 1][k] for b in range(B)], axis=0)
    y_prompt = seqcat("y")
    y_sample = cat("ys")
    return (y_prompt, y_sample,
            last("bkp").reshape(1, B, 512, H, HD), last("bvp").reshape(1, B, 512, H, HD),
            cat("bks").reshape(1, SB, 16, H, HD), cat("bvs").reshape(1, SB, 16, H, HD),
            seqcat("fkp").reshape(1, B, T, H, HD), seqcat("fvp").reshape(1, B, T, H, HD), seqcat("flp").reshape(1, B, T, H),
            cat("fks").reshape(1, SB, 16, H, HD), cat("fvs").reshape(1, SB, 16, H, HD), cat("fls").reshape(1, SB, 16, H))
```
